# Optimizing a Trainium2 kernel written in Bass

```python
import math
import jax, jax.numpy as jnp
from jax import lax
import numpy as np

D_MODEL = 1024
BATCH = 2
SEQ = 8192
DEPTH = 1

PLE_DIM = 256
D_FF = 2816
CONV_WIDTH = D_MODEL // 2
CONV_K = 3
ATTN_WIDTH = D_MODEL // 2
HEAD_DIM = 64
N_DIFF_HEADS = ATTN_WIDTH // (2 * HEAD_DIM)
V_HEAD_DIM = 2 * HEAD_DIM
MIX_WIDTH = CONV_WIDTH + ATTN_WIDTH
IN_PROJ_WIDTH = 3 * CONV_WIDTH + 3 * ATTN_WIDTH
Q_BLOCK = 128
ROPE_THETA = 10000.0
EPS = 1e-6
LAMBDA_STD = 0.1

kernel_name = "hymba_conv_diffattn_macaron_ple"


def rmsnorm(x, g):
    xf = x.astype(jnp.float32)
    xf = xf * lax.rsqrt(jnp.mean(xf * xf, axis=-1, keepdims=True) + EPS)
    return xf.astype(x.dtype) * g


def swiglu(x, w_gate, w_up, w_down):
    return (jax.nn.silu(x @ w_gate) * (x @ w_up)) @ w_down


def causal_conv3(u, w):
    s = u.shape[1]
    up = jnp.pad(u, ((0, 0), (CONV_K - 1, 0), (0, 0)))
    return w[0] * up[:, 0:s] + w[1] * up[:, 1:s + 1] + w[2] * up[:, 2:s + 2]


def rope_tables(s):
    inv_freq = 1.0 / (ROPE_THETA ** (jnp.arange(0, HEAD_DIM, 2, dtype=jnp.float32) / HEAD_DIM))
    ang = jnp.arange(s, dtype=jnp.float32)[:, None] * inv_freq[None, :]
    return jnp.cos(ang), jnp.sin(ang)


def apply_rope(x, cos, sin):
    c = cos[:, None, None, :].astype(x.dtype)
    sn = sin[:, None, None, :].astype(x.dtype)
    x1, x2 = jnp.split(x, 2, axis=-1)
    return jnp.concatenate([x1 * c - x2 * sn, x2 * c + x1 * sn], axis=-1)


def diff_attention(q, k, v, lam):
    b, s = q.shape[0], q.shape[1]
    nblk = s // Q_BLOCK
    scale = HEAD_DIM ** -0.5
    qb = q.reshape(b, nblk, Q_BLOCK, N_DIFF_HEADS, 2, HEAD_DIM).transpose(1, 0, 2, 3, 4, 5)
    kf = k.astype(jnp.float32)
    vf = v.astype(jnp.float32)
    kpos = jnp.arange(s)

    def one_block(args):
        qi, blk = args
        qpos = blk * Q_BLOCK + jnp.arange(Q_BLOCK)
        sc = jnp.einsum('bqhcd,bkhcd->bhcqk', qi.astype(jnp.float32), kf) * scale
        mask = kpos[None, :] <= qpos[:, None]
        sc = jnp.where(mask, sc, -jnp.inf)
        pr = jax.nn.softmax(sc, axis=-1)
        a = pr[:, :, 0] - lam * pr[:, :, 1]
        return jnp.einsum('bhqk,bkhe->bqhe', a, vf).astype(v.dtype)

    out = lax.map(one_block, (qb, jnp.arange(nblk)))
    return out.transpose(1, 0, 2, 3, 4).reshape(b, s, N_DIFF_HEADS, V_HEAD_DIM)


def setup_inputs(seed: int = 0) -> dict:
    key = jax.random.key(seed)
    ks = jax.random.split(key, 24)

    def w(k, shape, fan_in):
        return jax.random.normal(k, shape, jnp.float32) * (fan_in ** -0.5)

    def gain(k, shape):
        return 1.0 + 0.01 * jax.random.normal(k, shape, jnp.float32)

    L = DEPTH
    return {
        "x": jax.random.normal(ks[0], (BATCH, SEQ, D_MODEL), jnp.float32),
        "p": jax.random.normal(ks[1], (DEPTH, BATCH, SEQ, PLE_DIM), jnp.float32),
        "ffn1_norm": gain(ks[2], (L, D_MODEL)),
        "ffn1_w_gate": w(ks[3], (L, D_MODEL, D_FF), D_MODEL),
        "ffn1_w_up": w(ks[4], (L, D_MODEL, D_FF), D_MODEL),
        "ffn1_w_down": w(ks[5], (L, D_FF, D_MODEL), D_FF),
        "mix_norm": gain(ks[6], (L, D_MODEL)),
        "w_in": w(ks[7], (L, D_MODEL, IN_PROJ_WIDTH), D_MODEL),
        "conv_w": w(ks[8], (L, CONV_K, CONV_WIDTH), CONV_K),
        "lambda_q1": LAMBDA_STD * jax.random.normal(ks[9], (L, HEAD_DIM), jnp.float32),
        "lambda_k1": LAMBDA_STD * jax.random.normal(ks[10], (L, HEAD_DIM), jnp.float32),
        "lambda_q2": LAMBDA_STD * jax.random.normal(ks[11], (L, HEAD_DIM), jnp.float32),
        "lambda_k2": LAMBDA_STD * jax.random.normal(ks[12], (L, HEAD_DIM), jnp.float32),
        "subln_g": gain(ks[13], (L, V_HEAD_DIM)),
        "w_out": w(ks[14], (L, MIX_WIDTH, D_MODEL), MIX_WIDTH),
        "ffn2_norm": gain(ks[15], (L, D_MODEL)),
        "ffn2_w_gate": w(ks[16], (L, D_MODEL, D_FF), D_MODEL),
        "ffn2_w_up": w(ks[17], (L, D_MODEL, D_FF), D_MODEL),
        "ffn2_w_down": w(ks[18], (L, D_FF, D_MODEL), D_FF),
        "ple_norm": gain(ks[19], (L, D_MODEL)),
        "w_ple_gate": w(ks[20], (L, D_MODEL, D_MODEL), D_MODEL),
        "w_ple_proj": w(ks[21], (L, PLE_DIM, D_MODEL), PLE_DIM),
        "final_norm": gain(ks[22], (D_MODEL,)),
    }


def reference(x, p, ffn1_norm, ffn1_w_gate, ffn1_w_up, ffn1_w_down, mix_norm, w_in, conv_w,
              lambda_q1, lambda_k1, lambda_q2, lambda_k2, subln_g, w_out,
              ffn2_norm, ffn2_w_gate, ffn2_w_up, ffn2_w_down,
              ple_norm, w_ple_gate, w_ple_proj, final_norm):
    b, s, _ = x.shape
    cos, sin = rope_tables(s)
    splits = [CONV_WIDTH, 2 * CONV_WIDTH, 3 * CONV_WIDTH,
              3 * CONV_WIDTH + ATTN_WIDTH, 3 * CONV_WIDTH + 2 * ATTN_WIDTH]
    h = x
    for i in range(DEPTH):
        lam_init = 0.8 - 0.6 * math.exp(-0.3 * i)

        h = h + 0.5 * swiglu(rmsnorm(h, ffn1_norm[i]), ffn1_w_gate[i], ffn1_w_up[i], ffn1_w_down[i])

        n = rmsnorm(h, mix_norm[i])
        proj = n @ w_in[i]
        c_b, c_c, c_x, q, k, v = jnp.split(proj, splits, axis=-1)

        y_conv = c_b * causal_conv3(c_c * c_x, conv_w[i])

        q = apply_rope(q.reshape(b, s, N_DIFF_HEADS, 2, HEAD_DIM), cos, sin)
        k = apply_rope(k.reshape(b, s, N_DIFF_HEADS, 2, HEAD_DIM), cos, sin)
        v = v.reshape(b, s, N_DIFF_HEADS, V_HEAD_DIM)
        lam = (jnp.exp(jnp.sum(lambda_q1[i].astype(jnp.float32) * lambda_k1[i].astype(jnp.float32)))
               - jnp.exp(jnp.sum(lambda_q2[i].astype(jnp.float32) * lambda_k2[i].astype(jnp.float32)))
               + lam_init)
        o = diff_attention(q, k, v, lam)
        o = rmsnorm(o, subln_g[i]) * (1.0 - lam_init)
        y_attn = o.reshape(b, s, ATTN_WIDTH)

        h = h + jnp.concatenate([y_conv, y_attn], axis=-1) @ w_out[i]

        h = h + 0.5 * swiglu(rmsnorm(h, ffn2_norm[i]), ffn2_w_gate[i], ffn2_w_up[i], ffn2_w_down[i])

        gate = jax.nn.sigmoid(rmsnorm(h, ple_norm[i]) @ w_ple_gate[i])
        h = h + gate * (p[i] @ w_ple_proj[i])
    return rmsnorm(h, final_norm)
```

```python
import numpy as np
import concourse.bass as bass
import concourse.mybir as mybir
from concourse.bass_utils import run_bass_kernel_spmd

F32 = mybir.dt.float32
BF16 = mybir.dt.bfloat16
AF = mybir.ActivationFunctionType
ALU = mybir.AluOpType

D = 1024
DFF = 2816
NT = 2048
SL = 1024
TT = 512
EPS = 1e-6
NEG = -30000.0
GROUPS = [[0, 1, 2, 3], [4, 5, 6, 7]]


class Buf:
    __slots__ = ("lw", "rd")

    def __init__(self):
        self.lw = None
        self.rd = {}


class Eng:
    def __init__(self, name, sem, is_pe=False):
        self.name = name
        self.sem = sem
        self.cnt = 0
        self.seen = {}
        self.ops = []
        self.is_pe = is_pe
        self.step = 1


class DSem:
    def __init__(self, sem, step=16):
        self.sem = sem
        self.cnt = 0
        self.step = step
        self.is_pe = False
        self.noval = (step == 1)


class Prog:
    def __init__(self):
        self.engs = {}

    def _deps(self, eng, reads, writes):
        deps = {}

        def add(tok):
            if tok is None:
                return
            k, v = tok
            if deps.get(k, 0) < v:
                deps[k] = v

        for b in reads:
            add(b.lw)
        for b in writes:
            add(b.lw)
            for k, v in b.rd.items():
                add((k, v))
        waits = []
        for k, v in deps.items():
            if k is eng and eng.is_pe:
                continue
            if eng.seen.get(k, 0) >= v:
                continue
            eng.seen[k] = v
            waits.append((k.sem, v))
        return waits

    @staticmethod
    def _mark(tok, reads, writes):
        k, v = tok
        for b in reads:
            if b.rd.get(k, 0) < v:
                b.rd[k] = v
        for b in writes:
            b.lw = tok
            b.rd = {}

    def op(self, eng, fn, reads=(), writes=()):
        waits = self._deps(eng, reads, writes)
        eng.cnt += 1
        tok = (eng, eng.cnt)
        eng.ops.append((waits, fn, (eng.sem, 1)))
        self._mark(tok, reads, writes)
        return tok

    def group(self, eng, fns, reads=(), writes=()):
        waits = self._deps(eng, reads, writes)
        eng.cnt += 1
        tok = (eng, eng.cnt)
        n = len(fns)
        for i, fn in enumerate(fns):
            eng.ops.append((waits if i == 0 else [], fn, (eng.sem, 1) if i == n - 1 else None))
        self._mark(tok, reads, writes)
        return tok

    def dma(self, eng, dsem, fn, reads=(), writes=()):
        waits = self._deps(eng, reads, writes)
        dsem.cnt += dsem.step
        tok = (dsem, dsem.cnt)
        eng.ops.append((waits, fn, (dsem.sem, 0 if dsem.noval else dsem.step)))
        self._mark(tok, reads, writes)
        return tok


def emit(e, eng):
    for waits, fn, inc in eng.ops:
        for sem, v in waits:
            e.wait_ge(sem, v)
        ins = fn(e)
        if inc is not None:
            if inc[1] == 0:
                ins.then_inc(inc[0])
            else:
                ins.then_inc(inc[0], inc[1])


def alias_barrier(old, new):
    for nb in new:
        for ob in old:
            if ob.lw is not None:
                k, v = ob.lw
                if nb.rd.get(k, 0) < v:
                    nb.rd[k] = v
            for k, v in ob.rd.items():
                if nb.rd.get(k, 0) < v:
                    nb.rd[k] = v


def build_nc(STOP=99):
    nc = bass.Bass("TRN2", target_bir_lowering=False)

    def din(name, shape, dt=F32):
        return nc.dram_tensor(name, list(shape), dt, kind="ExternalInput").ap()

    xT = din("xT", [D, NT])
    pT = din("pT", [256, NT])
    rope = din("rope", [128, 4, 2 * TT])
    vecs = din("vecs", [128, 64])
    bcast = din("bcast", [128, 416])
    w_g1 = din("w_g1", [D, DFF]); w_u1 = din("w_u1", [D, DFF]); w_d1 = din("w_d1", [DFF, D])
    w_g2 = din("w_g2", [D, DFF]); w_u2 = din("w_u2", [D, DFF]); w_d2 = din("w_d2", [DFF, D])
    w_inx = din("w_inx", [D, 4096])
    w_out = din("w_out", [D, D])
    w_pg = din("w_pg", [D, D])
    w_pp = din("w_pp", [256, D])
    outT = nc.dram_tensor("outT", [D, NT], F32, kind="ExternalOutput").ap()

    kT_mine = [nc.dram_tensor(f"kT_mine{s}", [128, 4 * SL], BF16) for s in range(2)]
    kT_all = [nc.dram_tensor(f"kT_all{s}", [4 * 128, 4 * SL], BF16) for s in range(2)]
    v_mine = [[nc.dram_tensor(f"v_mine{s}_{hp}", [256, 1056], BF16) for hp in range(2)] for s in range(2)]
    v_all = [[nc.dram_tensor(f"v_all{s}_{hp}", [4 * 256, 1056], BF16) for hp in range(2)] for s in range(2)]
    halo_mine = nc.dram_tensor("halo_mine", [128, 16], F32)
    halo_all = nc.dram_tensor("halo_all", [512, 16], F32)

    from contextlib import ExitStack
    with ExitStack() as es:
        def sb(name, shape, dt):
            return es.enter_context(nc.sbuf_tensor(name, list(shape), dt))

        def sem(name):
            return es.enter_context(nc.semaphore(name))

        hT = sb("hT", [128, 8, NT], F32)
        xn = sb("xn", [128, 2, 8, SL], BF16)
        wsm = sb("wsm", [128, 4, 2048], BF16)
        ARENA = 27264
        arena = sb("arena", [128, ARENA], BF16)
        tmp = sb("tmp", [128, 4, TT], F32)
        rstd = sb("rstd", [128, 2, TT], F32)
        ropes = sb("ropes", [128, 2, 1032], F32)
        ubuf = sb("ubuf", [128, SL + 2], F32)
        accb = sb("accb", [128, SL], F32)
        ones_bf = sb("ones_bf", [128, 128], BF16)
        ident = sb("ident", [128, 128], BF16)
        vecs_sb = sb("vecs_sb", [128, 64], F32)
        bc_sb = sb("bc_sb", [128, 416], F32)
        gsub = sb("gsub", [128, 128], F32)
        small = sb("small", [128, 96], F32)
        junk = sb("junk", [128, 128], F32)
        halo_sb = sb("halo_sb", [128, 4, 16], F32)
        halo_st = sb("halo_st", [128, 16], F32)
        acc01 = sb("acc01", [128, 16], F32)
        cb01 = sb("cb01", [128, 16], F32)
        hfix = sb("hfix", [128, 32], F32)
        obuf = sb("obuf", [128, 2, 128], F32)
        tbuf = sb("tbuf", [128, 2, 128], F32)

        banks = [es.enter_context(nc.psum_tensor(f"bk{i}", [128, TT], F32)) for i in range(7)]
        tpb = es.enter_context(nc.psum_tensor("tpb", [128, TT], BF16))

        def av(off, n, **kw):
            v = arena[:, off:off + n]
            return v

        QT = arena[:, 0:8192].rearrange("p (h t) -> p h t", h=4)
        ycv = arena[:, 8192:16384].rearrange("p (h t) -> p h t", h=4)
        A0 = 16384
        Kb = arena[:, A0:A0 + 3072].rearrange("p (s k) -> p s k", s=3)
        Vb = arena[:, A0 + 3072:A0 + 6240].rearrange("p (s k e) -> p s k e", s=3, k=8)
        Eb = arena[:, A0 + 6240:A0 + 8288].rearrange("p (s q) -> p s q", s=4)
        kst = arena[:, A0 + 8288:A0 + 9312].rearrange("p (s q) -> p s q", s=2)
        NVST = 4
        vst = arena[:, A0:A0 + NVST * 528].rearrange("p (s h e) -> p s h e", s=NVST, h=4)
        ybf = arena[:, A0 + 10368:A0 + 10880].rearrange("p (s e) -> p s e", s=4)
        h1 = arena[:, 0:12288].rearrange("p (f t) -> p f t", f=12)
        wdn = arena[:, 12288:18432].rearrange("p (s f d) -> p s f d", s=2, f=12)
        pTb = arena[:, 20480:24576].rearrange("p (k t) -> p k t", k=2)
        yat = xn[:, 0].rearrange("p c t -> p (c t)").rearrange("p (h t) -> p h t", h=4)

        P = Prog()
        PE = Eng("pe", sem("s_pe"), is_pe=True)
        ACT = Eng("act", sem("s_act"))
        DVE = Eng("dve", sem("s_dve"))
        POOL = Eng("pool", sem("s_pool"))
        SP = Eng("sp", sem("s_sp"))

        def dsem(name, step=16):
            return DSem(sem(name), step)

        B_hT = [[Buf() for _ in range(4)] for _ in range(8)]
        B_xn = [[Buf() for _ in range(2)] for _ in range(2)]
        B_ws = [Buf() for _ in range(4)]
        B_bank = [Buf() for _ in range(7)]
        B_tpb = Buf()
        B_tmp = [Buf() for _ in range(4)]
        B_rstd = [Buf() for _ in range(2)]
        B_rope = [Buf() for _ in range(2)]
        B_h1 = [Buf() for _ in range(12)]
        B_wdn = [Buf() for _ in range(2)]
        B_QT = [[Buf() for _ in range(4)] for _ in range(4)]
        B_ycv = [[Buf() for _ in range(4)] for _ in range(4)]
        B_yat = [[Buf() for _ in range(4)] for _ in range(4)]
        B_Kb = [Buf() for _ in range(3)]
        B_Vb = [Buf() for _ in range(3)]
        B_E = [Buf() for _ in range(4)]
        B_kst = [Buf() for _ in range(2)]
        B_vst = [Buf() for _ in range(4)]
        B_ybf = Buf()
        B_pT = Buf()
        B_const = Buf()
        B_small = Buf()
        B_u = Buf()
        B_acc = Buf()
        B_halo = Buf()
        B_halo_sb = Buf()
        B_hfix = Buf()
        B_obuf = [Buf(), Buf()]
        B_tbuf = [Buf(), Buf()]
        B_kTm = [[Buf() for _ in range(4)] for _ in range(4)]
        B_vm = [Buf() for _ in range(16)]
        B_halom = Buf()
        B_kTall = [Buf(), Buf()]
        B_vall = [Buf() for _ in range(4)]
        B_haloall = Buf()
        B_out = Buf()

        S_ws = [dsem(f"d_ws{i}") for i in range(4)]
        S_wdn = [dsem(f"d_wd{i}") for i in range(2)]
        S_misc = dsem("d_misc")
        S_x = dsem("d_x")
        S_rope = [dsem(f"d_rp{i}") for i in range(2)]
        S_kst = [dsem(f"d_ks{i}") for i in range(2)]
        S_vst = [dsem(f"d_vs{i}") for i in range(4)]
        S_Kb = [dsem(f"d_kb{i}") for i in range(3)]
        S_Vb = [dsem(f"d_vb{i}") for i in range(3)]
        S_cc = [dsem(f"d_cc{i}", 1) for i in range(3)]
        S_cck = [dsem(f"d_cck{i}", 1) for i in range(2)]
        S_ccv = [dsem(f"d_ccv{i}", 1) for i in range(4)]
        S_halo = dsem("d_halo")
        S_pT = dsem("d_pT")
        S_out = [dsem(f"d_out{i}") for i in range(4)]

        rr = {"bank": 0, "tmp": 0, "ws": 0, "wd": 0, "E": 0, "kv": 0, "rope": 0, "kst": 0, "vst": 0, "ob": 0, "out": 0}

        def nxt(key, n):
            i = rr[key]
            rr[key] = (i + 1) % n
            return i

        P.dma(SP, S_misc, lambda e: e.dma_start(out=vecs_sb[:, :], in_=vecs), writes=[B_const])
        P.dma(SP, S_misc, lambda e: e.dma_start(out=bc_sb[:, :], in_=bcast), writes=[B_const])
        xTv = xT.rearrange("(c p) t -> p c t", p=128)
        S_x2 = dsem("d_x2")
        for slot_, sx in ((0, S_x), (1, S_x2)):
            for c in range(8):
                xtok = P.dma(SP, sx, lambda e, c=c, slot_=slot_: e.dma_start(
                    out=hT[:, c, slot_ * SL:(slot_ + 1) * SL], in_=xTv[:, c, slot_ * SL:(slot_ + 1) * SL]),
                    writes=[B_hT[c][2 * slot_], B_hT[c][2 * slot_ + 1]])
            for c in range(8):
                for t in range(2):
                    B_hT[c][2 * slot_ + t].lw = xtok
        P.op(POOL, lambda e: e.memset(ones_bf[:, :], 1.0), writes=[B_const])
        P.op(POOL, lambda e: e.affine_select(out=ident[:, :], in_=ones_bf[:, :], pattern=[[1, 128]],
                                              compare_op=ALU.is_equal, fill=0.0, base=0,
                                              channel_multiplier=-1), reads=[B_const], writes=[B_const])
        P.op(POOL, lambda e: e.memset(ubuf[:, 0:2], 0.0), writes=[B_u])
        P.op(DVE, lambda e: e.tensor_scalar(out=gsub[:, :], in0=bc_sb[:, 0:128], scalar1=0.8, scalar2=None,
                                            op0=ALU.mult), reads=[B_const], writes=[B_const])
        P.op(DVE, lambda e: e.scalar_tensor_tensor(out=junk[:, 0:64], in0=bc_sb[:, 128:192], scalar=1.0,
                                                   in1=bc_sb[:, 192:256], op0=ALU.mult, op1=ALU.mult,
                                                   accum_out=small[:, 0:1]), reads=[B_const], writes=[B_small])
        P.op(DVE, lambda e: e.scalar_tensor_tensor(out=junk[:, 64:128], in0=bc_sb[:, 256:320], scalar=1.0,
                                                   in1=bc_sb[:, 320:384], op0=ALU.mult, op1=ALU.mult,
                                                   accum_out=small[:, 1:2]), reads=[B_const, B_small], writes=[B_small])
        P.op(ACT, lambda e: e.activation(out=small[:, 2:4], in_=small[:, 0:2], func=AF.Exp),
             reads=[B_small], writes=[B_small])
        P.op(DVE, lambda e: e.tensor_tensor(out=small[:, 4:5], in0=small[:, 2:3], in1=small[:, 3:4],
                                            op=ALU.subtract), reads=[B_small], writes=[B_small])
        P.op(DVE, lambda e: e.tensor_scalar(out=small[:, 5:6], in0=small[:, 4:5], scalar1=0.2, scalar2=None,
                                            op0=ALU.add), reads=[B_small], writes=[B_small])
        LAM = small[:, 5:6]
        ZERO = small[:, 6:7]
        P.op(DVE, lambda e: e.memset(small[:, 6:8], 0.0), reads=[B_small], writes=[B_small])
        P.op(DVE, lambda e: e.tensor_copy(out=small[:, 7:8], in_=small[:, 6:7]),
             reads=[B_small], writes=[B_const])

        def bank_alloc():
            i = nxt("bank", 7)
            return i, banks[i], B_bank[i]

        def tmp_alloc():
            i = nxt("tmp", 4)
            return tmp[:, i, :], B_tmp[i]

        ws_free = [0, 1, 2, 3]
        wd_free = [0, 1]
        ws_slot_of = {}

        def ws_release(B):
            ws_free.append(ws_slot_of.pop(id(B)))

        def load_wsm(w, col0, kchunks=8, ncols=256):
            i = ws_free.pop(0)
            ws_slot_of[id(B_ws[i])] = i
            src = w.rearrange("(c p) f -> p c f", p=128)[:, :, col0:col0 + ncols]
            dst = wsm[:, i, 0:kchunks * ncols].rearrange("p (c f) -> p c f", c=kchunks)
            P.dma(POOL, S_ws[i], lambda e: e.dma_start(out=dst, in_=src), writes=[B_ws[i]])
            return dst, B_ws[i]

        def wd_release(B):
            wd_free.append(B_wdn.index(B))

        def load_wdn(w, row0, nf, col0):
            i = wd_free.pop(0)
            src = w[row0 * 128:(row0 + nf) * 128, :].rearrange("(c p) d -> p c d", p=128)[:, :, col0:col0 + 256]
            dst = wdn[:, i, 0:nf, :]
            P.dma(POOL, S_wdn[i], lambda e: e.dma_start(out=dst, in_=src), writes=[B_wdn[i]])
            return dst, B_wdn[i]

        def norm_sq(slot, gcol):
            for c in range(8):
                P.op(ACT, lambda e, c=c: e.activation(out=xn[:, slot, c, :], in_=hT[:, c, slot * SL:(slot + 1) * SL],
                                                      func=AF.Square),
                     reads=[B_hT[c][2 * slot], B_hT[c][2 * slot + 1]], writes=[B_xn[slot][0], B_xn[slot][1]])

        def norm_stat(slot):
            for ttl in range(2):
                bi, bk, bb = bank_alloc()
                fns = []
                for c in range(8):
                    fns.append(lambda e, c=c, bk=bk, ttl=ttl: e.matmul(bk[:, :], lhsT=ones_bf[:, :],
                                                                       rhs=xn[:, slot, c, ttl * TT:(ttl + 1) * TT],
                                                                       start=(c == 0), stop=(c == 7)))
                P.group(PE, fns, reads=[B_const, B_xn[slot][ttl]], writes=[bb])
                r = rstd[:, ttl, :]
                P.op(DVE, lambda e, bk=bk, r=r: e.tensor_scalar(out=r, in0=bk[:, :], scalar1=1.0 / D, scalar2=EPS,
                                                                op0=ALU.mult, op1=ALU.add),
                     reads=[bb], writes=[B_rstd[ttl]])
                P.op(ACT, lambda e, r=r: e.activation(out=r, in_=r, func=AF.Sqrt),
                     reads=[B_rstd[ttl]], writes=[B_rstd[ttl]])
                P.op(DVE, lambda e, r=r: e.reciprocal(out=r, in_=r), reads=[B_rstd[ttl]], writes=[B_rstd[ttl]])

        def norm_apply(slot, gcol):
            for ttl in range(2):
                r = rstd[:, ttl, :]
                for c in range(8):
                    tok0 = slot * SL + ttl * TT
                    P.op(DVE, lambda e, c=c, r=r, tok0=tok0, ttl=ttl: e.scalar_tensor_tensor(
                        out=xn[:, slot, c, ttl * TT:(ttl + 1) * TT], in0=hT[:, c, tok0:tok0 + TT],
                        scalar=vecs_sb[:, gcol + c:gcol + c + 1], in1=r, op0=ALU.mult, op1=ALU.mult),
                        reads=[B_hT[c][2 * slot + ttl], B_rstd[ttl], B_const], writes=[B_xn[slot][ttl]])

        def norm(slot, gcol):
            norm_sq(slot, gcol)
            norm_stat(slot)
            norm_apply(slot, gcol)

        def ffn(slot, wg, wu, wd, hook=None):
            halves = [(0, 12), (12, 10)]
            for hi, (f0, nf) in enumerate(halves):
                nblk = nf // 2
                pend = []

                def issue(b):
                    g = load_wsm(wg, (f0 + 2 * b) * 128)
                    u = load_wsm(wu, (f0 + 2 * b) * 128)
                    pend.append((g, u))

                issue(0)
                for b in range(nblk):
                    if b + 1 < nblk:
                        issue(b + 1)
                    (gw, gB), (uw, uB) = pend.pop(0)
                    for fl in range(2):
                        fcl = 2 * b + fl
                        res = {}
                        for name, w_, wB in (("g", gw, gB), ("u", uw, uB)):
                            for ttl in range(2):
                                bi, bk, bb = bank_alloc()
                                fns = []
                                for c in range(8):
                                    fns.append(lambda e, c=c, bk=bk, w_=w_, ttl=ttl, fl=fl: e.matmul(
                                        bk[:, :], lhsT=w_[:, c, fl * 128:(fl + 1) * 128],
                                        rhs=xn[:, slot, c, ttl * TT:(ttl + 1) * TT], start=(c == 0), stop=(c == 7)))
                                P.group(PE, fns, reads=[wB, B_xn[slot][ttl]], writes=[bb])
                                res[(name, ttl)] = (bk, bb)
                        for ttl in range(2):
                            gk, gb = res[("g", ttl)]
                            uk, ub = res[("u", ttl)]
                            t, tB = tmp_alloc()
                            P.op(ACT, lambda e, gk=gk, t=t: e.activation(out=t, in_=gk[:, :], func=AF.Silu),
                                 reads=[gb], writes=[tB])
                            P.op(DVE, lambda e, t=t, uk=uk, fcl=fcl, ttl=ttl: e.tensor_tensor(
                                out=h1[:, fcl, ttl * TT:(ttl + 1) * TT], in0=t, in1=uk[:, :], op=ALU.mult),
                                reads=[tB, ub], writes=[B_h1[fcl]])
                    ws_release(gB)
                    ws_release(uB)
                    if hook is not None and hi == 0 and b == 2:
                        hook()
                dpend = []

                def dissue(db):
                    dpend.append(load_wdn(wd, f0, nf, db * 256))

                dissue(0)
                for db in range(4):
                    if db + 1 < 4:
                        dissue(db + 1)
                    dw, dB = dpend.pop(0)
                    for dl in range(2):
                        dc = 2 * db + dl
                        for ttl in range(2):
                            bi, bk, bb = bank_alloc()
                            fns = []
                            for f in range(nf):
                                fns.append(lambda e, f=f, bk=bk, dw=dw, dl=dl, ttl=ttl, nf=nf: e.matmul(
                                    bk[:, :], lhsT=dw[:, f, dl * 128:(dl + 1) * 128],
                                    rhs=h1[:, f, ttl * TT:(ttl + 1) * TT], start=(f == 0), stop=(f == nf - 1)))
                            P.group(PE, fns, reads=[dB] + B_h1[:nf], writes=[bb])
                            tok0 = slot * SL + ttl * TT
                            hb = B_hT[dc][2 * slot + ttl]
                            P.op(DVE, lambda e, bk=bk, dc=dc, tok0=tok0: e.scalar_tensor_tensor(
                                out=hT[:, dc, tok0:tok0 + TT], in0=bk[:, :], scalar=0.5, in1=hT[:, dc, tok0:tok0 + TT],
                                op0=ALU.mult, op1=ALU.add), reads=[bb, hb], writes=[hb])
                    wd_release(dB)

        class _Stop(Exception):
            pass

        def body():
            if STOP < 1:
                raise _Stop()
            B_ffn_arena = B_h1 + B_wdn
            norm(0, 0)
            norm(1, 0)
            ffn(0, w_g1, w_u1, w_d1)
            ffn(1, w_g1, w_u1, w_d1, hook=lambda: norm(0, 8))
            norm(1, 8)

            if STOP < 2:
                raise _Stop()
            attn_bufs = ([b for r in B_QT for b in r] + [b for r in B_ycv for b in r] + B_Kb + B_Vb + B_E + B_kst
                         + B_vst + [B_ybf])
            alias_barrier(B_ffn_arena, attn_bufs)
            for s in range(4):
                P.op(POOL, lambda e, s=s: e.memset(vst[:, s, :, 128:132], 1.0), writes=[B_vst[s]])

            def xn_rhs(c, tt):
                return xn[:, tt // 2, c, (tt % 2) * TT:(tt % 2 + 1) * TT]

            def proj_mm(w_, wB, fl, tt):
                bi, bk, bb = bank_alloc()
                fns = []
                for c in range(8):
                    fns.append(lambda e, c=c, bk=bk: e.matmul(bk[:, :], lhsT=w_[:, c, fl * 128:(fl + 1) * 128],
                                                              rhs=xn_rhs(c, tt), start=(c == 0), stop=(c == 7)))
                P.group(PE, fns, reads=[wB, B_xn[tt // 2][tt % 2]], writes=[bb])
                return bk, bb

            wq = []
            order = [10, 11, 12, 13, 14, 15, 6, 7, 8, 9, 0, 2, 3, 1, 4, 5]
            blocks = {}
            oi = {"i": 0}

            def prefetch_in(n=1):
                for _ in range(n):
                    if oi["i"] < len(order):
                        b = order[oi["i"]]
                        oi["i"] += 1
                        blocks[b] = load_wsm(w_inx, b * 256)

            prefetch_in(4)

            def load_rope(tt):
                i = nxt("rope", 2)
                P.dma(SP, S_rope[i], lambda e: e.dma_start(out=ropes[:, i, 0:2 * TT], in_=rope[:, tt, :]), writes=[B_rope[i]])
                return ropes[:, i, 0:2 * TT], B_rope[i]

            def rope_proj(kind, pre=None):
                base = 6 if kind == "q" else 10
                loaded = dict(pre or {})

                def get_rope(i):
                    if i not in loaded:
                        loaded[i] = load_rope(i % 4)
                    return loaded[i]

                for g in range(2):
                  hblocks = {h: blocks.pop(base + h) for h in (2 * g, 2 * g + 1)}
                  for tt in range(4):
                    i_ = g * 4 + tt
                    rp, rB = get_rope(i_)
                    if i_ + 1 < 8:
                        get_rope(i_ + 1)
                    for h in (2 * g, 2 * g + 1):
                        w_, wB = hblocks[h]
                        qk, qb = proj_mm(w_, wB, 0, tt)
                        rk, rb = proj_mm(w_, wB, 1, tt)
                        t1, t1B = tmp_alloc()
                        t2, t2B = tmp_alloc()
                        P.op(DVE, lambda e, qk=qk, t1=t1, rp=rp: e.tensor_tensor(out=t1, in0=qk[:, :], in1=rp[:, 0:TT],
                                                                                 op=ALU.mult),
                             reads=[qb, rB], writes=[t1B])
                        P.op(DVE, lambda e, rk=rk, t2=t2, rp=rp: e.tensor_tensor(out=t2, in0=rk[:, :], in1=rp[:, TT:2 * TT],
                                                                                 op=ALU.mult),
                             reads=[rb, rB], writes=[t2B])
                        if kind == "q":
                            P.op(DVE, lambda e, t1=t1, t2=t2, h=h, tt=tt: e.tensor_tensor(
                                out=QT[:, h, tt * TT:(tt + 1) * TT], in0=t1, in1=t2, op=ALU.add),
                                reads=[t1B, t2B], writes=[B_QT[h][tt]])
                        else:
                            si = nxt("kst", 2)
                            P.op(DVE, lambda e, t1=t1, t2=t2, si=si: e.tensor_tensor(out=kst[:, si, :], in0=t1, in1=t2,
                                                                                     op=ALU.add),
                                 reads=[t1B, t2B], writes=[B_kst[si]])
                            dst = kT_mine[tt // 2].ap()[:, h * SL + (tt % 2) * TT:h * SL + (tt % 2 + 1) * TT]
                            P.dma(SP, S_kst[si], lambda e, si=si, dst=dst: e.dma_start(out=dst, in_=kst[:, si, :]),
                                  reads=[B_kst[si]], writes=[B_kTm[h][tt]])
                  for h in (2 * g, 2 * g + 1):
                    ws_release(hblocks[h][1])
                  prefetch_in(2)


            rope_proj("k")
            q_pre = {0: load_rope(0), 1: load_rope(1)}
            wv0, wv0B = blocks.pop(14)
            wv1, wv1B = blocks.pop(15)
            v_view = [[v_mine[s_][hp].ap().rearrange("(h p) (k e) -> k p h e", h=2, p=128, k=8) for hp in range(2)]
                  for s_ in range(2)]
            def v_gather(s_):
                for hp in range(2):
                    P.dma(POOL, S_ccv[2 * s_ + hp], lambda e, s_=s_, hp=hp: e.collective_compute(
                        "AllGather", ALU.bypass, replica_groups=GROUPS, dma_qos="P2", ins=[v_mine[s_][hp].ap().opt()],
                        outs=[v_all[s_][hp].ap().opt()]),
                        reads=B_vm[8 * s_:8 * s_ + 8], writes=[B_vall[2 * s_ + hp]])

            for ts in range(16):
                s_, kt = ts // 8, ts % 8
                si = nxt("vst", 4)
                for half, (wv, wvB) in enumerate(((wv0, wv0B), (wv1, wv1B))):
                    bi, bk, bb = bank_alloc()
                    fns = []
                    for c in range(8):
                        fns.append(lambda e, c=c, bk=bk, wv=wv, s_=s_, kt=kt: e.matmul(
                            bk[:, 0:256], lhsT=xn[:, s_, c, kt * 128:(kt + 1) * 128], rhs=wv[:, c, :],
                            start=(c == 0), stop=(c == 7)))
                    P.group(PE, fns, reads=[wvB, B_xn[s_][kt // 4]], writes=[bb])
                    P.op(ACT, lambda e, bk=bk, si=si, half=half: e.activation(
                        out=vst[:, si, 2 * half:2 * half + 2, 0:128],
                        in_=bk[:, 0:256].rearrange("p (h e) -> p h e", h=2), func=AF.Copy),
                        reads=[bb], writes=[B_vst[si]])
                for hp in range(2):
                    P.dma(SP, S_vst[si], lambda e, si=si, s_=s_, kt=kt, hp=hp: e.dma_start(
                        out=v_view[s_][hp][kt], in_=vst[:, si, 2 * hp:2 * hp + 2, :]),
                        reads=[B_vst[si]], writes=[B_vm[ts]] if hp == 1 else [])
            ws_release(wv0B)
            ws_release(wv1B)
            prefetch_in(2)

            if STOP < 3:
                raise _Stop()
            for s_ in range(2):
                P.dma(POOL, S_cck[s_], lambda e, s_=s_: e.collective_compute(
                    "AllGather", ALU.bypass, replica_groups=GROUPS, dma_qos="P2", ins=[kT_mine[s_].ap().opt()],
                    outs=[kT_all[s_].ap().opt()]),
                    reads=[B_kTm[h][2 * s_ + t] for h in range(4) for t in range(2)], writes=[B_kTall[s_]])
                v_gather(s_)

            rope_proj("q", pre=q_pre)

            for cc in range(4):
                if cc == 0:
                    wb_, wbB = blocks.pop(0)
                elif cc == 2:
                    wb_, wbB = blocks.pop(1)
                wcx, wcxB = blocks.pop(2 + cc)
                for slot in range(2):
                    for ttl in range(2):
                        tt = 2 * slot + ttl
                        ck, cb_ = proj_mm(wcx, wcxB, 0, tt)
                        xk, xb_ = proj_mm(wcx, wcxB, 1, tt)
                        t, tB = tmp_alloc()
                        P.op(ACT, lambda e, ck=ck, t=t: e.activation(out=t, in_=ck[:, :], func=AF.Copy),
                             reads=[cb_], writes=[tB])
                        P.op(DVE, lambda e, t=t, xk=xk, ttl=ttl: e.tensor_tensor(
                            out=ubuf[:, 2 + ttl * TT:2 + (ttl + 1) * TT], in0=t, in1=xk[:, :], op=ALU.mult),
                            reads=[tB, xb_, B_u], writes=[B_u])
                    w0 = vecs_sb[:, 40 + 0 + cc:40 + 0 + cc + 1]
                    w1 = vecs_sb[:, 40 + 4 + cc:40 + 4 + cc + 1]
                    w2 = vecs_sb[:, 40 + 8 + cc:40 + 8 + cc + 1]
                    P.op(DVE, lambda e, w2=w2: e.tensor_scalar(out=accb[:, :], in0=ubuf[:, 2:SL + 2], scalar1=w2,
                                                               scalar2=None, op0=ALU.mult),
                         reads=[B_u, B_const, B_acc], writes=[B_acc])
                    P.op(DVE, lambda e, w1=w1: e.scalar_tensor_tensor(out=accb[:, :], in0=ubuf[:, 1:SL + 1], scalar=w1,
                                                                      in1=accb[:, :], op0=ALU.mult, op1=ALU.add),
                         reads=[B_u, B_const, B_acc], writes=[B_acc])
                    P.op(DVE, lambda e, w0=w0: e.scalar_tensor_tensor(out=accb[:, :], in0=ubuf[:, 0:SL], scalar=w0,
                                                                      in1=accb[:, :], op0=ALU.mult, op1=ALU.add),
                         reads=[B_u, B_const, B_acc], writes=[B_acc])
                    hcol = (cc * 2 + slot) * 2
                    P.op(DVE, lambda e, hcol=hcol: e.tensor_copy(out=halo_st[:, hcol:hcol + 2], in_=ubuf[:, SL:SL + 2]),
                         reads=[B_u, B_halo], writes=[B_halo])
                    P.op(DVE, lambda e, hcol=hcol: e.tensor_copy(out=acc01[:, hcol:hcol + 2], in_=accb[:, 0:2]),
                         reads=[B_acc, B_halo], writes=[B_halo])
                    for ttl in range(2):
                        tt = 2 * slot + ttl
                        bk_, bb_ = proj_mm(wb_, wbB, cc % 2, tt)
                        if ttl == 0:
                            P.op(DVE, lambda e, bk_=bk_, hcol=hcol: e.tensor_copy(out=cb01[:, hcol:hcol + 2],
                                                                                   in_=bk_[:, 0:2]),
                                 reads=[bb_, B_halo], writes=[B_halo])
                        P.op(DVE, lambda e, bk_=bk_, ttl=ttl, tt=tt, cc=cc: e.tensor_tensor(
                            out=ycv[:, cc, tt * TT:(tt + 1) * TT], in0=accb[:, ttl * TT:(ttl + 1) * TT], in1=bk_[:, :],
                            op=ALU.mult), reads=[bb_, B_acc], writes=[B_ycv[cc][tt]])
                ws_release(wcxB)
                if cc % 2 == 1:
                    ws_release(wbB)
                prefetch_in(1 if cc == 0 else 2)
            P.dma(SP, S_halo, lambda e: e.dma_start(out=halo_mine.ap(), in_=halo_st[:, :]), reads=[B_halo],
                  writes=[B_halom])
            P.dma(POOL, S_cc[2], lambda e: e.collective_compute("AllGather", ALU.bypass, replica_groups=GROUPS, dma_qos="P2",
                                                                ins=[halo_mine.ap().opt()], outs=[halo_all.ap().opt()]),
                  reads=[B_halom], writes=[B_haloall])

            if STOP < 4:
                raise _Stop()
            alias_barrier(B_xn[0] + B_xn[1], [b for r in B_yat for b in r])
            alias_barrier(B_vst, B_Kb + B_Vb)
            cmask = arena[:, A0 + 8288:A0 + 8288 + 2048].rearrange("p (m q) -> p m q", m=4)
            B_cmask = Buf()
            attn_bufs.append(B_cmask)
            alias_barrier(B_kst, [B_cmask])
            P.op(POOL, lambda e: e.memset(cmask[:, :, :], 0.0), writes=[B_cmask])
            for mm_ in range(4):
                P.op(POOL, lambda e, mm_=mm_: e.affine_select(
                    out=cmask[:, mm_, :], in_=cmask[:, mm_, :], pattern=[[1, TT]], compare_op=ALU.is_ge,
                    fill=NEG, base=-128 * mm_, channel_multiplier=-1), reads=[B_cmask], writes=[B_cmask])
            SEL = 384
            O_BANKS = [0, 1, 2]
            ST_BANKS = [3, 4, 5, 6]
            st_rr = {"i": 0}
            obanks_B = [B_bank[b] for b in O_BANKS]

            def oacc(a):
                b = a // 3
                off = (a % 3) * 129
                return banks[O_BANKS[b]][:, off:off + 129]

            oab = ropes
            B_oab = [Buf(), Buf()]
            alias_barrier(B_rope, B_oab)
            qsel = wsm[:, 0:3, :].rearrange("p s f -> p (s f)").rearrange("p (b v q) -> p b v q", b=2, v=3)
            B_qsel = [Buf(), Buf()]
            alias_barrier(B_ws, B_qsel)
            kaux = xn[:, 1, 0:2, :].rearrange("p c t -> p (c t)").rearrange("p (s k) -> p s k", s=2)
            vaux = xn[:, 1, 2:5, :].rearrange("p c t -> p (c t)")[:, 0:2112].rearrange("p (s k) -> p s k", s=2)
            B_kaux = [Buf(), Buf()]
            B_vaux = [Buf(), Buf()]
            alias_barrier(B_xn[1], B_kaux + B_vaux)
            S_kaux = [dsem(f"d_kx{i}") for i in range(2)]
            S_vaux = [dsem(f"d_vx{i}") for i in range(2)]

            def sAcol(v):
                return bc_sb[:, SEL + v:SEL + v + 1]

            def sBcol(v):
                return bc_sb[:, SEL + 3 + v:SEL + 3 + v + 1]

            work = []
            for h in range(4):
                for qt in range(2):
                    units = []
                    units.append(("B", [("own", None, 1)] + [("g", r, 0) for r in range(4)]))
                    units.append(("A", [("own", None, 0)]))
                    for v in range(3):
                        units.append((("V", v), [("v", v, None)]))
                    for ui, (ukind, chunks) in enumerate(units):
                        for ci, (kind, r, ks) in enumerate(chunks):
                            if kind == "own":
                                kts = list(range(0, 4 * qt + 4))
                            else:
                                kts = list(range(8))
                            work.append(dict(h=h, qt=qt, ukind=ukind, kind=kind, r=r, ks=ks, kts=kts,
                                             first=(ci == 0), lastc=(ci == len(chunks) - 1),
                                             lastu=(ui == len(units) - 1)))
            nv = {"n": 0}

            def kv_issue(w, i):
                h, ks, r = w["h"], w["ks"], w["r"]
                if w["kind"] == "own":
                    src_k = kT_mine[ks].ap()[:, h * SL:(h + 1) * SL]
                    src_v = v_mine[ks][h // 2].ap()[(h % 2) * 128:(h % 2 + 1) * 128, :]
                    kdeps = [B_kTm[h][2 * ks], B_kTm[h][2 * ks + 1]]
                    vdeps = B_vm[8 * ks:8 * ks + 8]
                else:
                    if w["kind"] == "v":
                        r, ks = w["r"], 0
                    src_k = kT_all[ks].ap()[r * 128:(r + 1) * 128, h * SL:(h + 1) * SL]
                    src_v = v_all[ks][h // 2].ap()[r * 256 + (h % 2) * 128:r * 256 + (h % 2 + 1) * 128, :]
                    kdeps = [B_kTall[ks]]
                    vdeps = [B_vall[2 * ks + h // 2]]
                P.dma(SP, S_Kb[i], lambda e: e.dma_start(out=Kb[:, i, :], in_=src_k), reads=kdeps, writes=[B_Kb[i]])
                P.dma(SP, S_Vb[i], lambda e: e.dma_start(out=Vb[:, i, :, :].rearrange("p k e -> p (k e)"), in_=src_v),
                      reads=vdeps, writes=[B_Vb[i]])
                if w["kind"] == "v":
                    ax = nv["n"] % 2
                    nv["n"] += 1
                    w["ax"] = ax
                    r2 = w["r"] + 1
                    src_k2 = kT_all[1].ap()[r2 * 128:(r2 + 1) * 128, h * SL:(h + 1) * SL]
                    src_v2 = v_all[1][h // 2].ap()[r2 * 256 + (h % 2) * 128:r2 * 256 + (h % 2 + 1) * 128, :]
                    P.dma(SP, S_kaux[ax], lambda e: e.dma_start(out=kaux[:, ax, :], in_=src_k2),
                          reads=[B_kTall[1]], writes=[B_kaux[ax]])
                    P.dma(SP, S_vaux[ax], lambda e: e.dma_start(out=vaux[:, ax, :], in_=src_v2),
                          reads=[B_vall[2 + h // 2]], writes=[B_vaux[ax]])

            issued = {"n": 0}

            def ensure_loaded(upto):
                while issued["n"] < len(work) and issued["n"] <= upto:
                    kv_issue(work[issued["n"]], issued["n"] % 3)
                    issued["n"] += 1

            tiles = []
            for wi, w in enumerate(work):
                for ki, kt in enumerate(w["kts"]):
                    tiles.append((wi, ki, kt))

            def make_qsel(h):
                hb = h % 2
                for v in range(3):
                    P.op(DVE, lambda e, hb=hb, v=v, h=h: e.tensor_scalar(
                        out=qsel[:, hb, v, :], in0=QT[:, h, 0:SL], scalar1=sAcol(v), scalar2=None, op0=ALU.mult),
                        reads=[B_QT[h][0], B_QT[h][1], B_const], writes=[B_qsel[hb]])
                    P.op(DVE, lambda e, hb=hb, v=v, h=h: e.scalar_tensor_tensor(
                        out=qsel[:, hb, v, :], in0=QT[:, h, SL:2 * SL], scalar=sBcol(v), in1=qsel[:, hb, v, :],
                        op0=ALU.mult, op1=ALU.add),
                        reads=[B_QT[h][2], B_QT[h][3], B_const, B_qsel[hb]], writes=[B_qsel[hb]])

            def blend_kv(w, bi):
                v, ax = w["r"], w["ax"]
                P.op(DVE, lambda e, bi=bi, v=v: e.tensor_scalar(
                    out=Kb[:, bi, :], in0=Kb[:, bi, :], scalar1=sAcol(v), scalar2=None, op0=ALU.mult),
                    reads=[B_Kb[bi], B_const], writes=[B_Kb[bi]])
                P.op(DVE, lambda e, bi=bi, v=v, ax=ax: e.scalar_tensor_tensor(
                    out=Kb[:, bi, :], in0=kaux[:, ax, :], scalar=sBcol(v), in1=Kb[:, bi, :], op0=ALU.mult,
                    op1=ALU.add), reads=[B_Kb[bi], B_kaux[ax], B_const], writes=[B_Kb[bi]])
                vflat = Vb[:, bi, :, :].rearrange("p k e -> p (k e)")
                P.op(DVE, lambda e, vflat=vflat, v=v: e.tensor_scalar(
                    out=vflat, in0=vflat, scalar1=sAcol(v), scalar2=None, op0=ALU.mult),
                    reads=[B_Vb[bi], B_const], writes=[B_Vb[bi]])
                P.op(DVE, lambda e, vflat=vflat, v=v, ax=ax: e.scalar_tensor_tensor(
                    out=vflat, in0=vaux[:, ax, :], scalar=sBcol(v), in1=vflat, op0=ALU.mult, op1=ALU.add),
                    reads=[B_Vb[bi], B_vaux[ax], B_const], writes=[B_Vb[bi]])

            def emit_qk(wi, ki, kt):
                w = work[wi]
                bi = wi % 3
                h, qt = w["h"], w["qt"]
                if ki == 0:
                    if w["ukind"] == "B" and qt == 0 and w["first"]:
                        make_qsel(h)
                if w["kind"] == "v":
                    v = w["r"]
                    q_ap = lambda m, h=h, qt=qt, v=v: qsel[m * 64:(m + 1) * 64, h % 2, v, qt * TT:(qt + 1) * TT]
                    q_B = [B_qsel[h % 2]]
                else:
                    slot = 0 if w["ukind"] == "A" else 1
                    tt = 2 * slot + qt
                    q_ap = lambda m, h=h, tt=tt: QT[m * 64:(m + 1) * 64, h, tt * TT:(tt + 1) * TT]
                    q_B = [B_QT[h][tt]]
                crossing = (w["kind"] == "own") and (kt >= 4 * qt)
                es_ = []
                for m in range(2):
                    sb_i = ST_BANKS[m * 2 + (st_rr["i"] % 2)]
                    stb = banks[sb_i]
                    qlo = 128 * (kt - 4 * qt) if crossing else 0
                    fns = [lambda e, stb=stb, m=m, bi=bi, kt=kt, q_ap=q_ap, crossing=crossing, qlo=qlo: e.matmul(
                        stb[:, qlo:TT], lhsT=Kb[m * 64:(m + 1) * 64, bi, kt * 128:(kt + 1) * 128],
                        rhs=q_ap(m)[:, qlo:TT], start=True, stop=not crossing)]
                    rds = [B_Kb[bi]] + q_B
                    if crossing:
                        mm_ = kt - 4 * qt
                        fns.append(lambda e, stb=stb, mm_=mm_, qlo=qlo: e.matmul(
                            stb[:, qlo:TT], lhsT=ident[:, :], rhs=cmask[:, mm_, qlo:TT], start=False, stop=True))
                        rds = rds + [B_cmask, B_const]
                    P.group(PE, fns, reads=rds, writes=[B_bank[sb_i]])
                    ei = nxt("E", 4)
                    P.op(ACT, lambda e, stb=stb, ei=ei, qlo=qlo: e.activation(
                        out=Eb[:, ei, qlo:TT], in_=stb[:, qlo:TT], func=AF.Exp, scale=0.125),
                        reads=[B_bank[sb_i]], writes=[B_E[ei]])
                    es_.append(ei)
                st_rr["i"] += 1
                return tuple(es_)

            def emit_av(wi, ki, kt, es_):
                w = work[wi]
                bi = wi % 3
                first = w["first"] and ki == 0
                last = w["lastc"] and ki == len(w["kts"]) - 1
                crossing = (w["kind"] == "own") and (kt >= 4 * w["qt"])
                qs_lo = (kt - 4 * w["qt"]) if crossing else 0
                for b3 in range(3):
                    fns = []
                    for a in range(3 * b3, min(3 * b3 + 3, 8)):
                        qs, m = a // 2, a % 2
                        if qs < qs_lo:
                            continue
                        st_flag = first and (a % 3 == 0)
                        fns.append(lambda e, a=a, m=m, qs=qs, es_=es_, bi=bi, kt=kt, st_flag=st_flag,
                                   last=last: e.matmul(
                            oacc(a), lhsT=Eb[:, es_[m], qs * 128:(qs + 1) * 128], rhs=Vb[:, bi, kt, 0:129],
                            start=st_flag, stop=last, skip_group_check=True))
                    if fns:
                        P.group(PE, fns, reads=[B_E[es_[0]], B_E[es_[1]], B_Vb[bi]], writes=[obanks_B[b3]])
                if ki == len(w["kts"]) - 1:
                    ensure_loaded(wi + 3)
                    if wi + 2 < len(work) and work[wi + 2]["kind"] == "v":
                        blend_kv(work[wi + 2], (wi + 2) % 3)
                if last:
                    unit_done(w)

            def unit_done(w):
                uk = w["ukind"]
                for b3 in range(3):
                    ncol = 387 if b3 < 2 else 258
                    src = banks[O_BANKS[b3]][:, 0:ncol]
                    if uk in ("A", "B"):
                        sl = 0 if uk == "A" else 1
                        P.op(DVE, lambda e, src=src, sl=sl, b3=b3, ncol=ncol: e.tensor_copy(
                            out=oab[:, sl, 387 * b3:387 * b3 + ncol], in_=src),
                            reads=[obanks_B[b3]], writes=[B_oab[sl]])
                    else:
                        v = uk[1]
                        for sl, scol in ((0, sAcol(v)), (1, sBcol(v))):
                            dst = oab[:, sl, 387 * b3:387 * b3 + ncol]
                            P.op(DVE, lambda e, src=src, dst=dst, scol=scol: e.scalar_tensor_tensor(
                                out=dst, in0=src, scalar=scol, in1=dst, op0=ALU.mult, op1=ALU.add),
                                reads=[obanks_B[b3], B_oab[sl], B_const], writes=[B_oab[sl]])
                if w["lastu"]:
                    post(w["h"], w["qt"], 0)
                    post(w["h"], w["qt"], 1)

            cur = {"ti": 0}
            otile = tmp[:, :, :].rearrange("p a (b e) -> p (a b) e", e=128)
            B_otile = [Buf() for _ in range(4)]
            alias_barrier(B_tmp, B_otile)
            par_ctr = {"n": 0}

            def post(h, qt, slot):
                tt = 2 * slot + qt
                q0 = tt * TT
                oB = B_oab[slot]
                par = (par_ctr["n"] // 2) % 2
                par_ctr["n"] += 1
                ps = par * 2 + slot
                oTB = B_otile[ps]
                c_ss = 40 + ps * 4
                c_rs = 56 + ps * 4

                def osb(a):
                    return oab[:, slot, a * 129:(a + 1) * 129]

                P.op(DVE, lambda e: e.tensor_copy(
                    out=small[:, 8:16], in_=oab[:, slot, 0:1032].rearrange("p (a e) -> p a e", e=129)[:, :, 128]),
                    reads=[oB, B_small], writes=[B_small])
                P.op(DVE, lambda e: e.reciprocal(out=small[:, 16:24], in_=small[:, 8:16]),
                     reads=[B_small], writes=[B_small])
                P.op(DVE, lambda e: e.tensor_scalar(
                    out=small[:, 24:28], in0=small[:, 16:24].rearrange("p (q m) -> p q m", m=2)[:, :, 1],
                    scalar1=LAM, scalar2=None, op0=ALU.mult), reads=[B_small, B_const], writes=[B_small])
                for qs in range(4):
                    oi_ = nxt("ob", 2)
                    tb = tbuf[:, oi_, :]
                    ob = otile[:, ps * 4 + qs, :]
                    P.op(DVE, lambda e, qs=qs, tb=tb: e.tensor_scalar(
                        out=tb, in0=osb(qs * 2 + 1)[:, 0:128], scalar1=small[:, 24 + qs:25 + qs], scalar2=None,
                        op0=ALU.mult), reads=[oB, B_small], writes=[B_tbuf[oi_]])
                    P.op(DVE, lambda e, qs=qs, tb=tb, ob=ob: e.scalar_tensor_tensor(
                        out=ob, in0=osb(qs * 2)[:, 0:128], scalar=small[:, 16 + 2 * qs:17 + 2 * qs], in1=tb,
                        op0=ALU.mult, op1=ALU.subtract), reads=[oB, B_small, B_tbuf[oi_]],
                        writes=[oTB])
                    P.op(DVE, lambda e, qs=qs, ob=ob, tb=tb: e.scalar_tensor_tensor(
                        out=tb, in0=ob, scalar=1.0, in1=ob, op0=ALU.mult, op1=ALU.mult,
                        accum_out=small[:, 28 + qs:29 + qs]), reads=[oTB, B_small],
                        writes=[B_tbuf[oi_], B_small])
                P.op(DVE, lambda e: e.tensor_scalar(
                    out=small[:, c_ss:c_ss + 4], in0=small[:, 28:32], scalar1=1.0 / 128.0,
                    scalar2=EPS, op0=ALU.mult, op1=ALU.add), reads=[B_small], writes=[oTB])

                def stage2_fn():
                    P.op(ACT, lambda e: e.activation(out=small[:, c_ss:c_ss + 4], in_=small[:, c_ss:c_ss + 4],
                                                     func=AF.Ln), reads=[oTB], writes=[oTB])
                    P.op(ACT, lambda e: e.activation(out=small[:, c_rs:c_rs + 4], in_=small[:, c_ss:c_ss + 4],
                                                     func=AF.Exp, scale=-0.5), reads=[oTB], writes=[oTB])
                    for qs in range(4):
                        ob = otile[:, ps * 4 + qs, :]
                        P.op(DVE, lambda e, qs=qs, ob=ob: e.scalar_tensor_tensor(
                            out=ybf[:, qs, :], in0=ob, scalar=small[:, c_rs + qs:c_rs + qs + 1], in1=gsub[:, :],
                            op0=ALU.mult, op1=ALU.mult), reads=[oTB, B_const], writes=[B_ybf])

                    def stage2b_fn():
                        fns = []
                        for qs in range(4):
                            fns.append(lambda e, qs=qs: e.transpose(out=tpb[:, qs * 128:(qs + 1) * 128],
                                                                    in_=ybf[:, qs, :], identity=ident[:, :]))
                        P.group(PE, fns, reads=[B_ybf, B_const], writes=[B_tpb])
                        P.op(DVE, lambda e, h=h, q0=q0: e.tensor_copy(out=yat[:, h, q0:q0 + TT], in_=tpb[:, :]),
                             reads=[B_tpb], writes=[B_yat[h][tt]])

                    stage2.append((cur["ti"] + 6, stage2b_fn))
                    stage2.sort(key=lambda x: x[0])

                stage2.append((cur["ti"] + 20 + 8 * slot, stage2_fn))
                stage2.sort(key=lambda x: x[0])

            def halo_fix():
                P.dma(SP, S_halo, lambda e: e.dma_start(out=halo_sb[:, :, :],
                                                        in_=halo_all.ap().rearrange("(r p) f -> p r f", p=128)),
                      reads=[B_haloall], writes=[B_halo_sb])

                hs4 = halo_sb[:, :, :].rearrange("p r (c s t) -> p r c s t", c=4, s=2)
                hf4 = hfix[:, 0:16].rearrange("p (c s t) -> p c s t", c=4, s=2)
                P.op(DVE, lambda e: e.memset(hfix[:, :], 0.0), reads=[B_hfix], writes=[B_hfix])
                for r in range(4):
                    P.op(DVE, lambda e, r=r: e.scalar_tensor_tensor(
                        out=hf4[:, :, 0, :], in0=hs4[:, r, :, 0, :], scalar=bc_sb[:, SEL + 6 + r:SEL + 7 + r],
                        in1=hf4[:, :, 0, :], op0=ALU.mult, op1=ALU.add), reads=[B_halo_sb, B_const, B_hfix], writes=[B_hfix])
                    P.op(DVE, lambda e, r=r: e.scalar_tensor_tensor(
                        out=hf4[:, :, 1, :], in0=hs4[:, r, :, 0, :], scalar=bc_sb[:, SEL + 10 + r:SEL + 11 + r],
                        in1=hf4[:, :, 1, :], op0=ALU.mult, op1=ALU.add), reads=[B_halo_sb, B_const, B_hfix], writes=[B_hfix])
                    if r >= 1:
                        P.op(DVE, lambda e, r=r: e.scalar_tensor_tensor(
                            out=hf4[:, :, 1, :], in0=hs4[:, r, :, 1, :], scalar=bc_sb[:, 400 + (r - 1):401 + (r - 1)],
                            in1=hf4[:, :, 1, :], op0=ALU.mult, op1=ALU.add), reads=[B_halo_sb, B_const, B_hfix],
                            writes=[B_hfix])
                for cc in range(4):
                    w0 = vecs_sb[:, 40 + cc:41 + cc]
                    w1 = vecs_sb[:, 44 + cc:45 + cc]
                    for slot in range(2):
                        hc = (cc * 2 + slot) * 2
                        um2 = hfix[:, hc:hc + 1]
                        um1 = hfix[:, hc + 1:hc + 2]
                        a0 = acc01[:, hc:hc + 1]
                        a1 = acc01[:, hc + 1:hc + 2]
                        f0 = hfix[:, 16 + hc:16 + hc + 1]
                        f1 = hfix[:, 16 + hc + 1:16 + hc + 2]
                        P.op(DVE, lambda e, um1=um1, w1=w1, a0=a0, f0=f0: e.scalar_tensor_tensor(
                            out=f0, in0=um1, scalar=w1, in1=a0, op0=ALU.mult, op1=ALU.add),
                            reads=[B_hfix, B_halo, B_const], writes=[B_hfix])
                        P.op(DVE, lambda e, um2=um2, w0=w0, f0=f0: e.scalar_tensor_tensor(
                            out=f0, in0=um2, scalar=w0, in1=f0, op0=ALU.mult, op1=ALU.add),
                            reads=[B_hfix, B_const], writes=[B_hfix])
                        P.op(DVE, lambda e, um1=um1, w0=w0, a1=a1, f1=f1: e.scalar_tensor_tensor(
                            out=f1, in0=um1, scalar=w0, in1=a1, op0=ALU.mult, op1=ALU.add),
                            reads=[B_hfix, B_halo, B_const], writes=[B_hfix])
                        tok0 = slot * SL
                        P.op(DVE, lambda e, cc=cc, tok0=tok0, hc=hc: e.tensor_tensor(
                            out=ycv[:, cc, tok0:tok0 + 2], in0=hfix[:, 16 + hc:16 + hc + 2], in1=cb01[:, hc:hc + 2],
                            op=ALU.mult), reads=[B_hfix, B_halo], writes=[B_ycv[cc][2 * slot]])


            ensure_loaded(2)
            pend = None
            stage2 = []
            for ti, (wi, ki, kt) in enumerate(tiles):
                es_ = emit_qk(wi, ki, kt)
                if pend is not None:
                    emit_av(*pend)
                pend = (wi, ki, kt, es_)
                cur["ti"] = ti
                if ti == 150:
                    halo_fix()
                while stage2 and stage2[0][0] <= ti:
                    stage2.pop(0)[1]()
            emit_av(*pend)
            while stage2:
                stage2.pop(0)[1]()
            alias_barrier(B_qsel, B_ws)
            alias_barrier(B_kaux + B_vaux, B_xn[1])
            alias_barrier(B_oab, B_rope)
            alias_barrier(B_otile, B_tmp)


            if STOP < 5:
                raise _Stop()
            oblocks = [load_wsm(w_out, b * 256) for b in range(4)]
            for slot in (1, 0):
                for b in range(4):
                    ow, oB = oblocks[b]
                    for dl in range(2):
                        dc = 2 * b + dl
                        for ttl in range(2):
                            tt = 2 * slot + ttl
                            bi, bk, bb = bank_alloc()
                            fns = []
                            for c in range(8):
                                src = ycv[:, c, tt * TT:(tt + 1) * TT] if c < 4 else yat[:, c - 4, tt * TT:(tt + 1) * TT]
                                fns.append(lambda e, c=c, bk=bk, src=src, ow=ow, dl=dl: e.matmul(
                                    bk[:, :], lhsT=ow[:, c, dl * 128:(dl + 1) * 128], rhs=src, start=(c == 0),
                                    stop=(c == 7)))
                            P.group(PE, fns, reads=[oB] + [B_ycv[c][tt] for c in range(4)]
                                    + [B_yat[c][tt] for c in range(4)], writes=[bb])
                            hb = B_hT[dc][tt]
                            P.op(DVE, lambda e, bk=bk, dc=dc, tt=tt: e.tensor_tensor(
                                out=hT[:, dc, tt * TT:(tt + 1) * TT], in0=bk[:, :], in1=hT[:, dc, tt * TT:(tt + 1) * TT],
                                op=ALU.add), reads=[bb, hb], writes=[hb])
                    if slot == 0 and b == 0 and STOP >= 6:
                        norm(1, 16)
            for b in range(4):
                ws_release(oblocks[b][1])

            if STOP < 6:
                raise _Stop()
            alias_barrier([b for r in B_yat for b in r], B_xn[0])
            alias_barrier(attn_bufs, B_ffn_arena)
            ffn(1, w_g2, w_u2, w_d2, hook=lambda: norm(0, 16))
            ffn(0, w_g2, w_u2, w_d2, hook=lambda: norm(1, 24))

            if STOP < 7:
                raise _Stop()
            alias_barrier(B_ffn_arena, [B_pT])
            P.dma(POOL, S_pT, lambda e: e.dma_start(out=pTb[:, :, :], in_=pT.rearrange("(k p) t -> p k t", p=128)),
                  writes=[B_pT])
            wpp_sb = arena[:, 24576:26624].rearrange("p (k d) -> p k d", k=2)
            wppB = Buf()
            alias_barrier(attn_bufs, [wppB, B_pT])
            S_wpp = dsem("d_wpp")
            P.dma(POOL, S_wpp, lambda e: e.dma_start(out=wpp_sb, in_=w_pp.rearrange("(k p) d -> p k d", p=128)),
                  writes=[wppB])
            gblocks = [load_wsm(w_pg, b * 256) for b in range(4)]
            outv = outT.rearrange("(c p) t -> p c t", p=128)

            def final_norm(slot):
                norm_sq(slot, 32)
                norm_stat(slot)
                for ttl in range(2):
                    tt = 2 * slot + ttl
                    r = rstd[:, ttl, :]
                    for c in range(8):
                        ti_ = nxt("tmp", 4)
                        t, tB = tmp[:, ti_, :], B_tmp[ti_]
                        P.op(DVE, lambda e, c=c, r=r, tt=tt, t=t: e.scalar_tensor_tensor(
                            out=t, in0=hT[:, c, tt * TT:(tt + 1) * TT], scalar=vecs_sb[:, 32 + c:33 + c], in1=r,
                            op0=ALU.mult, op1=ALU.mult), reads=[B_hT[c][tt], B_rstd[ttl], B_const], writes=[tB])
                        P.dma(SP, S_out[ti_], lambda e, c=c, tt=tt, t=t: e.dma_start(
                            out=outv[:, c, tt * TT:(tt + 1) * TT], in_=t), reads=[tB])

            for slot in (1, 0):
                for b in range(4):
                    gw, gB = gblocks[b]
                    for dl in range(2):
                        dc = 2 * b + dl
                        for ttl in range(2):
                            tt = 2 * slot + ttl
                            gk, gb = proj_mm(gw, gB, dl, tt)
                            bi, pk, pb = bank_alloc()
                            fns = []
                            for k in range(2):
                                fns.append(lambda e, k=k, pk=pk, dc=dc, tt=tt: e.matmul(
                                    pk[:, :], lhsT=wpp_sb[:, k, dc * 128:(dc + 1) * 128],
                                    rhs=pTb[:, k, tt * TT:(tt + 1) * TT], start=(k == 0), stop=(k == 1)))
                            P.group(PE, fns, reads=[wppB, B_pT], writes=[pb])
                            t, tB = tmp_alloc()
                            P.op(ACT, lambda e, gk=gk, t=t: e.activation(out=t, in_=gk[:, :], func=AF.Sigmoid),
                                 reads=[gb], writes=[tB])
                            P.op(DVE, lambda e, t=t, pk=pk: e.tensor_tensor(out=t, in0=t, in1=pk[:, :], op=ALU.mult),
                                 reads=[tB, pb], writes=[tB])
                            hb = B_hT[dc][tt]
                            P.op(POOL, lambda e, t=t, dc=dc, tt=tt: e.tensor_tensor(
                                out=hT[:, dc, tt * TT:(tt + 1) * TT], in0=t, in1=hT[:, dc, tt * TT:(tt + 1) * TT],
                                op=ALU.add), reads=[tB, hb], writes=[hb])
                    if slot == 1 and b == 0:
                        norm(0, 24)
                    if STOP >= 8 and slot == 0 and b == 0:
                        final_norm(1)
            if STOP >= 8:
                final_norm(0)
            for b in range(4):
                ws_release(gblocks[b][1])
            if STOP < 8:
                raise _Stop()

        stopped = False
        try:
            body()
        except _Stop:
            stopped = True
        if stopped:
            outv_dbg = outT.rearrange("(c p) t -> p c t", p=128)
            for c in range(8):
                oi_ = c % 4
                P.dma(SP, S_out[oi_], lambda e, c=c: e.dma_start(out=outv_dbg[:, c, :], in_=hT[:, c, :]),
                      reads=[B_hT[c][t] for t in range(4)])
        finals = [(d.sem, d.cnt) for d in S_out if d.cnt > 0]

        def _check():
            engs = [PE, ACT, DVE, POOL, SP]
            pc = {e.name: 0 for e in engs}
            semv = {}
            progress = True
            while progress:
                progress = False
                for e in engs:
                    while pc[e.name] < len(e.ops):
                        waits, fn, inc = e.ops[pc[e.name]]
                        if any(semv.get(id(sm), 0) < v for sm, v in waits):
                            break
                        if inc is not None:
                            semv[id(inc[0])] = semv.get(id(inc[0]), 0) + (inc[1] if inc[1] else 1)
                        pc[e.name] += 1
                        progress = True
            stuck = {e.name: (pc[e.name], len(e.ops)) for e in engs if pc[e.name] < len(e.ops)}
            if stuck:
                msg = []
                for e in engs:
                    if pc[e.name] < len(e.ops):
                        waits, fn, inc = e.ops[pc[e.name]]
                        msg.append((e.name, pc[e.name], len(e.ops),
                                    [(getattr(sm, "name", str(sm)), v, semv.get(id(sm), 0)) for sm, v in waits]))
                raise RuntimeError(f"DEADLOCK in recorded program: {msg}")
            print("sync check ok; ops per engine:", {e.name: len(e.ops) for e in engs}, flush=True)

        _check()

        with nc.Block() as block:
            @block.tensor
            def _(e):
                emit(e, PE)

            @block.scalar
            def _(e):
                emit(e, ACT)

            @block.vector
            def _(e):
                emit(e, DVE)

            @block.gpsimd
            def _(e):
                emit(e, POOL)

            @block.sync
            def _(e):
                emit(e, SP)
                for s_, v in finals:
                    e.wait_ge(s_, v)
    return nc


def _prep_shared(inp):
    f32 = np.float32
    w_in = np.asarray(inp["w_in"][0], f32)
    cb, cc_, cx = w_in[:, 0:512], w_in[:, 512:1024], w_in[:, 1024:1536]
    q, k, v = w_in[:, 1536:2048], w_in[:, 2048:2560], w_in[:, 2560:3072]
    idx = np.arange(512)
    dloc = idx % 64
    partner = idx - dloc + (dloc + 32) % 64
    rq, rk = q[:, partner], k[:, partner]
    blocks = []
    blocks.append(np.concatenate([cb[:, 0:128], cb[:, 128:256]], 1))
    blocks.append(np.concatenate([cb[:, 256:384], cb[:, 384:512]], 1))
    for c in range(4):
        blocks.append(np.concatenate([cc_[:, c * 128:(c + 1) * 128], cx[:, c * 128:(c + 1) * 128]], 1))
    for h in range(4):
        blocks.append(np.concatenate([q[:, h * 128:(h + 1) * 128], rq[:, h * 128:(h + 1) * 128]], 1))
    for h in range(4):
        blocks.append(np.concatenate([k[:, h * 128:(h + 1) * 128], rk[:, h * 128:(h + 1) * 128]], 1))
    blocks.append(v[:, 0:256])
    blocks.append(v[:, 256:512])
    w_inx = np.ascontiguousarray(np.concatenate(blocks, 1))

    vecs = np.zeros((128, 64), f32)

    def put(col, vec):
        vecs[:, col:col + 8] = np.asarray(vec, f32).reshape(8, 128).T

    put(0, inp["ffn1_norm"][0]); put(8, inp["mix_norm"][0]); put(16, inp["ffn2_norm"][0])
    put(24, inp["ple_norm"][0]); put(32, inp["final_norm"])
    cw = np.asarray(inp["conv_w"][0], f32)
    for tap in range(3):
        vecs[:, 40 + tap * 4:40 + tap * 4 + 4] = cw[tap].reshape(4, 128).T
    shared = {
        "w_g1": np.ascontiguousarray(inp["ffn1_w_gate"][0], dtype=f32), "w_u1": np.ascontiguousarray(inp["ffn1_w_up"][0], dtype=f32),
        "w_d1": np.ascontiguousarray(inp["ffn1_w_down"][0], dtype=f32),
        "w_g2": np.ascontiguousarray(inp["ffn2_w_gate"][0], dtype=f32), "w_u2": np.ascontiguousarray(inp["ffn2_w_up"][0], dtype=f32),
        "w_d2": np.ascontiguousarray(inp["ffn2_w_down"][0], dtype=f32),
        "w_inx": w_inx, "w_out": np.ascontiguousarray(inp["w_out"][0], dtype=f32),
        "w_pg": np.ascontiguousarray(inp["w_ple_gate"][0], dtype=f32), "w_pp": np.ascontiguousarray(inp["w_ple_proj"][0], dtype=f32),
        "vecs": vecs,
    }
    bc = np.zeros((128, 416), f32)
    bc[:, 0:128] = np.asarray(inp["subln_g"][0], f32)[None, :]
    bc[:, 128:192] = np.asarray(inp["lambda_q1"][0], f32)[None, :]
    bc[:, 192:256] = np.asarray(inp["lambda_k1"][0], f32)[None, :]
    bc[:, 256:320] = np.asarray(inp["lambda_q2"][0], f32)[None, :]
    bc[:, 320:384] = np.asarray(inp["lambda_k2"][0], f32)[None, :]
    return shared, bc


def _core_tokens(j):
    a = np.arange(1024 * j, 1024 * j + 1024)
    b = np.arange(1024 * (7 - j), 1024 * (7 - j) + 1024)
    return np.concatenate([a, b])


def kernel(**inp):
    f32 = np.float32
    x = np.asarray(inp["x"], f32)
    p = np.asarray(inp["p"], f32)[0]
    shared, bc0 = _prep_shared(inp)
    inv_freq = (1.0 / (np.float32(10000.0) ** (np.arange(0, 64, 2, dtype=f32) / np.float32(64)))).astype(f32)
    in_maps = []
    for c in range(8):
        b, j = c // 4, c % 4
        tok = _core_tokens(j)
        m = dict(shared)
        m["xT"] = np.ascontiguousarray(x[b, tok, :].T)
        m["pT"] = np.ascontiguousarray(p[b, tok, :].T)
        ang = (tok.astype(f32)[:, None] * inv_freq[None, :]).astype(f32)
        cos = np.cos(ang).astype(f32).T
        sin = np.sin(ang).astype(f32).T
        pi = np.arange(128)
        cosT = cos[pi % 32, :]
        sgn = np.where((pi % 64) < 32, -1.0, 1.0).astype(f32)[:, None]
        sinT = sin[pi % 32, :] * sgn
        rope = np.zeros((128, 4, 1024), f32)
        for tt in range(4):
            rope[:, tt, 0:512] = cosT[:, tt * 512:(tt + 1) * 512]
            rope[:, tt, 512:1024] = sinT[:, tt * 512:(tt + 1) * 512]
        m["rope"] = rope
        bc = bc0.copy()
        S = 384
        for v in range(3):
            bc[:, S + v] = 1.0 if v < j else 0.0
            bc[:, S + 3 + v] = 0.0 if v < j else 1.0
        for r in range(4):
            bc[:, S + 6 + r] = 1.0 if (j >= 1 and r == j - 1) else 0.0
        for r in range(4):
            bc[:, S + 10 + r] = 1.0 if (j == 3 and r == 3) else 0.0
        for r in range(1, 4):
            bc[:, 400 + (r - 1)] = 1.0 if (j <= 2 and r == j + 1) else 0.0
        m["bcast"] = bc
        in_maps.append(m)
    import os
    nc = build_nc(int(os.environ.get('KSTOP', '99')))
    res = run_bass_kernel_spmd(nc, in_maps, core_ids=list(range(8)))
    out = np.zeros((2, 8192, D), f32)
    for c in range(8):
        b, j = c // 4, c % 4
        out[b, _core_tokens(j), :] = res.results[c]["outT"].T
    return out
```

```python
import numpy as np
import concourse.bass as bass
import concourse.mybir as mybir
from concourse.bass_utils import run_bass_kernel_spmd

F32 = mybir.dt.float32
BF16 = mybir.dt.bfloat16
AF = mybir.ActivationFunctionType
ALU = mybir.AluOpType

D = 1024
DFF = 2816
NT = 2048
SL = 1024
TT = 512
EPS = 1e-6
NEG = -30000.0
GROUPS = [[0, 1, 2, 3], [4, 5, 6, 7]]


class Buf:
    __slots__ = ("lw", "rd")

    def __init__(self):
        self.lw = None
        self.rd = {}


class Eng:
    def __init__(self, name, sem, is_pe=False):
        self.name = name
        self.sem = sem
        self.cnt = 0
        self.seen = {}
        self.ops = []
        self.is_pe = is_pe
        self.step = 1


class DSem:
    def __init__(self, sem, step=16):
        self.sem = sem
        self.cnt = 0
        self.step = step
        self.is_pe = False
        self.noval = (step == 1)


class Prog:
    def __init__(self):
        self.engs = {}

    def _deps(self, eng, reads, writes):
        deps = {}

        def add(tok):
            if tok is None:
                return
            k, v = tok
            if deps.get(k, 0) < v:
                deps[k] = v

        for b in reads:
            add(b.lw)
        for b in writes:
            add(b.lw)
            for k, v in b.rd.items():
                add((k, v))
        waits = []
        for k, v in deps.items():
            if k is eng and eng.is_pe:
                continue
            if eng.seen.get(k, 0) >= v:
                continue
            eng.seen[k] = v
            waits.append((k.sem, v))
        return waits

    @staticmethod
    def _mark(tok, reads, writes):
        k, v = tok
        for b in reads:
            if b.rd.get(k, 0) < v:
                b.rd[k] = v
        for b in writes:
            b.lw = tok
            b.rd = {}

    def op(self, eng, fn, reads=(), writes=()):
        waits = self._deps(eng, reads, writes)
        eng.cnt += 1
        tok = (eng, eng.cnt)
        eng.ops.append((waits, fn, (eng.sem, 1)))
        self._mark(tok, reads, writes)
        return tok

    def group(self, eng, fns, reads=(), writes=()):
        waits = self._deps(eng, reads, writes)
        eng.cnt += 1
        tok = (eng, eng.cnt)
        n = len(fns)
        for i, fn in enumerate(fns):
            eng.ops.append((waits if i == 0 else [], fn, (eng.sem, 1) if i == n - 1 else None))
        self._mark(tok, reads, writes)
        return tok

    def dma(self, eng, dsem, fn, reads=(), writes=()):
        waits = self._deps(eng, reads, writes)
        dsem.cnt += dsem.step
        tok = (dsem, dsem.cnt)
        eng.ops.append((waits, fn, (dsem.sem, 0 if dsem.noval else dsem.step)))
        self._mark(tok, reads, writes)
        return tok


def emit(e, eng):
    for waits, fn, inc in eng.ops:
        for sem, v in waits:
            e.wait_ge(sem, v)
        ins = fn(e)
        if inc is not None:
            if inc[1] == 0:
                ins.then_inc(inc[0])
            else:
                ins.then_inc(inc[0], inc[1])


def alias_barrier(old, new):
    for nb in new:
        for ob in old:
            if ob.lw is not None:
                k, v = ob.lw
                if nb.rd.get(k, 0) < v:
                    nb.rd[k] = v
            for k, v in ob.rd.items():
                if nb.rd.get(k, 0) < v:
                    nb.rd[k] = v


def build_nc(STOP=99):
    nc = bass.Bass("TRN2", target_bir_lowering=False)

    def din(name, shape, dt=F32):
        return nc.dram_tensor(name, list(shape), dt, kind="ExternalInput").ap()

    xT = din("xT", [D, NT])
    pT = din("pT", [256, NT])
    rope = din("rope", [128, 4, 2 * TT])
    vecs = din("vecs", [128, 64])
    bcast = din("bcast", [128, 416])
    w_g1 = din("w_g1", [D, DFF]); w_u1 = din("w_u1", [D, DFF]); w_d1 = din("w_d1", [DFF, D])
    w_g2 = din("w_g2", [D, DFF]); w_u2 = din("w_u2", [D, DFF]); w_d2 = din("w_d2", [DFF, D])
    w_inx = din("w_inx", [D, 4096])
    w_out = din("w_out", [D, D])
    w_pg = din("w_pg", [D, D])
    w_pp = din("w_pp", [256, D])
    outT = nc.dram_tensor("outT", [D, NT], F32, kind="ExternalOutput").ap()

    kT_mine = [nc.dram_tensor(f"kT_mine{s}", [128, 4 * SL], BF16) for s in range(2)]
    kT_all = [nc.dram_tensor(f"kT_all{s}", [4 * 128, 4 * SL], BF16) for s in range(2)]
    v_mine = [[nc.dram_tensor(f"v_mine{s}_{hp}", [256, 1056], BF16) for hp in range(2)] for s in range(2)]
    v_all = [[nc.dram_tensor(f"v_all{s}_{hp}", [4 * 256, 1056], BF16) for hp in range(2)] for s in range(2)]
    halo_mine = nc.dram_tensor("halo_mine", [128, 16], F32)
    halo_all = nc.dram_tensor("halo_all", [512, 16], F32)

    from contextlib import ExitStack
    with ExitStack() as es:
        def sb(name, shape, dt):
            return es.enter_context(nc.sbuf_tensor(name, list(shape), dt))

        def sem(name):
            return es.enter_context(nc.semaphore(name))

        hT = sb("hT", [128, 8, NT], F32)
        xn = sb("xn", [128, 2, 8, SL], BF16)
        wsm = sb("wsm", [128, 4, 2048], BF16)
        ARENA = 27264
        arena = sb("arena", [128, ARENA], BF16)
        tmp = sb("tmp", [128, 4, TT], F32)
        rstd = sb("rstd", [128, 2, TT], F32)
        ropes = sb("ropes", [128, 2, 1032], F32)
        ubuf = sb("ubuf", [128, SL + 2], F32)
        accb = sb("accb", [128, SL], F32)
        ones_bf = sb("ones_bf", [128, 128], BF16)
        ident = sb("ident", [128, 128], BF16)
        vecs_sb = sb("vecs_sb", [128, 64], F32)
        bc_sb = sb("bc_sb", [128, 416], F32)
        gsub = sb("gsub", [128, 128], F32)
        small = sb("small", [128, 96], F32)
        junk = sb("junk", [128, 128], F32)
        halo_sb = sb("halo_sb", [128, 4, 16], F32)
        halo_st = sb("halo_st", [128, 16], F32)
        acc01 = sb("acc01", [128, 16], F32)
        cb01 = sb("cb01", [128, 16], F32)
        hfix = sb("hfix", [128, 32], F32)
        obuf = sb("obuf", [128, 2, 128], F32)
        tbuf = sb("tbuf", [128, 2, 128], F32)

        banks = [es.enter_context(nc.psum_tensor(f"bk{i}", [128, TT], F32)) for i in range(7)]
        tpb = es.enter_context(nc.psum_tensor("tpb", [128, TT], BF16))

        def av(off, n, **kw):
            v = arena[:, off:off + n]
            return v

        QT = arena[:, 0:8192].rearrange("p (h t) -> p h t", h=4)
        ycv = arena[:, 8192:16384].rearrange("p (h t) -> p h t", h=4)
        A0 = 16384
        Kb = arena[:, A0:A0 + 3072].rearrange("p (s k) -> p s k", s=3)
        Vb = arena[:, A0 + 3072:A0 + 6240].rearrange("p (s k e) -> p s k e", s=3, k=8)
        Eb = arena[:, A0 + 6240:A0 + 8288].rearrange("p (s q) -> p s q", s=4)
        kst = arena[:, A0 + 8288:A0 + 9312].rearrange("p (s q) -> p s q", s=2)
        NVST = 4
        vst = arena[:, A0:A0 + NVST * 528].rearrange("p (s h e) -> p s h e", s=NVST, h=4)
        ybf = arena[:, A0 + 10368:A0 + 10880].rearrange("p (s e) -> p s e", s=4)
        h1 = arena[:, 0:12288].rearrange("p (f t) -> p f t", f=12)
        wdn = arena[:, 12288:18432].rearrange("p (s f d) -> p s f d", s=2, f=12)
        pTb = arena[:, 20480:24576].rearrange("p (k t) -> p k t", k=2)
        yat = xn[:, 0].rearrange("p c t -> p (c t)").rearrange("p (h t) -> p h t", h=4)

        P = Prog()
        PE = Eng("pe", sem("s_pe"), is_pe=True)
        ACT = Eng("act", sem("s_act"))
        DVE = Eng("dve", sem("s_dve"))
        POOL = Eng("pool", sem("s_pool"))
        SP = Eng("sp", sem("s_sp"))

        def dsem(name, step=16):
            return DSem(sem(name), step)

        B_hT = [[Buf() for _ in range(4)] for _ in range(8)]
        B_xn = [[Buf() for _ in range(2)] for _ in range(2)]
        B_ws = [Buf() for _ in range(4)]
        B_bank = [Buf() for _ in range(7)]
        B_tpb = Buf()
        B_tmp = [Buf() for _ in range(4)]
        B_rstd = [Buf() for _ in range(2)]
        B_rope = [Buf() for _ in range(2)]
        B_h1 = [Buf() for _ in range(12)]
        B_wdn = [Buf() for _ in range(2)]
        B_QT = [[Buf() for _ in range(4)] for _ in range(4)]
        B_ycv = [[Buf() for _ in range(4)] for _ in range(4)]
        B_yat = [[Buf() for _ in range(4)] for _ in range(4)]
        B_Kb = [Buf() for _ in range(3)]
        B_Vb = [Buf() for _ in range(3)]
        B_E = [Buf() for _ in range(4)]
        B_kst = [Buf() for _ in range(2)]
        B_vst = [Buf() for _ in range(4)]
        B_ybf = Buf()
        B_pT = Buf()
        B_const = Buf()
        B_small = Buf()
        B_u = Buf()
        B_acc = Buf()
        B_halo = Buf()
        B_halo_sb = Buf()
        B_hfix = Buf()
        B_obuf = [Buf(), Buf()]
        B_tbuf = [Buf(), Buf()]
        B_kTm = [[Buf() for _ in range(4)] for _ in range(4)]
        B_vm = [Buf() for _ in range(16)]
        B_halom = Buf()
        B_kTall = [Buf(), Buf()]
        B_vall = [Buf() for _ in range(4)]
        B_haloall = Buf()
        B_out = Buf()

        S_ws = [dsem(f"d_ws{i}") for i in range(4)]
        S_wdn = [dsem(f"d_wd{i}") for i in range(2)]
        S_misc = dsem("d_misc")
        S_x = dsem("d_x")
        S_rope = [dsem(f"d_rp{i}") for i in range(2)]
        S_kst = [dsem(f"d_ks{i}") for i in range(2)]
        S_vst = [dsem(f"d_vs{i}") for i in range(4)]
        S_Kb = [dsem(f"d_kb{i}") for i in range(3)]
        S_Vb = [dsem(f"d_vb{i}") for i in range(3)]
        S_cc = [dsem(f"d_cc{i}", 1) for i in range(3)]
        S_cck = [dsem(f"d_cck{i}", 1) for i in range(2)]
        S_ccv = [dsem(f"d_ccv{i}", 1) for i in range(4)]
        S_halo = dsem("d_halo")
        S_pT = dsem("d_pT")
        S_out = [dsem(f"d_out{i}") for i in range(4)]

        rr = {"bank": 0, "tmp": 0, "ws": 0, "wd": 0, "E": 0, "kv": 0, "rope": 0, "kst": 0, "vst": 0, "ob": 0, "out": 0}

        def nxt(key, n):
            i = rr[key]
            rr[key] = (i + 1) % n
            return i

        P.dma(SP, S_misc, lambda e: e.dma_start(out=vecs_sb[:, :], in_=vecs), writes=[B_const])
        P.dma(SP, S_misc, lambda e: e.dma_start(out=bc_sb[:, :], in_=bcast), writes=[B_const])
        xTv = xT.rearrange("(c p) t -> p c t", p=128)
        S_x2 = dsem("d_x2")
        for slot_, sx in ((0, S_x), (1, S_x2)):
            for c in range(8):
                xtok = P.dma(SP, sx, lambda e, c=c, slot_=slot_: e.dma_start(
                    out=hT[:, c, slot_ * SL:(slot_ + 1) * SL], in_=xTv[:, c, slot_ * SL:(slot_ + 1) * SL]),
                    writes=[B_hT[c][2 * slot_], B_hT[c][2 * slot_ + 1]])
            for c in range(8):
                for t in range(2):
                    B_hT[c][2 * slot_ + t].lw = xtok
        P.op(POOL, lambda e: e.memset(ones_bf[:, :], 1.0), writes=[B_const])
        P.op(POOL, lambda e: e.affine_select(out=ident[:, :], in_=ones_bf[:, :], pattern=[[1, 128]],
                                              compare_op=ALU.is_equal, fill=0.0, base=0,
                                              channel_multiplier=-1), reads=[B_const], writes=[B_const])
        P.op(POOL, lambda e: e.memset(ubuf[:, 0:2], 0.0), writes=[B_u])
        P.op(DVE, lambda e: e.tensor_scalar(out=gsub[:, :], in0=bc_sb[:, 0:128], scalar1=0.8, scalar2=None,
                                            op0=ALU.mult), reads=[B_const], writes=[B_const])
        P.op(DVE, lambda e: e.scalar_tensor_tensor(out=junk[:, 0:64], in0=bc_sb[:, 128:192], scalar=1.0,
                                                   in1=bc_sb[:, 192:256], op0=ALU.mult, op1=ALU.mult,
                                                   accum_out=small[:, 0:1]), reads=[B_const], writes=[B_small])
        P.op(DVE, lambda e: e.scalar_tensor_tensor(out=junk[:, 64:128], in0=bc_sb[:, 256:320], scalar=1.0,
                                                   in1=bc_sb[:, 320:384], op0=ALU.mult, op1=ALU.mult,
                                                   accum_out=small[:, 1:2]), reads=[B_const, B_small], writes=[B_small])
        P.op(ACT, lambda e: e.activation(out=small[:, 2:4], in_=small[:, 0:2], func=AF.Exp),
             reads=[B_small], writes=[B_small])
        P.op(DVE, lambda e: e.tensor_tensor(out=small[:, 4:5], in0=small[:, 2:3], in1=small[:, 3:4],
                                            op=ALU.subtract), reads=[B_small], writes=[B_small])
        P.op(DVE, lambda e: e.tensor_scalar(out=small[:, 5:6], in0=small[:, 4:5], scalar1=0.2, scalar2=None,
                                            op0=ALU.add), reads=[B_small], writes=[B_small])
        LAM = small[:, 5:6]
        ZERO = small[:, 6:7]
        P.op(DVE, lambda e: e.memset(small[:, 6:8], 0.0), reads=[B_small], writes=[B_small])
        P.op(DVE, lambda e: e.tensor_copy(out=small[:, 7:8], in_=small[:, 6:7]),
             reads=[B_small], writes=[B_const])

        def bank_alloc():
            i = nxt("bank", 7)
            return i, banks[i], B_bank[i]

        def tmp_alloc():
            i = nxt("tmp", 4)
            return tmp[:, i, :], B_tmp[i]

        ws_free = [0, 1, 2, 3]
        wd_free = [0, 1]
        ws_slot_of = {}

        def ws_release(B):
            ws_free.append(ws_slot_of.pop(id(B)))

        def load_wsm(w, col0, kchunks=8, ncols=256):
            i = ws_free.pop(0)
            ws_slot_of[id(B_ws[i])] = i
            src = w.rearrange("(c p) f -> p c f", p=128)[:, :, col0:col0 + ncols]
            dst = wsm[:, i, 0:kchunks * ncols].rearrange("p (c f) -> p c f", c=kchunks)
            P.dma(POOL, S_ws[i], lambda e: e.dma_start(out=dst, in_=src), writes=[B_ws[i]])
            return dst, B_ws[i]

        def wd_release(B):
            wd_free.append(B_wdn.index(B))

        def load_wdn(w, row0, nf, col0):
            i = wd_free.pop(0)
            src = w[row0 * 128:(row0 + nf) * 128, :].rearrange("(c p) d -> p c d", p=128)[:, :, col0:col0 + 256]
            dst = wdn[:, i, 0:nf, :]
            P.dma(POOL, S_wdn[i], lambda e: e.dma_start(out=dst, in_=src), writes=[B_wdn[i]])
            return dst, B_wdn[i]

        def norm_sq(slot, gcol):
            for c in range(8):
                P.op(ACT, lambda e, c=c: e.activation(out=xn[:, slot, c, :], in_=hT[:, c, slot * SL:(slot + 1) * SL],
                                                      func=AF.Square),
                     reads=[B_hT[c][2 * slot], B_hT[c][2 * slot + 1]], writes=[B_xn[slot][0], B_xn[slot][1]])

        def norm_stat(slot):
            for ttl in range(2):
                bi, bk, bb = bank_alloc()
                fns = []
                for c in range(8):
                    fns.append(lambda e, c=c, bk=bk, ttl=ttl: e.matmul(bk[:, :], lhsT=ones_bf[:, :],
                                                                       rhs=xn[:, slot, c, ttl * TT:(ttl + 1) * TT],
                                                                       start=(c == 0), stop=(c == 7)))
                P.group(PE, fns, reads=[B_const, B_xn[slot][ttl]], writes=[bb])
                r = rstd[:, ttl, :]
                P.op(DVE, lambda e, bk=bk, r=r: e.tensor_scalar(out=r, in0=bk[:, :], scalar1=1.0 / D, scalar2=EPS,
                                                                op0=ALU.mult, op1=ALU.add),
                     reads=[bb], writes=[B_rstd[ttl]])
                P.op(ACT, lambda e, r=r: e.activation(out=r, in_=r, func=AF.Sqrt),
                     reads=[B_rstd[ttl]], writes=[B_rstd[ttl]])
                P.op(DVE, lambda e, r=r: e.reciprocal(out=r, in_=r), reads=[B_rstd[ttl]], writes=[B_rstd[ttl]])

        def norm_apply(slot, gcol):
            for ttl in range(2):
                r = rstd[:, ttl, :]
                for c in range(8):
                    tok0 = slot * SL + ttl * TT
                    P.op(DVE, lambda e, c=c, r=r, tok0=tok0, ttl=ttl: e.scalar_tensor_tensor(
                        out=xn[:, slot, c, ttl * TT:(ttl + 1) * TT], in0=hT[:, c, tok0:tok0 + TT],
                        scalar=vecs_sb[:, gcol + c:gcol + c + 1], in1=r, op0=ALU.mult, op1=ALU.mult),
                        reads=[B_hT[c][2 * slot + ttl], B_rstd[ttl], B_const], writes=[B_xn[slot][ttl]])

        def norm(slot, gcol):
            norm_sq(slot, gcol)
            norm_stat(slot)
            norm_apply(slot, gcol)

        def ffn(slot, wg, wu, wd, hook=None):
            halves = [(0, 12), (12, 10)]
            for hi, (f0, nf) in enumerate(halves):
                nblk = nf // 2
                pend = []

                def issue(b):
                    g = load_wsm(wg, (f0 + 2 * b) * 128)
                    u = load_wsm(wu, (f0 + 2 * b) * 128)
                    pend.append((g, u))

                issue(0)
                for b in range(nblk):
                    if b + 1 < nblk:
                        issue(b + 1)
                    (gw, gB), (uw, uB) = pend.pop(0)
                    for fl in range(2):
                        fcl = 2 * b + fl
                        res = {}
                        for name, w_, wB in (("g", gw, gB), ("u", uw, uB)):
                            for ttl in range(2):
                                bi, bk, bb = bank_alloc()
                                fns = []
                                for c in range(8):
                                    fns.append(lambda e, c=c, bk=bk, w_=w_, ttl=ttl, fl=fl: e.matmul(
                                        bk[:, :], lhsT=w_[:, c, fl * 128:(fl + 1) * 128],
                                        rhs=xn[:, slot, c, ttl * TT:(ttl + 1) * TT], start=(c == 0), stop=(c == 7)))
                                P.group(PE, fns, reads=[wB, B_xn[slot][ttl]], writes=[bb])
                                res[(name, ttl)] = (bk, bb)
                        for ttl in range(2):
                            gk, gb = res[("g", ttl)]
                            uk, ub = res[("u", ttl)]
                            t, tB = tmp_alloc()
                            P.op(ACT, lambda e, gk=gk, t=t: e.activation(out=t, in_=gk[:, :], func=AF.Silu),
                                 reads=[gb], writes=[tB])
                            P.op(DVE, lambda e, t=t, uk=uk, fcl=fcl, ttl=ttl: e.tensor_tensor(
                                out=h1[:, fcl, ttl * TT:(ttl + 1) * TT], in0=t, in1=uk[:, :], op=ALU.mult),
                                reads=[tB, ub], writes=[B_h1[fcl]])
                    ws_release(gB)
                    ws_release(uB)
                    if hook is not None and hi == 0 and b == 1:
                        norm_sq(*hook)
                    if hook is not None and hi == 0 and b == 3:
                        norm_stat(hook[0])
                        norm_apply(*hook)
                dpend = []

                def dissue(db):
                    dpend.append(load_wdn(wd, f0, nf, db * 256))

                dissue(0)
                for db in range(4):
                    if db + 1 < 4:
                        dissue(db + 1)
                    dw, dB = dpend.pop(0)
                    for dl in range(2):
                        dc = 2 * db + dl
                        for ttl in range(2):
                            bi, bk, bb = bank_alloc()
                            fns = []
                            for f in range(nf):
                                fns.append(lambda e, f=f, bk=bk, dw=dw, dl=dl, ttl=ttl, nf=nf: e.matmul(
                                    bk[:, :], lhsT=dw[:, f, dl * 128:(dl + 1) * 128],
                                    rhs=h1[:, f, ttl * TT:(ttl + 1) * TT], start=(f == 0), stop=(f == nf - 1)))
                            P.group(PE, fns, reads=[dB] + B_h1[:nf], writes=[bb])
                            tok0 = slot * SL + ttl * TT
                            hb = B_hT[dc][2 * slot + ttl]
                            P.op(DVE, lambda e, bk=bk, dc=dc, tok0=tok0: e.scalar_tensor_tensor(
                                out=hT[:, dc, tok0:tok0 + TT], in0=bk[:, :], scalar=0.5, in1=hT[:, dc, tok0:tok0 + TT],
                                op0=ALU.mult, op1=ALU.add), reads=[bb, hb], writes=[hb])
                    wd_release(dB)

        class _Stop(Exception):
            pass

        def body():
            if STOP < 1:
                raise _Stop()
            B_ffn_arena = B_h1 + B_wdn
            norm(0, 0)
            norm(1, 0)
            ffn(0, w_g1, w_u1, w_d1)
            ffn(1, w_g1, w_u1, w_d1, hook=(0, 8))
            norm(1, 8)

            if STOP < 2:
                raise _Stop()
            attn_bufs = ([b for r in B_QT for b in r] + [b for r in B_ycv for b in r] + B_Kb + B_Vb + B_E + B_kst
                         + B_vst + [B_ybf])
            alias_barrier(B_ffn_arena, attn_bufs)
            for s in range(4):
                P.op(POOL, lambda e, s=s: e.memset(vst[:, s, :, 128:132], 1.0), writes=[B_vst[s]])

            def xn_rhs(c, tt):
                return xn[:, tt // 2, c, (tt % 2) * TT:(tt % 2 + 1) * TT]

            def proj_mm(w_, wB, fl, tt):
                bi, bk, bb = bank_alloc()
                fns = []
                for c in range(8):
                    fns.append(lambda e, c=c, bk=bk: e.matmul(bk[:, :], lhsT=w_[:, c, fl * 128:(fl + 1) * 128],
                                                              rhs=xn_rhs(c, tt), start=(c == 0), stop=(c == 7)))
                P.group(PE, fns, reads=[wB, B_xn[tt // 2][tt % 2]], writes=[bb])
                return bk, bb

            wq = []
            order = [10, 11, 12, 13, 14, 15, 6, 7, 8, 9, 0, 2, 3, 1, 4, 5]
            blocks = {}
            oi = {"i": 0}

            def prefetch_in(n=1):
                for _ in range(n):
                    if oi["i"] < len(order):
                        b = order[oi["i"]]
                        oi["i"] += 1
                        blocks[b] = load_wsm(w_inx, b * 256)

            prefetch_in(4)

            def load_rope(tt):
                i = nxt("rope", 2)
                P.dma(SP, S_rope[i], lambda e: e.dma_start(out=ropes[:, i, 0:2 * TT], in_=rope[:, tt, :]), writes=[B_rope[i]])
                return ropes[:, i, 0:2 * TT], B_rope[i]

            def rope_proj(kind, pre=None):
                base = 6 if kind == "q" else 10
                loaded = dict(pre or {})

                def get_rope(i):
                    if i not in loaded:
                        loaded[i] = load_rope(i % 4)
                    return loaded[i]

                for g in range(2):
                  hblocks = {h: blocks.pop(base + h) for h in (2 * g, 2 * g + 1)}
                  for tt in range(4):
                    i_ = g * 4 + tt
                    rp, rB = get_rope(i_)
                    if i_ + 1 < 8:
                        get_rope(i_ + 1)
                    for h in (2 * g, 2 * g + 1):
                        w_, wB = hblocks[h]
                        qk, qb = proj_mm(w_, wB, 0, tt)
                        rk, rb = proj_mm(w_, wB, 1, tt)
                        t1, t1B = tmp_alloc()
                        t2, t2B = tmp_alloc()
                        P.op(DVE, lambda e, qk=qk, t1=t1, rp=rp: e.tensor_tensor(out=t1, in0=qk[:, :], in1=rp[:, 0:TT],
                                                                                 op=ALU.mult),
                             reads=[qb, rB], writes=[t1B])
                        P.op(DVE, lambda e, rk=rk, t2=t2, rp=rp: e.tensor_tensor(out=t2, in0=rk[:, :], in1=rp[:, TT:2 * TT],
                                                                                 op=ALU.mult),
                             reads=[rb, rB], writes=[t2B])
                        if kind == "q":
                            P.op(DVE, lambda e, t1=t1, t2=t2, h=h, tt=tt: e.tensor_tensor(
                                out=QT[:, h, tt * TT:(tt + 1) * TT], in0=t1, in1=t2, op=ALU.add),
                                reads=[t1B, t2B], writes=[B_QT[h][tt]])
                        else:
                            si = nxt("kst", 2)
                            P.op(DVE, lambda e, t1=t1, t2=t2, si=si: e.tensor_tensor(out=kst[:, si, :], in0=t1, in1=t2,
                                                                                     op=ALU.add),
                                 reads=[t1B, t2B], writes=[B_kst[si]])
                            dst = kT_mine[tt // 2].ap()[:, h * SL + (tt % 2) * TT:h * SL + (tt % 2 + 1) * TT]
                            P.dma(SP, S_kst[si], lambda e, si=si, dst=dst: e.dma_start(out=dst, in_=kst[:, si, :]),
                                  reads=[B_kst[si]], writes=[B_kTm[h][tt]])
                  for h in (2 * g, 2 * g + 1):
                    ws_release(hblocks[h][1])
                  prefetch_in(2)


            rope_proj("k")
            q_pre = {0: load_rope(0), 1: load_rope(1)}
            wv0, wv0B = blocks.pop(14)
            wv1, wv1B = blocks.pop(15)
            v_view = [[v_mine[s_][hp].ap().rearrange("(h p) (k e) -> k p h e", h=2, p=128, k=8) for hp in range(2)]
                  for s_ in range(2)]
            def v_gather(s_):
                for hp in range(2):
                    P.dma(POOL, S_ccv[2 * s_ + hp], lambda e, s_=s_, hp=hp: e.collective_compute(
                        "AllGather", ALU.bypass, replica_groups=GROUPS, dma_qos="P2", ins=[v_mine[s_][hp].ap().opt()],
                        outs=[v_all[s_][hp].ap().opt()]),
                        reads=B_vm[8 * s_:8 * s_ + 8], writes=[B_vall[2 * s_ + hp]])

            for ts in range(16):
                s_, kt = ts // 8, ts % 8
                si = nxt("vst", 4)
                for half, (wv, wvB) in enumerate(((wv0, wv0B), (wv1, wv1B))):
                    bi, bk, bb = bank_alloc()
                    fns = []
                    for c in range(8):
                        fns.append(lambda e, c=c, bk=bk, wv=wv, s_=s_, kt=kt: e.matmul(
                            bk[:, 0:256], lhsT=xn[:, s_, c, kt * 128:(kt + 1) * 128], rhs=wv[:, c, :],
                            start=(c == 0), stop=(c == 7)))
                    P.group(PE, fns, reads=[wvB, B_xn[s_][kt // 4]], writes=[bb])
                    P.op(ACT, lambda e, bk=bk, si=si, half=half: e.activation(
                        out=vst[:, si, 2 * half:2 * half + 2, 0:128],
                        in_=bk[:, 0:256].rearrange("p (h e) -> p h e", h=2), func=AF.Copy),
                        reads=[bb], writes=[B_vst[si]])
                for hp in range(2):
                    P.dma(SP, S_vst[si], lambda e, si=si, s_=s_, kt=kt, hp=hp: e.dma_start(
                        out=v_view[s_][hp][kt], in_=vst[:, si, 2 * hp:2 * hp + 2, :]),
                        reads=[B_vst[si]], writes=[B_vm[ts]] if hp == 1 else [])
            ws_release(wv0B)
            ws_release(wv1B)
            prefetch_in(2)

            if STOP < 3:
                raise _Stop()
            for s_ in range(2):
                P.dma(POOL, S_cck[s_], lambda e, s_=s_: e.collective_compute(
                    "AllGather", ALU.bypass, replica_groups=GROUPS, dma_qos="P2", ins=[kT_mine[s_].ap().opt()],
                    outs=[kT_all[s_].ap().opt()]),
                    reads=[B_kTm[h][2 * s_ + t] for h in range(4) for t in range(2)], writes=[B_kTall[s_]])
                v_gather(s_)

            rope_proj("q", pre=q_pre)

            for cc in range(4):
                if cc == 0:
                    wb_, wbB = blocks.pop(0)
                elif cc == 2:
                    wb_, wbB = blocks.pop(1)
                wcx, wcxB = blocks.pop(2 + cc)
                for slot in range(2):
                    for ttl in range(2):
                        tt = 2 * slot + ttl
                        ck, cb_ = proj_mm(wcx, wcxB, 0, tt)
                        xk, xb_ = proj_mm(wcx, wcxB, 1, tt)
                        t, tB = tmp_alloc()
                        P.op(ACT, lambda e, ck=ck, t=t: e.activation(out=t, in_=ck[:, :], func=AF.Copy),
                             reads=[cb_], writes=[tB])
                        P.op(DVE, lambda e, t=t, xk=xk, ttl=ttl: e.tensor_tensor(
                            out=ubuf[:, 2 + ttl * TT:2 + (ttl + 1) * TT], in0=t, in1=xk[:, :], op=ALU.mult),
                            reads=[tB, xb_, B_u], writes=[B_u])
                    w0 = vecs_sb[:, 40 + 0 + cc:40 + 0 + cc + 1]
                    w1 = vecs_sb[:, 40 + 4 + cc:40 + 4 + cc + 1]
                    w2 = vecs_sb[:, 40 + 8 + cc:40 + 8 + cc + 1]
                    P.op(DVE, lambda e, w2=w2: e.tensor_scalar(out=accb[:, :], in0=ubuf[:, 2:SL + 2], scalar1=w2,
                                                               scalar2=None, op0=ALU.mult),
                         reads=[B_u, B_const, B_acc], writes=[B_acc])
                    P.op(DVE, lambda e, w1=w1: e.scalar_tensor_tensor(out=accb[:, :], in0=ubuf[:, 1:SL + 1], scalar=w1,
                                                                      in1=accb[:, :], op0=ALU.mult, op1=ALU.add),
                         reads=[B_u, B_const, B_acc], writes=[B_acc])
                    P.op(DVE, lambda e, w0=w0: e.scalar_tensor_tensor(out=accb[:, :], in0=ubuf[:, 0:SL], scalar=w0,
                                                                      in1=accb[:, :], op0=ALU.mult, op1=ALU.add),
                         reads=[B_u, B_const, B_acc], writes=[B_acc])
                    hcol = (cc * 2 + slot) * 2
                    P.op(DVE, lambda e, hcol=hcol: e.tensor_copy(out=halo_st[:, hcol:hcol + 2], in_=ubuf[:, SL:SL + 2]),
                         reads=[B_u, B_halo], writes=[B_halo])
                    P.op(DVE, lambda e, hcol=hcol: e.tensor_copy(out=acc01[:, hcol:hcol + 2], in_=accb[:, 0:2]),
                         reads=[B_acc, B_halo], writes=[B_halo])
                    for ttl in range(2):
                        tt = 2 * slot + ttl
                        bk_, bb_ = proj_mm(wb_, wbB, cc % 2, tt)
                        if ttl == 0:
                            P.op(DVE, lambda e, bk_=bk_, hcol=hcol: e.tensor_copy(out=cb01[:, hcol:hcol + 2],
                                                                                   in_=bk_[:, 0:2]),
                                 reads=[bb_, B_halo], writes=[B_halo])
                        P.op(DVE, lambda e, bk_=bk_, ttl=ttl, tt=tt, cc=cc: e.tensor_tensor(
                            out=ycv[:, cc, tt * TT:(tt + 1) * TT], in0=accb[:, ttl * TT:(ttl + 1) * TT], in1=bk_[:, :],
                            op=ALU.mult), reads=[bb_, B_acc], writes=[B_ycv[cc][tt]])
                ws_release(wcxB)
                if cc % 2 == 1:
                    ws_release(wbB)
                prefetch_in(1 if cc == 0 else 2)
            P.dma(SP, S_halo, lambda e: e.dma_start(out=halo_mine.ap(), in_=halo_st[:, :]), reads=[B_halo],
                  writes=[B_halom])
            P.dma(POOL, S_cc[2], lambda e: e.collective_compute("AllGather", ALU.bypass, replica_groups=GROUPS, dma_qos="P2",
                                                                ins=[halo_mine.ap().opt()], outs=[halo_all.ap().opt()]),
                  reads=[B_halom], writes=[B_haloall])

            if STOP < 4:
                raise _Stop()
            alias_barrier(B_xn[0] + B_xn[1], [b for r in B_yat for b in r])
            alias_barrier(B_vst, B_Kb + B_Vb)
            cmask = arena[:, A0 + 8288:A0 + 8288 + 2048].rearrange("p (m q) -> p m q", m=4)
            B_cmask = Buf()
            attn_bufs.append(B_cmask)
            alias_barrier(B_kst, [B_cmask])
            P.op(POOL, lambda e: e.memset(cmask[:, :, :], 0.0), writes=[B_cmask])
            for mm_ in range(4):
                P.op(POOL, lambda e, mm_=mm_: e.affine_select(
                    out=cmask[:, mm_, :], in_=cmask[:, mm_, :], pattern=[[1, TT]], compare_op=ALU.is_ge,
                    fill=NEG, base=-128 * mm_, channel_multiplier=-1), reads=[B_cmask], writes=[B_cmask])
            SEL = 384
            O_BANKS = [0, 1, 2]
            ST_BANKS = [3, 4, 5, 6]
            st_rr = {"i": 0}
            obanks_B = [B_bank[b] for b in O_BANKS]

            def oacc(a):
                b = a // 3
                off = (a % 3) * 129
                return banks[O_BANKS[b]][:, off:off + 129]

            oab = ropes
            B_oab = [Buf(), Buf()]
            alias_barrier(B_rope, B_oab)
            qsel = wsm[:, 0:3, :].rearrange("p s f -> p (s f)").rearrange("p (b v q) -> p b v q", b=2, v=3)
            B_qsel = [Buf(), Buf()]
            alias_barrier(B_ws, B_qsel)
            kaux = xn[:, 1, 0:2, :].rearrange("p c t -> p (c t)").rearrange("p (s k) -> p s k", s=2)
            vaux = xn[:, 1, 2:5, :].rearrange("p c t -> p (c t)")[:, 0:2112].rearrange("p (s k) -> p s k", s=2)
            B_kaux = [Buf(), Buf()]
            B_vaux = [Buf(), Buf()]
            alias_barrier(B_xn[1], B_kaux + B_vaux)
            S_kaux = [dsem(f"d_kx{i}") for i in range(2)]
            S_vaux = [dsem(f"d_vx{i}") for i in range(2)]

            def sAcol(v):
                return bc_sb[:, SEL + v:SEL + v + 1]

            def sBcol(v):
                return bc_sb[:, SEL + 3 + v:SEL + 3 + v + 1]

            work = []
            for h in range(4):
                for qt in range(2):
                    units = []
                    units.append(("B", [("own", None, 1)] + [("g", r, 0) for r in range(4)]))
                    units.append(("A", [("own", None, 0)]))
                    for v in range(3):
                        units.append((("V", v), [("v", v, None)]))
                    for ui, (ukind, chunks) in enumerate(units):
                        for ci, (kind, r, ks) in enumerate(chunks):
                            if kind == "own":
                                kts = list(range(0, 4 * qt + 4))
                            else:
                                kts = list(range(8))
                            work.append(dict(h=h, qt=qt, ukind=ukind, kind=kind, r=r, ks=ks, kts=kts,
                                             first=(ci == 0), lastc=(ci == len(chunks) - 1),
                                             lastu=(ui == len(units) - 1)))
            nv = {"n": 0}

            def kv_issue(w, i):
                h, ks, r = w["h"], w["ks"], w["r"]
                if w["kind"] == "own":
                    src_k = kT_mine[ks].ap()[:, h * SL:(h + 1) * SL]
                    src_v = v_mine[ks][h // 2].ap()[(h % 2) * 128:(h % 2 + 1) * 128, :]
                    kdeps = [B_kTm[h][2 * ks], B_kTm[h][2 * ks + 1]]
                    vdeps = B_vm[8 * ks:8 * ks + 8]
                else:
                    if w["kind"] == "v":
                        r, ks = w["r"], 0
                    src_k = kT_all[ks].ap()[r * 128:(r + 1) * 128, h * SL:(h + 1) * SL]
                    src_v = v_all[ks][h // 2].ap()[r * 256 + (h % 2) * 128:r * 256 + (h % 2 + 1) * 128, :]
                    kdeps = [B_kTall[ks]]
                    vdeps = [B_vall[2 * ks + h // 2]]
                P.dma(SP, S_Kb[i], lambda e: e.dma_start(out=Kb[:, i, :], in_=src_k), reads=kdeps, writes=[B_Kb[i]])
                P.dma(SP, S_Vb[i], lambda e: e.dma_start(out=Vb[:, i, :, :].rearrange("p k e -> p (k e)"), in_=src_v),
                      reads=vdeps, writes=[B_Vb[i]])
                if w["kind"] == "v":
                    ax = nv["n"] % 2
                    nv["n"] += 1
                    w["ax"] = ax
                    r2 = w["r"] + 1
                    src_k2 = kT_all[1].ap()[r2 * 128:(r2 + 1) * 128, h * SL:(h + 1) * SL]
                    src_v2 = v_all[1][h // 2].ap()[r2 * 256 + (h % 2) * 128:r2 * 256 + (h % 2 + 1) * 128, :]
                    P.dma(SP, S_kaux[ax], lambda e: e.dma_start(out=kaux[:, ax, :], in_=src_k2),
                          reads=[B_kTall[1]], writes=[B_kaux[ax]])
                    P.dma(SP, S_vaux[ax], lambda e: e.dma_start(out=vaux[:, ax, :], in_=src_v2),
                          reads=[B_vall[2 + h // 2]], writes=[B_vaux[ax]])

            issued = {"n": 0}

            def ensure_loaded(upto):
                while issued["n"] < len(work) and issued["n"] <= upto:
                    kv_issue(work[issued["n"]], issued["n"] % 3)
                    issued["n"] += 1

            tiles = []
            for wi, w in enumerate(work):
                for ki, kt in enumerate(w["kts"]):
                    tiles.append((wi, ki, kt))

            def make_qsel(h):
                hb = h % 2
                for v in range(3):
                    P.op(DVE, lambda e, hb=hb, v=v, h=h: e.tensor_scalar(
                        out=qsel[:, hb, v, :], in0=QT[:, h, 0:SL], scalar1=sAcol(v), scalar2=None, op0=ALU.mult),
                        reads=[B_QT[h][0], B_QT[h][1], B_const], writes=[B_qsel[hb]])
                    P.op(DVE, lambda e, hb=hb, v=v, h=h: e.scalar_tensor_tensor(
                        out=qsel[:, hb, v, :], in0=QT[:, h, SL:2 * SL], scalar=sBcol(v), in1=qsel[:, hb, v, :],
                        op0=ALU.mult, op1=ALU.add),
                        reads=[B_QT[h][2], B_QT[h][3], B_const, B_qsel[hb]], writes=[B_qsel[hb]])

            def blend_kv(w, bi):
                v, ax = w["r"], w["ax"]
                P.op(DVE, lambda e, bi=bi, v=v: e.tensor_scalar(
                    out=Kb[:, bi, :], in0=Kb[:, bi, :], scalar1=sAcol(v), scalar2=None, op0=ALU.mult),
                    reads=[B_Kb[bi], B_const], writes=[B_Kb[bi]])
                P.op(DVE, lambda e, bi=bi, v=v, ax=ax: e.scalar_tensor_tensor(
                    out=Kb[:, bi, :], in0=kaux[:, ax, :], scalar=sBcol(v), in1=Kb[:, bi, :], op0=ALU.mult,
                    op1=ALU.add), reads=[B_Kb[bi], B_kaux[ax], B_const], writes=[B_Kb[bi]])
                vflat = Vb[:, bi, :, :].rearrange("p k e -> p (k e)")
                P.op(DVE, lambda e, vflat=vflat, v=v: e.tensor_scalar(
                    out=vflat, in0=vflat, scalar1=sAcol(v), scalar2=None, op0=ALU.mult),
                    reads=[B_Vb[bi], B_const], writes=[B_Vb[bi]])
                P.op(DVE, lambda e, vflat=vflat, v=v, ax=ax: e.scalar_tensor_tensor(
                    out=vflat, in0=vaux[:, ax, :], scalar=sBcol(v), in1=vflat, op0=ALU.mult, op1=ALU.add),
                    reads=[B_Vb[bi], B_vaux[ax], B_const], writes=[B_Vb[bi]])

            def emit_qk(wi, ki, kt):
                w = work[wi]
                bi = wi % 3
                h, qt = w["h"], w["qt"]
                if ki == 0:
                    if w["ukind"] == "B" and qt == 0 and w["first"]:
                        make_qsel(h)
                if w["kind"] == "v":
                    v = w["r"]
                    q_ap = lambda m, h=h, qt=qt, v=v: qsel[m * 64:(m + 1) * 64, h % 2, v, qt * TT:(qt + 1) * TT]
                    q_B = [B_qsel[h % 2]]
                else:
                    slot = 0 if w["ukind"] == "A" else 1
                    tt = 2 * slot + qt
                    q_ap = lambda m, h=h, tt=tt: QT[m * 64:(m + 1) * 64, h, tt * TT:(tt + 1) * TT]
                    q_B = [B_QT[h][tt]]
                crossing = (w["kind"] == "own") and (kt >= 4 * qt)
                es_ = []
                for m in range(2):
                    sb_i = ST_BANKS[m * 2 + (st_rr["i"] % 2)]
                    stb = banks[sb_i]
                    qlo = 128 * (kt - 4 * qt) if crossing else 0
                    fns = [lambda e, stb=stb, m=m, bi=bi, kt=kt, q_ap=q_ap, crossing=crossing, qlo=qlo: e.matmul(
                        stb[:, qlo:TT], lhsT=Kb[m * 64:(m + 1) * 64, bi, kt * 128:(kt + 1) * 128],
                        rhs=q_ap(m)[:, qlo:TT], start=True, stop=not crossing)]
                    rds = [B_Kb[bi]] + q_B
                    if crossing:
                        mm_ = kt - 4 * qt
                        fns.append(lambda e, stb=stb, mm_=mm_, qlo=qlo: e.matmul(
                            stb[:, qlo:TT], lhsT=ident[:, :], rhs=cmask[:, mm_, qlo:TT], start=False, stop=True))
                        rds = rds + [B_cmask, B_const]
                    P.group(PE, fns, reads=rds, writes=[B_bank[sb_i]])
                    ei = nxt("E", 4)
                    P.op(ACT, lambda e, stb=stb, ei=ei, qlo=qlo: e.activation(
                        out=Eb[:, ei, qlo:TT], in_=stb[:, qlo:TT], func=AF.Exp, scale=0.125),
                        reads=[B_bank[sb_i]], writes=[B_E[ei]])
                    es_.append(ei)
                st_rr["i"] += 1
                return tuple(es_)

            def emit_av(wi, ki, kt, es_):
                w = work[wi]
                bi = wi % 3
                first = w["first"] and ki == 0
                last = w["lastc"] and ki == len(w["kts"]) - 1
                crossing = (w["kind"] == "own") and (kt >= 4 * w["qt"])
                qs_lo = (kt - 4 * w["qt"]) if crossing else 0
                for b3 in range(3):
                    fns = []
                    for a in range(3 * b3, min(3 * b3 + 3, 8)):
                        qs, m = a // 2, a % 2
                        if qs < qs_lo:
                            continue
                        st_flag = first and (a % 3 == 0)
                        fns.append(lambda e, a=a, m=m, qs=qs, es_=es_, bi=bi, kt=kt, st_flag=st_flag,
                                   last=last: e.matmul(
                            oacc(a), lhsT=Eb[:, es_[m], qs * 128:(qs + 1) * 128], rhs=Vb[:, bi, kt, 0:129],
                            start=st_flag, stop=last, skip_group_check=True))
                    if fns:
                        P.group(PE, fns, reads=[B_E[es_[0]], B_E[es_[1]], B_Vb[bi]], writes=[obanks_B[b3]])
                if ki == len(w["kts"]) - 1:
                    ensure_loaded(wi + 3)
                    if wi + 2 < len(work) and work[wi + 2]["kind"] == "v":
                        blend_kv(work[wi + 2], (wi + 2) % 3)
                if last:
                    unit_done(w)

            def unit_done(w):
                uk = w["ukind"]
                for b3 in range(3):
                    ncol = 387 if b3 < 2 else 258
                    src = banks[O_BANKS[b3]][:, 0:ncol]
                    if uk in ("A", "B"):
                        sl = 0 if uk == "A" else 1
                        P.op(DVE, lambda e, src=src, sl=sl, b3=b3, ncol=ncol: e.tensor_copy(
                            out=oab[:, sl, 387 * b3:387 * b3 + ncol], in_=src),
                            reads=[obanks_B[b3]], writes=[B_oab[sl]])
                    else:
                        v = uk[1]
                        for sl, scol in ((0, sAcol(v)), (1, sBcol(v))):
                            dst = oab[:, sl, 387 * b3:387 * b3 + ncol]
                            P.op(DVE, lambda e, src=src, dst=dst, scol=scol: e.scalar_tensor_tensor(
                                out=dst, in0=src, scalar=scol, in1=dst, op0=ALU.mult, op1=ALU.add),
                                reads=[obanks_B[b3], B_oab[sl], B_const], writes=[B_oab[sl]])
                if w["lastu"]:
                    post(w["h"], w["qt"], 0)
                    post(w["h"], w["qt"], 1)

            cur = {"ti": 0}
            otile = tmp[:, :, :].rearrange("p a (b e) -> p (a b) e", e=128)
            B_otile = [Buf() for _ in range(4)]
            alias_barrier(B_tmp, B_otile)
            par_ctr = {"n": 0}

            def post(h, qt, slot):
                tt = 2 * slot + qt
                q0 = tt * TT
                oB = B_oab[slot]
                par = (par_ctr["n"] // 2) % 2
                par_ctr["n"] += 1
                ps = par * 2 + slot
                oTB = B_otile[ps]
                c_ss = 40 + ps * 4
                c_rs = 56 + ps * 4

                def osb(a):
                    return oab[:, slot, a * 129:(a + 1) * 129]

                P.op(DVE, lambda e: e.tensor_copy(
                    out=small[:, 8:16], in_=oab[:, slot, 0:1032].rearrange("p (a e) -> p a e", e=129)[:, :, 128]),
                    reads=[oB, B_small], writes=[B_small])
                P.op(DVE, lambda e: e.reciprocal(out=small[:, 16:24], in_=small[:, 8:16]),
                     reads=[B_small], writes=[B_small])
                P.op(DVE, lambda e: e.tensor_scalar(
                    out=small[:, 24:28], in0=small[:, 16:24].rearrange("p (q m) -> p q m", m=2)[:, :, 1],
                    scalar1=LAM, scalar2=None, op0=ALU.mult), reads=[B_small, B_const], writes=[B_small])
                for qs in range(4):
                    oi_ = nxt("ob", 2)
                    tb = tbuf[:, oi_, :]
                    ob = otile[:, ps * 4 + qs, :]
                    P.op(DVE, lambda e, qs=qs, tb=tb: e.tensor_scalar(
                        out=tb, in0=osb(qs * 2 + 1)[:, 0:128], scalar1=small[:, 24 + qs:25 + qs], scalar2=None,
                        op0=ALU.mult), reads=[oB, B_small], writes=[B_tbuf[oi_]])
                    P.op(DVE, lambda e, qs=qs, tb=tb, ob=ob: e.scalar_tensor_tensor(
                        out=ob, in0=osb(qs * 2)[:, 0:128], scalar=small[:, 16 + 2 * qs:17 + 2 * qs], in1=tb,
                        op0=ALU.mult, op1=ALU.subtract), reads=[oB, B_small, B_tbuf[oi_]],
                        writes=[oTB])
                    P.op(DVE, lambda e, qs=qs, ob=ob, tb=tb: e.scalar_tensor_tensor(
                        out=tb, in0=ob, scalar=1.0, in1=ob, op0=ALU.mult, op1=ALU.mult,
                        accum_out=small[:, 28 + qs:29 + qs]), reads=[oTB, B_small],
                        writes=[B_tbuf[oi_], B_small])
                P.op(DVE, lambda e: e.tensor_scalar(
                    out=small[:, c_ss:c_ss + 4], in0=small[:, 28:32], scalar1=1.0 / 128.0,
                    scalar2=EPS, op0=ALU.mult, op1=ALU.add), reads=[B_small], writes=[oTB])

                def stage2_fn():
                    P.op(ACT, lambda e: e.activation(out=small[:, c_ss:c_ss + 4], in_=small[:, c_ss:c_ss + 4],
                                                     func=AF.Ln), reads=[oTB], writes=[oTB])
                    P.op(ACT, lambda e: e.activation(out=small[:, c_rs:c_rs + 4], in_=small[:, c_ss:c_ss + 4],
                                                     func=AF.Exp, scale=-0.5), reads=[oTB], writes=[oTB])
                    for qs in range(4):
                        ob = otile[:, ps * 4 + qs, :]
                        P.op(DVE, lambda e, qs=qs, ob=ob: e.scalar_tensor_tensor(
                            out=ybf[:, qs, :], in0=ob, scalar=small[:, c_rs + qs:c_rs + qs + 1], in1=gsub[:, :],
                            op0=ALU.mult, op1=ALU.mult), reads=[oTB, B_const], writes=[B_ybf])

                    def stage2b_fn():
                        fns = []
                        for qs in range(4):
                            fns.append(lambda e, qs=qs: e.transpose(out=tpb[:, qs * 128:(qs + 1) * 128],
                                                                    in_=ybf[:, qs, :], identity=ident[:, :]))
                        P.group(PE, fns, reads=[B_ybf, B_const], writes=[B_tpb])
                        P.op(DVE, lambda e, h=h, q0=q0: e.tensor_copy(out=yat[:, h, q0:q0 + TT], in_=tpb[:, :]),
                             reads=[B_tpb], writes=[B_yat[h][tt]])

                    stage2.append((cur["ti"] + 6, stage2b_fn))
                    stage2.sort(key=lambda x: x[0])

                stage2.append((cur["ti"] + 20 + 8 * slot, stage2_fn))
                stage2.sort(key=lambda x: x[0])

            def halo_fix():
                P.dma(SP, S_halo, lambda e: e.dma_start(out=halo_sb[:, :, :],
                                                        in_=halo_all.ap().rearrange("(r p) f -> p r f", p=128)),
                      reads=[B_haloall], writes=[B_halo_sb])

                hs4 = halo_sb[:, :, :].rearrange("p r (c s t) -> p r c s t", c=4, s=2)
                hf4 = hfix[:, 0:16].rearrange("p (c s t) -> p c s t", c=4, s=2)
                P.op(DVE, lambda e: e.memset(hfix[:, :], 0.0), reads=[B_hfix], writes=[B_hfix])
                for r in range(4):
                    P.op(DVE, lambda e, r=r: e.scalar_tensor_tensor(
                        out=hf4[:, :, 0, :], in0=hs4[:, r, :, 0, :], scalar=bc_sb[:, SEL + 6 + r:SEL + 7 + r],
                        in1=hf4[:, :, 0, :], op0=ALU.mult, op1=ALU.add), reads=[B_halo_sb, B_const, B_hfix], writes=[B_hfix])
                    P.op(DVE, lambda e, r=r: e.scalar_tensor_tensor(
                        out=hf4[:, :, 1, :], in0=hs4[:, r, :, 0, :], scalar=bc_sb[:, SEL + 10 + r:SEL + 11 + r],
                        in1=hf4[:, :, 1, :], op0=ALU.mult, op1=ALU.add), reads=[B_halo_sb, B_const, B_hfix], writes=[B_hfix])
                    if r >= 1:
                        P.op(DVE, lambda e, r=r: e.scalar_tensor_tensor(
                            out=hf4[:, :, 1, :], in0=hs4[:, r, :, 1, :], scalar=bc_sb[:, 400 + (r - 1):401 + (r - 1)],
                            in1=hf4[:, :, 1, :], op0=ALU.mult, op1=ALU.add), reads=[B_halo_sb, B_const, B_hfix],
                            writes=[B_hfix])
                for cc in range(4):
                    w0 = vecs_sb[:, 40 + cc:41 + cc]
                    w1 = vecs_sb[:, 44 + cc:45 + cc]
                    for slot in range(2):
                        hc = (cc * 2 + slot) * 2
                        um2 = hfix[:, hc:hc + 1]
                        um1 = hfix[:, hc + 1:hc + 2]
                        a0 = acc01[:, hc:hc + 1]
                        a1 = acc01[:, hc + 1:hc + 2]
                        f0 = hfix[:, 16 + hc:16 + hc + 1]
                        f1 = hfix[:, 16 + hc + 1:16 + hc + 2]
                        P.op(DVE, lambda e, um1=um1, w1=w1, a0=a0, f0=f0: e.scalar_tensor_tensor(
                            out=f0, in0=um1, scalar=w1, in1=a0, op0=ALU.mult, op1=ALU.add),
                            reads=[B_hfix, B_halo, B_const], writes=[B_hfix])
                        P.op(DVE, lambda e, um2=um2, w0=w0, f0=f0: e.scalar_tensor_tensor(
                            out=f0, in0=um2, scalar=w0, in1=f0, op0=ALU.mult, op1=ALU.add),
                            reads=[B_hfix, B_const], writes=[B_hfix])
                        P.op(DVE, lambda e, um1=um1, w0=w0, a1=a1, f1=f1: e.scalar_tensor_tensor(
                            out=f1, in0=um1, scalar=w0, in1=a1, op0=ALU.mult, op1=ALU.add),
                            reads=[B_hfix, B_halo, B_const], writes=[B_hfix])
                        tok0 = slot * SL
                        P.op(DVE, lambda e, cc=cc, tok0=tok0, hc=hc: e.tensor_tensor(
                            out=ycv[:, cc, tok0:tok0 + 2], in0=hfix[:, 16 + hc:16 + hc + 2], in1=cb01[:, hc:hc + 2],
                            op=ALU.mult), reads=[B_hfix, B_halo], writes=[B_ycv[cc][2 * slot]])


            ensure_loaded(2)
            pend = None
            stage2 = []
            for ti, (wi, ki, kt) in enumerate(tiles):
                es_ = emit_qk(wi, ki, kt)
                if pend is not None:
                    emit_av(*pend)
                pend = (wi, ki, kt, es_)
                cur["ti"] = ti
                if ti == 150:
                    halo_fix()
                while stage2 and stage2[0][0] <= ti:
                    stage2.pop(0)[1]()
            emit_av(*pend)
            while stage2:
                stage2.pop(0)[1]()
            alias_barrier(B_qsel, B_ws)
            alias_barrier(B_kaux + B_vaux, B_xn[1])
            alias_barrier(B_oab, B_rope)
            alias_barrier(B_otile, B_tmp)


            if STOP < 5:
                raise _Stop()
            oblocks = [load_wsm(w_out, b * 256) for b in range(4)]
            for slot in (1, 0):
                for b in range(4):
                    ow, oB = oblocks[b]
                    for dl in range(2):
                        dc = 2 * b + dl
                        for ttl in range(2):
                            tt = 2 * slot + ttl
                            bi, bk, bb = bank_alloc()
                            fns = []
                            for c in range(8):
                                src = ycv[:, c, tt * TT:(tt + 1) * TT] if c < 4 else yat[:, c - 4, tt * TT:(tt + 1) * TT]
                                fns.append(lambda e, c=c, bk=bk, src=src, ow=ow, dl=dl: e.matmul(
                                    bk[:, :], lhsT=ow[:, c, dl * 128:(dl + 1) * 128], rhs=src, start=(c == 0),
                                    stop=(c == 7)))
                            P.group(PE, fns, reads=[oB] + [B_ycv[c][tt] for c in range(4)]
                                    + [B_yat[c][tt] for c in range(4)], writes=[bb])
                            hb = B_hT[dc][tt]
                            P.op(DVE, lambda e, bk=bk, dc=dc, tt=tt: e.tensor_tensor(
                                out=hT[:, dc, tt * TT:(tt + 1) * TT], in0=bk[:, :], in1=hT[:, dc, tt * TT:(tt + 1) * TT],
                                op=ALU.add), reads=[bb, hb], writes=[hb])
                    if slot == 0 and b == 0 and STOP >= 6:
                        norm_sq(1, 16)
                    if slot == 0 and b == 2 and STOP >= 6:
                        norm_stat(1)
                        norm_apply(1, 16)
            for b in range(4):
                ws_release(oblocks[b][1])

            if STOP < 6:
                raise _Stop()
            alias_barrier([b for r in B_yat for b in r], B_xn[0])
            alias_barrier(attn_bufs, B_ffn_arena)
            ffn(1, w_g2, w_u2, w_d2, hook=(0, 16))
            ffn(0, w_g2, w_u2, w_d2, hook=(1, 24))

            if STOP < 7:
                raise _Stop()
            alias_barrier(B_ffn_arena, [B_pT])
            P.dma(POOL, S_pT, lambda e: e.dma_start(out=pTb[:, :, :], in_=pT.rearrange("(k p) t -> p k t", p=128)),
                  writes=[B_pT])
            wpp_sb = arena[:, 24576:26624].rearrange("p (k d) -> p k d", k=2)
            wppB = Buf()
            alias_barrier(attn_bufs, [wppB, B_pT])
            S_wpp = dsem("d_wpp")
            P.dma(POOL, S_wpp, lambda e: e.dma_start(out=wpp_sb, in_=w_pp.rearrange("(k p) d -> p k d", p=128)),
                  writes=[wppB])
            gblocks = [load_wsm(w_pg, b * 256) for b in range(4)]
            outv = outT.rearrange("(c p) t -> p c t", p=128)

            def final_norm(slot, part="both"):
                if part in ("both", "early"):
                    norm_sq(slot, 32)
                if part == "early":
                    return
                norm_stat(slot)
                for ttl in range(2):
                    tt = 2 * slot + ttl
                    r = rstd[:, ttl, :]
                    for c in range(8):
                        ti_ = nxt("tmp", 4)
                        t, tB = tmp[:, ti_, :], B_tmp[ti_]
                        P.op(DVE, lambda e, c=c, r=r, tt=tt, t=t: e.scalar_tensor_tensor(
                            out=t, in0=hT[:, c, tt * TT:(tt + 1) * TT], scalar=vecs_sb[:, 32 + c:33 + c], in1=r,
                            op0=ALU.mult, op1=ALU.mult), reads=[B_hT[c][tt], B_rstd[ttl], B_const], writes=[tB])
                        P.dma(SP, S_out[ti_], lambda e, c=c, tt=tt, t=t: e.dma_start(
                            out=outv[:, c, tt * TT:(tt + 1) * TT], in_=t), reads=[tB])

            for slot in (1, 0):
                for b in range(4):
                    gw, gB = gblocks[b]
                    for dl in range(2):
                        dc = 2 * b + dl
                        for ttl in range(2):
                            tt = 2 * slot + ttl
                            gk, gb = proj_mm(gw, gB, dl, tt)
                            bi, pk, pb = bank_alloc()
                            fns = []
                            for k in range(2):
                                fns.append(lambda e, k=k, pk=pk, dc=dc, tt=tt: e.matmul(
                                    pk[:, :], lhsT=wpp_sb[:, k, dc * 128:(dc + 1) * 128],
                                    rhs=pTb[:, k, tt * TT:(tt + 1) * TT], start=(k == 0), stop=(k == 1)))
                            P.group(PE, fns, reads=[wppB, B_pT], writes=[pb])
                            t, tB = tmp_alloc()
                            P.op(ACT, lambda e, gk=gk, t=t: e.activation(out=t, in_=gk[:, :], func=AF.Sigmoid),
                                 reads=[gb], writes=[tB])
                            P.op(DVE, lambda e, t=t, pk=pk: e.tensor_tensor(out=t, in0=t, in1=pk[:, :], op=ALU.mult),
                                 reads=[tB, pb], writes=[tB])
                            hb = B_hT[dc][tt]
                            P.op(DVE, lambda e, t=t, dc=dc, tt=tt: e.tensor_tensor(
                                out=hT[:, dc, tt * TT:(tt + 1) * TT], in0=t, in1=hT[:, dc, tt * TT:(tt + 1) * TT],
                                op=ALU.add), reads=[tB, hb], writes=[hb])
                    if slot == 1 and b == 0:
                        norm_sq(0, 24)
                    if slot == 1 and b == 2:
                        norm_stat(0)
                        norm_apply(0, 24)
                    if STOP >= 8 and slot == 0 and b == 0:
                        final_norm(1, "early")
                    if STOP >= 8 and slot == 0 and b == 2:
                        final_norm(1, "late")
            if STOP >= 8:
                final_norm(0)
            for b in range(4):
                ws_release(gblocks[b][1])
            if STOP < 8:
                raise _Stop()

        stopped = False
        try:
            body()
        except _Stop:
            stopped = True
        if stopped:
            outv_dbg = outT.rearrange("(c p) t -> p c t", p=128)
            for c in range(8):
                oi_ = c % 4
                P.dma(SP, S_out[oi_], lambda e, c=c: e.dma_start(out=outv_dbg[:, c, :], in_=hT[:, c, :]),
                      reads=[B_hT[c][t] for t in range(4)])
        finals = [(d.sem, d.cnt) for d in S_out if d.cnt > 0]

        def _check():
            engs = [PE, ACT, DVE, POOL, SP]
            pc = {e.name: 0 for e in engs}
            semv = {}
            progress = True
            while progress:
                progress = False
                for e in engs:
                    while pc[e.name] < len(e.ops):
                        waits, fn, inc = e.ops[pc[e.name]]
                        if any(semv.get(id(sm), 0) < v for sm, v in waits):
                            break
                        if inc is not None:
                            semv[id(inc[0])] = semv.get(id(inc[0]), 0) + (inc[1] if inc[1] else 1)
                        pc[e.name] += 1
                        progress = True
            stuck = {e.name: (pc[e.name], len(e.ops)) for e in engs if pc[e.name] < len(e.ops)}
            if stuck:
                msg = []
                for e in engs:
                    if pc[e.name] < len(e.ops):
                        waits, fn, inc = e.ops[pc[e.name]]
                        msg.append((e.name, pc[e.name], len(e.ops),
                                    [(getattr(sm, "name", str(sm)), v, semv.get(id(sm), 0)) for sm, v in waits]))
                raise RuntimeError(f"DEADLOCK in recorded program: {msg}")
            print("sync check ok; ops per engine:", {e.name: len(e.ops) for e in engs}, flush=True)

        _check()

        with nc.Block() as block:
            @block.tensor
            def _(e):
                emit(e, PE)

            @block.scalar
            def _(e):
                emit(e, ACT)

            @block.vector
            def _(e):
                emit(e, DVE)

            @block.gpsimd
            def _(e):
                emit(e, POOL)

            @block.sync
            def _(e):
                emit(e, SP)
                for s_, v in finals:
                    e.wait_ge(s_, v)
    return nc


def _prep_shared(inp):
    f32 = np.float32
    w_in = np.asarray(inp["w_in"][0], f32)
    cb, cc_, cx = w_in[:, 0:512], w_in[:, 512:1024], w_in[:, 1024:1536]
    q, k, v = w_in[:, 1536:2048], w_in[:, 2048:2560], w_in[:, 2560:3072]
    idx = np.arange(512)
    dloc = idx % 64
    partner = idx - dloc + (dloc + 32) % 64
    rq, rk = q[:, partner], k[:, partner]
    blocks = []
    blocks.append(np.concatenate([cb[:, 0:128], cb[:, 128:256]], 1))
    blocks.append(np.concatenate([cb[:, 256:384], cb[:, 384:512]], 1))
    for c in range(4):
        blocks.append(np.concatenate([cc_[:, c * 128:(c + 1) * 128], cx[:, c * 128:(c + 1) * 128]], 1))
    for h in range(4):
        blocks.append(np.concatenate([q[:, h * 128:(h + 1) * 128], rq[:, h * 128:(h + 1) * 128]], 1))
    for h in range(4):
        blocks.append(np.concatenate([k[:, h * 128:(h + 1) * 128], rk[:, h * 128:(h + 1) * 128]], 1))
    blocks.append(v[:, 0:256])
    blocks.append(v[:, 256:512])
    w_inx = np.ascontiguousarray(np.concatenate(blocks, 1))

    vecs = np.zeros((128, 64), f32)

    def put(col, vec):
        vecs[:, col:col + 8] = np.asarray(vec, f32).reshape(8, 128).T

    put(0, inp["ffn1_norm"][0]); put(8, inp["mix_norm"][0]); put(16, inp["ffn2_norm"][0])
    put(24, inp["ple_norm"][0]); put(32, inp["final_norm"])
    cw = np.asarray(inp["conv_w"][0], f32)
    for tap in range(3):
        vecs[:, 40 + tap * 4:40 + tap * 4 + 4] = cw[tap].reshape(4, 128).T
    shared = {
        "w_g1": np.ascontiguousarray(inp["ffn1_w_gate"][0], dtype=f32), "w_u1": np.ascontiguousarray(inp["ffn1_w_up"][0], dtype=f32),
        "w_d1": np.ascontiguousarray(inp["ffn1_w_down"][0], dtype=f32),
        "w_g2": np.ascontiguousarray(inp["ffn2_w_gate"][0], dtype=f32), "w_u2": np.ascontiguousarray(inp["ffn2_w_up"][0], dtype=f32),
        "w_d2": np.ascontiguousarray(inp["ffn2_w_down"][0], dtype=f32),
        "w_inx": w_inx, "w_out": np.ascontiguousarray(inp["w_out"][0], dtype=f32),
        "w_pg": np.ascontiguousarray(inp["w_ple_gate"][0], dtype=f32), "w_pp": np.ascontiguousarray(inp["w_ple_proj"][0], dtype=f32),
        "vecs": vecs,
    }
    bc = np.zeros((128, 416), f32)
    bc[:, 0:128] = np.asarray(inp["subln_g"][0], f32)[None, :]
    bc[:, 128:192] = np.asarray(inp["lambda_q1"][0], f32)[None, :]
    bc[:, 192:256] = np.asarray(inp["lambda_k1"][0], f32)[None, :]
    bc[:, 256:320] = np.asarray(inp["lambda_q2"][0], f32)[None, :]
    bc[:, 320:384] = np.asarray(inp["lambda_k2"][0], f32)[None, :]
    return shared, bc


def _core_tokens(j):
    a = np.arange(1024 * j, 1024 * j + 1024)
    b = np.arange(1024 * (7 - j), 1024 * (7 - j) + 1024)
    return np.concatenate([a, b])


def kernel(**inp):
    f32 = np.float32
    x = np.asarray(inp["x"], f32)
    p = np.asarray(inp["p"], f32)[0]
    shared, bc0 = _prep_shared(inp)
    inv_freq = (1.0 / (np.float32(10000.0) ** (np.arange(0, 64, 2, dtype=f32) / np.float32(64)))).astype(f32)
    in_maps = []
    for c in range(8):
        b, j = c // 4, c % 4
        tok = _core_tokens(j)
        m = dict(shared)
        m["xT"] = np.ascontiguousarray(x[b, tok, :].T)
        m["pT"] = np.ascontiguousarray(p[b, tok, :].T)
        ang = (tok.astype(f32)[:, None] * inv_freq[None, :]).astype(f32)
        cos = np.cos(ang).astype(f32).T
        sin = np.sin(ang).astype(f32).T
        pi = np.arange(128)
        cosT = cos[pi % 32, :]
        sgn = np.where((pi % 64) < 32, -1.0, 1.0).astype(f32)[:, None]
        sinT = sin[pi % 32, :] * sgn
        rope = np.zeros((128, 4, 1024), f32)
        for tt in range(4):
            rope[:, tt, 0:512] = cosT[:, tt * 512:(tt + 1) * 512]
            rope[:, tt, 512:1024] = sinT[:, tt * 512:(tt + 1) * 512]
        m["rope"] = rope
        bc = bc0.copy()
        S = 384
        for v in range(3):
            bc[:, S + v] = 1.0 if v < j else 0.0
            bc[:, S + 3 + v] = 0.0 if v < j else 1.0
        for r in range(4):
            bc[:, S + 6 + r] = 1.0 if (j >= 1 and r == j - 1) else 0.0
        for r in range(4):
            bc[:, S + 10 + r] = 1.0 if (j == 3 and r == 3) else 0.0
        for r in range(1, 4):
            bc[:, 400 + (r - 1)] = 1.0 if (j <= 2 and r == j + 1) else 0.0
        m["bcast"] = bc
        in_maps.append(m)
    import os
    nc = build_nc(int(os.environ.get('KSTOP', '99')))
    res = run_bass_kernel_spmd(nc, in_maps, core_ids=list(range(8)))
    out = np.zeros((2, 8192, D), f32)
    for c in range(8):
        b, j = c // 4, c % 4
        out[b, _core_tokens(j), :] = res.results[c]["outT"].T
    return out
```

```python
import numpy as np
import concourse.bass as bass
import concourse.mybir as mybir
from concourse.bass_utils import run_bass_kernel_spmd

F32 = mybir.dt.float32
BF16 = mybir.dt.bfloat16
AF = mybir.ActivationFunctionType
ALU = mybir.AluOpType

D = 1024
DFF = 2816
NT = 2048
SL = 1024
TT = 512
EPS = 1e-6
NEG = -30000.0
GROUPS = [[0, 1, 2, 3], [4, 5, 6, 7]]


class Buf:
    __slots__ = ("lw", "rd")

    def __init__(self):
        self.lw = None
        self.rd = {}


class Eng:
    def __init__(self, name, sem, is_pe=False):
        self.name = name
        self.sem = sem
        self.cnt = 0
        self.seen = {}
        self.ops = []
        self.is_pe = is_pe
        self.step = 1


class DSem:
    def __init__(self, sem, step=16):
        self.sem = sem
        self.cnt = 0
        self.step = step
        self.is_pe = False
        self.noval = (step == 1)


class Prog:
    def __init__(self):
        self.engs = {}

    def _deps(self, eng, reads, writes):
        deps = {}

        def add(tok):
            if tok is None:
                return
            k, v = tok
            if deps.get(k, 0) < v:
                deps[k] = v

        for b in reads:
            add(b.lw)
        for b in writes:
            add(b.lw)
            for k, v in b.rd.items():
                add((k, v))
        waits = []
        for k, v in deps.items():
            if k is eng and eng.is_pe:
                continue
            if eng.seen.get(k, 0) >= v:
                continue
            eng.seen[k] = v
            waits.append((k.sem, v))
        return waits

    @staticmethod
    def _mark(tok, reads, writes):
        k, v = tok
        for b in reads:
            if b.rd.get(k, 0) < v:
                b.rd[k] = v
        for b in writes:
            b.lw = tok
            b.rd = {}

    def op(self, eng, fn, reads=(), writes=()):
        waits = self._deps(eng, reads, writes)
        eng.cnt += 1
        tok = (eng, eng.cnt)
        eng.ops.append((waits, fn, (eng.sem, 1)))
        self._mark(tok, reads, writes)
        return tok

    def group(self, eng, fns, reads=(), writes=()):
        waits = self._deps(eng, reads, writes)
        eng.cnt += 1
        tok = (eng, eng.cnt)
        n = len(fns)
        for i, fn in enumerate(fns):
            eng.ops.append((waits if i == 0 else [], fn, (eng.sem, 1) if i == n - 1 else None))
        self._mark(tok, reads, writes)
        return tok

    def dma(self, eng, dsem, fn, reads=(), writes=()):
        waits = self._deps(eng, reads, writes)
        dsem.cnt += dsem.step
        tok = (dsem, dsem.cnt)
        eng.ops.append((waits, fn, (dsem.sem, 0 if dsem.noval else dsem.step)))
        self._mark(tok, reads, writes)
        return tok


def emit(e, eng):
    for waits, fn, inc in eng.ops:
        for sem, v in waits:
            e.wait_ge(sem, v)
        ins = fn(e)
        if inc is not None:
            if inc[1] == 0:
                ins.then_inc(inc[0])
            else:
                ins.then_inc(inc[0], inc[1])


def alias_barrier(old, new):
    for nb in new:
        for ob in old:
            if ob.lw is not None:
                k, v = ob.lw
                if nb.rd.get(k, 0) < v:
                    nb.rd[k] = v
            for k, v in ob.rd.items():
                if nb.rd.get(k, 0) < v:
                    nb.rd[k] = v


def build_nc(STOP=99):
    nc = bass.Bass("TRN2", target_bir_lowering=False)

    def din(name, shape, dt=F32):
        return nc.dram_tensor(name, list(shape), dt, kind="ExternalInput").ap()

    xT = din("xT", [D, NT])
    pT = din("pT", [256, NT])
    rope = din("rope", [128, 4, 2 * TT])
    vecs = din("vecs", [128, 64])
    bcast = din("bcast", [128, 416])
    w_g1 = din("w_g1", [D, DFF]); w_u1 = din("w_u1", [D, DFF]); w_d1 = din("w_d1", [DFF, D])
    w_g2 = din("w_g2", [D, DFF]); w_u2 = din("w_u2", [D, DFF]); w_d2 = din("w_d2", [DFF, D])
    w_inx = din("w_inx", [D, 4096])
    w_out = din("w_out", [D, D])
    w_pg = din("w_pg", [D, D])
    w_pp = din("w_pp", [256, D])
    outT = nc.dram_tensor("outT", [D, NT], F32, kind="ExternalOutput").ap()

    kT_mine = [nc.dram_tensor(f"kT_mine{s}", [128, 4 * SL], BF16) for s in range(2)]
    kT_all = [nc.dram_tensor(f"kT_all{s}", [4 * 128, 4 * SL], BF16) for s in range(2)]
    v_mine = [[nc.dram_tensor(f"v_mine{s}_{hp}", [256, 1056], BF16) for hp in range(2)] for s in range(2)]
    v_all = [[nc.dram_tensor(f"v_all{s}_{hp}", [4 * 256, 1056], BF16) for hp in range(2)] for s in range(2)]
    halo_mine = nc.dram_tensor("halo_mine", [128, 16], F32)
    halo_all = nc.dram_tensor("halo_all", [512, 16], F32)

    from contextlib import ExitStack
    with ExitStack() as es:
        def sb(name, shape, dt):
            return es.enter_context(nc.sbuf_tensor(name, list(shape), dt))

        def sem(name):
            return es.enter_context(nc.semaphore(name))

        hT = sb("hT", [128, 8, NT], F32)
        xn = sb("xn", [128, 2, 8, SL], BF16)
        wsm = sb("wsm", [128, 4, 2048], BF16)
        ARENA = 27264
        arena = sb("arena", [128, ARENA], BF16)
        tmp = sb("tmp", [128, 4, TT], F32)
        rstd = sb("rstd", [128, 2, TT], F32)
        ropes = sb("ropes", [128, 2, 1032], F32)
        ubuf = sb("ubuf", [128, SL + 2], F32)
        accb = sb("accb", [128, SL], F32)
        ones_bf = sb("ones_bf", [128, 128], BF16)
        ident = sb("ident", [128, 128], BF16)
        vecs_sb = sb("vecs_sb", [128, 64], F32)
        bc_sb = sb("bc_sb", [128, 416], F32)
        gsub = sb("gsub", [128, 128], F32)
        small = sb("small", [128, 96], F32)
        junk = sb("junk", [128, 128], F32)
        halo_sb = sb("halo_sb", [128, 4, 16], F32)
        halo_st = sb("halo_st", [128, 16], F32)
        acc01 = sb("acc01", [128, 16], F32)
        cb01 = sb("cb01", [128, 16], F32)
        hfix = sb("hfix", [128, 32], F32)
        obuf = sb("obuf", [128, 2, 128], F32)
        tbuf = sb("tbuf", [128, 2, 128], F32)

        banks = [es.enter_context(nc.psum_tensor(f"bk{i}", [128, TT], F32)) for i in range(7)]
        tpb = es.enter_context(nc.psum_tensor("tpb", [128, TT], BF16))

        def av(off, n, **kw):
            v = arena[:, off:off + n]
            return v

        QT = arena[:, 0:8192].rearrange("p (h t) -> p h t", h=4)
        ycv = arena[:, 8192:16384].rearrange("p (h t) -> p h t", h=4)
        A0 = 16384
        Kb = arena[:, A0:A0 + 3072].rearrange("p (s k) -> p s k", s=3)
        Vb = arena[:, A0 + 3072:A0 + 6240].rearrange("p (s k e) -> p s k e", s=3, k=8)
        Eb = arena[:, A0 + 6240:A0 + 8288].rearrange("p (s q) -> p s q", s=4)
        kst = arena[:, A0 + 8288:A0 + 9312].rearrange("p (s q) -> p s q", s=2)
        NVST = 4
        vst = arena[:, A0:A0 + NVST * 528].rearrange("p (s h e) -> p s h e", s=NVST, h=4)
        ybf = arena[:, A0 + 10368:A0 + 10880].rearrange("p (s e) -> p s e", s=4)
        h1 = arena[:, 0:12288].rearrange("p (f t) -> p f t", f=12)
        wdn = arena[:, 12288:18432].rearrange("p (s f d) -> p s f d", s=2, f=12)
        pTb = arena[:, 20480:24576].rearrange("p (k t) -> p k t", k=2)
        yat = xn[:, 0].rearrange("p c t -> p (c t)").rearrange("p (h t) -> p h t", h=4)

        P = Prog()
        PE = Eng("pe", sem("s_pe"), is_pe=True)
        ACT = Eng("act", sem("s_act"))
        DVE = Eng("dve", sem("s_dve"))
        POOL = Eng("pool", sem("s_pool"))
        SP = Eng("sp", sem("s_sp"))

        def dsem(name, step=16):
            return DSem(sem(name), step)

        B_hT = [[Buf() for _ in range(4)] for _ in range(8)]
        B_xn = [[Buf() for _ in range(2)] for _ in range(2)]
        B_ws = [Buf() for _ in range(4)]
        B_bank = [Buf() for _ in range(7)]
        B_tpb = Buf()
        B_tmp = [Buf() for _ in range(4)]
        B_rstd = [Buf() for _ in range(2)]
        B_rope = [Buf() for _ in range(2)]
        B_h1 = [Buf() for _ in range(12)]
        B_wdn = [Buf() for _ in range(2)]
        B_QT = [[Buf() for _ in range(4)] for _ in range(4)]
        B_ycv = [[Buf() for _ in range(4)] for _ in range(4)]
        B_yat = [[Buf() for _ in range(4)] for _ in range(4)]
        B_Kb = [Buf() for _ in range(3)]
        B_Vb = [Buf() for _ in range(3)]
        B_E = [Buf() for _ in range(4)]
        B_kst = [Buf() for _ in range(2)]
        B_vst = [Buf() for _ in range(4)]
        B_ybf = Buf()
        B_pT = Buf()
        B_const = Buf()
        B_small = Buf()
        B_u = Buf()
        B_acc = Buf()
        B_halo = Buf()
        B_halo_sb = Buf()
        B_hfix = Buf()
        B_obuf = [Buf(), Buf()]
        B_tbuf = [Buf(), Buf()]
        B_kTm = [[Buf() for _ in range(4)] for _ in range(4)]
        B_vm = [Buf() for _ in range(16)]
        B_halom = Buf()
        B_kTall = [Buf(), Buf()]
        B_vall = [Buf() for _ in range(4)]
        B_haloall = Buf()
        B_out = Buf()

        S_ws = [dsem(f"d_ws{i}") for i in range(4)]
        S_wdn = [dsem(f"d_wd{i}") for i in range(2)]
        S_misc = dsem("d_misc")
        S_x = dsem("d_x")
        S_rope = [dsem(f"d_rp{i}") for i in range(2)]
        S_kst = [dsem(f"d_ks{i}") for i in range(2)]
        S_vst = [dsem(f"d_vs{i}") for i in range(4)]
        S_Kb = [dsem(f"d_kb{i}") for i in range(3)]
        S_Vb = [dsem(f"d_vb{i}") for i in range(3)]
        S_cc = [dsem(f"d_cc{i}", 1) for i in range(3)]
        S_cck = [dsem(f"d_cck{i}", 1) for i in range(2)]
        S_ccv = [dsem(f"d_ccv{i}", 1) for i in range(4)]
        S_halo = dsem("d_halo")
        S_pT = dsem("d_pT")
        S_out = [dsem(f"d_out{i}") for i in range(4)]

        rr = {"bank": 0, "tmp": 0, "ws": 0, "wd": 0, "E": 0, "kv": 0, "rope": 0, "kst": 0, "vst": 0, "ob": 0, "out": 0}

        def nxt(key, n):
            i = rr[key]
            rr[key] = (i + 1) % n
            return i

        P.dma(SP, S_misc, lambda e: e.dma_start(out=vecs_sb[:, :], in_=vecs), writes=[B_const])
        P.dma(SP, S_misc, lambda e: e.dma_start(out=bc_sb[:, :], in_=bcast), writes=[B_const])
        xTv = xT.rearrange("(c p) t -> p c t", p=128)
        S_x2 = dsem("d_x2")
        for slot_, sx in ((0, S_x), (1, S_x2)):
            for c in range(8):
                xtok = P.dma(SP, sx, lambda e, c=c, slot_=slot_: e.dma_start(
                    out=hT[:, c, slot_ * SL:(slot_ + 1) * SL], in_=xTv[:, c, slot_ * SL:(slot_ + 1) * SL]),
                    writes=[B_hT[c][2 * slot_], B_hT[c][2 * slot_ + 1]])
            for c in range(8):
                for t in range(2):
                    B_hT[c][2 * slot_ + t].lw = xtok
        P.op(POOL, lambda e: e.memset(ones_bf[:, :], 1.0), writes=[B_const])
        P.op(POOL, lambda e: e.affine_select(out=ident[:, :], in_=ones_bf[:, :], pattern=[[1, 128]],
                                              compare_op=ALU.is_equal, fill=0.0, base=0,
                                              channel_multiplier=-1), reads=[B_const], writes=[B_const])
        P.op(POOL, lambda e: e.memset(ubuf[:, 0:2], 0.0), writes=[B_u])
        P.op(DVE, lambda e: e.tensor_scalar(out=gsub[:, :], in0=bc_sb[:, 0:128], scalar1=0.8, scalar2=None,
                                            op0=ALU.mult), reads=[B_const], writes=[B_const])
        P.op(DVE, lambda e: e.scalar_tensor_tensor(out=junk[:, 0:64], in0=bc_sb[:, 128:192], scalar=1.0,
                                                   in1=bc_sb[:, 192:256], op0=ALU.mult, op1=ALU.mult,
                                                   accum_out=small[:, 0:1]), reads=[B_const], writes=[B_small])
        P.op(DVE, lambda e: e.scalar_tensor_tensor(out=junk[:, 64:128], in0=bc_sb[:, 256:320], scalar=1.0,
                                                   in1=bc_sb[:, 320:384], op0=ALU.mult, op1=ALU.mult,
                                                   accum_out=small[:, 1:2]), reads=[B_const, B_small], writes=[B_small])
        P.op(ACT, lambda e: e.activation(out=small[:, 2:4], in_=small[:, 0:2], func=AF.Exp),
             reads=[B_small], writes=[B_small])
        P.op(DVE, lambda e: e.tensor_tensor(out=small[:, 4:5], in0=small[:, 2:3], in1=small[:, 3:4],
                                            op=ALU.subtract), reads=[B_small], writes=[B_small])
        P.op(DVE, lambda e: e.tensor_scalar(out=small[:, 5:6], in0=small[:, 4:5], scalar1=0.2, scalar2=None,
                                            op0=ALU.add), reads=[B_small], writes=[B_small])
        LAM = small[:, 5:6]
        ZERO = small[:, 6:7]
        P.op(DVE, lambda e: e.memset(small[:, 6:8], 0.0), reads=[B_small], writes=[B_small])
        P.op(DVE, lambda e: e.tensor_copy(out=small[:, 7:8], in_=small[:, 6:7]),
             reads=[B_small], writes=[B_const])

        def bank_alloc():
            i = nxt("bank", 7)
            return i, banks[i], B_bank[i]

        def tmp_alloc():
            i = nxt("tmp", 4)
            return tmp[:, i, :], B_tmp[i]

        ws_free = [0, 1, 2, 3]
        wd_free = [0, 1]
        ws_slot_of = {}

        def ws_release(B):
            ws_free.append(ws_slot_of.pop(id(B)))

        def load_wsm(w, col0, kchunks=8, ncols=256):
            i = ws_free.pop(0)
            ws_slot_of[id(B_ws[i])] = i
            src = w.rearrange("(c p) f -> p c f", p=128)[:, :, col0:col0 + ncols]
            dst = wsm[:, i, 0:kchunks * ncols].rearrange("p (c f) -> p c f", c=kchunks)
            P.dma(POOL, S_ws[i], lambda e: e.dma_start(out=dst, in_=src), writes=[B_ws[i]])
            return dst, B_ws[i]

        def wd_release(B):
            wd_free.append(B_wdn.index(B))

        def load_wdn(w, row0, nf, col0):
            i = wd_free.pop(0)
            src = w[row0 * 128:(row0 + nf) * 128, :].rearrange("(c p) d -> p c d", p=128)[:, :, col0:col0 + 256]
            dst = wdn[:, i, 0:nf, :]
            P.dma(POOL, S_wdn[i], lambda e: e.dma_start(out=dst, in_=src), writes=[B_wdn[i]])
            return dst, B_wdn[i]

        def norm_sq(slot, gcol):
            for c in range(8):
                P.op(ACT, lambda e, c=c: e.activation(out=xn[:, slot, c, :], in_=hT[:, c, slot * SL:(slot + 1) * SL],
                                                      func=AF.Square),
                     reads=[B_hT[c][2 * slot], B_hT[c][2 * slot + 1]], writes=[B_xn[slot][0], B_xn[slot][1]])

        def norm_sq_list(slot, gcol):
            out = []
            for c in range(8):
                out.append(lambda c=c: P.op(ACT, lambda e, c=c: e.activation(
                    out=xn[:, slot, c, :], in_=hT[:, c, slot * SL:(slot + 1) * SL], func=AF.Square),
                    reads=[B_hT[c][2 * slot], B_hT[c][2 * slot + 1]], writes=[B_xn[slot][0], B_xn[slot][1]]))
            return out

        def norm_stat(slot):
            for ttl in range(2):
                bi, bk, bb = bank_alloc()
                fns = []
                for c in range(8):
                    fns.append(lambda e, c=c, bk=bk, ttl=ttl: e.matmul(bk[:, :], lhsT=ones_bf[:, :],
                                                                       rhs=xn[:, slot, c, ttl * TT:(ttl + 1) * TT],
                                                                       start=(c == 0), stop=(c == 7)))
                P.group(PE, fns, reads=[B_const, B_xn[slot][ttl]], writes=[bb])
                r = rstd[:, ttl, :]
                P.op(DVE, lambda e, bk=bk, r=r: e.tensor_scalar(out=r, in0=bk[:, :], scalar1=1.0 / D, scalar2=EPS,
                                                                op0=ALU.mult, op1=ALU.add),
                     reads=[bb], writes=[B_rstd[ttl]])
                P.op(ACT, lambda e, r=r: e.activation(out=r, in_=r, func=AF.Sqrt),
                     reads=[B_rstd[ttl]], writes=[B_rstd[ttl]])
                P.op(DVE, lambda e, r=r: e.reciprocal(out=r, in_=r), reads=[B_rstd[ttl]], writes=[B_rstd[ttl]])

        def norm_apply(slot, gcol):
            for ttl in range(2):
                r = rstd[:, ttl, :]
                for c in range(8):
                    tok0 = slot * SL + ttl * TT
                    P.op(DVE, lambda e, c=c, r=r, tok0=tok0, ttl=ttl: e.scalar_tensor_tensor(
                        out=xn[:, slot, c, ttl * TT:(ttl + 1) * TT], in0=hT[:, c, tok0:tok0 + TT],
                        scalar=vecs_sb[:, gcol + c:gcol + c + 1], in1=r, op0=ALU.mult, op1=ALU.mult),
                        reads=[B_hT[c][2 * slot + ttl], B_rstd[ttl], B_const], writes=[B_xn[slot][ttl]])

        def norm(slot, gcol):
            norm_sq(slot, gcol)
            norm_stat(slot)
            norm_apply(slot, gcol)

        def ffn(slot, wg, wu, wd, hook=None):
            halves = [(0, 12), (12, 10)]
            for hi, (f0, nf) in enumerate(halves):
                nblk = nf // 2
                pend = []

                def issue(b):
                    g = load_wsm(wg, (f0 + 2 * b) * 128)
                    u = load_wsm(wu, (f0 + 2 * b) * 128)
                    pend.append((g, u))

                issue(0)
                for b in range(nblk):
                    if b + 1 < nblk:
                        issue(b + 1)
                    (gw, gB), (uw, uB) = pend.pop(0)
                    for fl in range(2):
                        fcl = 2 * b + fl
                        res = {}
                        for name, w_, wB in (("g", gw, gB), ("u", uw, uB)):
                            for ttl in range(2):
                                bi, bk, bb = bank_alloc()
                                fns = []
                                for c in range(8):
                                    fns.append(lambda e, c=c, bk=bk, w_=w_, ttl=ttl, fl=fl: e.matmul(
                                        bk[:, :], lhsT=w_[:, c, fl * 128:(fl + 1) * 128],
                                        rhs=xn[:, slot, c, ttl * TT:(ttl + 1) * TT], start=(c == 0), stop=(c == 7)))
                                P.group(PE, fns, reads=[wB, B_xn[slot][ttl]], writes=[bb])
                                res[(name, ttl)] = (bk, bb)
                        for ttl in range(2):
                            gk, gb = res[("g", ttl)]
                            uk, ub = res[("u", ttl)]
                            t, tB = tmp_alloc()
                            P.op(ACT, lambda e, gk=gk, t=t: e.activation(out=t, in_=gk[:, :], func=AF.Silu),
                                 reads=[gb], writes=[tB])
                            P.op(DVE, lambda e, t=t, uk=uk, fcl=fcl, ttl=ttl: e.tensor_tensor(
                                out=h1[:, fcl, ttl * TT:(ttl + 1) * TT], in0=t, in1=uk[:, :], op=ALU.mult),
                                reads=[tB, ub], writes=[B_h1[fcl]])
                    ws_release(gB)
                    ws_release(uB)
                    if hook is not None and hi == 0 and b == 1:
                        norm_sq(*hook)
                    if hook is not None and hi == 0 and b == 3:
                        norm_stat(hook[0])
                        norm_apply(*hook)
                dpend = []

                def dissue(db):
                    dpend.append(load_wdn(wd, f0, nf, db * 256))

                dissue(0)
                for db in range(4):
                    if db + 1 < 4:
                        dissue(db + 1)
                    dw, dB = dpend.pop(0)
                    for dl in range(2):
                        dc = 2 * db + dl
                        for ttl in range(2):
                            bi, bk, bb = bank_alloc()
                            fns = []
                            for f in range(nf):
                                fns.append(lambda e, f=f, bk=bk, dw=dw, dl=dl, ttl=ttl, nf=nf: e.matmul(
                                    bk[:, :], lhsT=dw[:, f, dl * 128:(dl + 1) * 128],
                                    rhs=h1[:, f, ttl * TT:(ttl + 1) * TT], start=(f == 0), stop=(f == nf - 1)))
                            P.group(PE, fns, reads=[dB] + B_h1[:nf], writes=[bb])
                            tok0 = slot * SL + ttl * TT
                            hb = B_hT[dc][2 * slot + ttl]
                            P.op(DVE, lambda e, bk=bk, dc=dc, tok0=tok0: e.scalar_tensor_tensor(
                                out=hT[:, dc, tok0:tok0 + TT], in0=bk[:, :], scalar=0.5, in1=hT[:, dc, tok0:tok0 + TT],
                                op0=ALU.mult, op1=ALU.add), reads=[bb, hb], writes=[hb])
                    wd_release(dB)

        class _Stop(Exception):
            pass

        def body():
            if STOP < 1:
                raise _Stop()
            B_ffn_arena = B_h1 + B_wdn
            norm(0, 0)
            norm(1, 0)
            ffn(0, w_g1, w_u1, w_d1)
            ffn(1, w_g1, w_u1, w_d1, hook=(0, 8))
            norm(1, 8)

            if STOP < 2:
                raise _Stop()
            attn_bufs = ([b for r in B_QT for b in r] + [b for r in B_ycv for b in r] + B_Kb + B_Vb + B_E + B_kst
                         + B_vst + [B_ybf])
            alias_barrier(B_ffn_arena, attn_bufs)
            for s in range(4):
                P.op(POOL, lambda e, s=s: e.memset(vst[:, s, :, 128:132], 1.0), writes=[B_vst[s]])

            def xn_rhs(c, tt):
                return xn[:, tt // 2, c, (tt % 2) * TT:(tt % 2 + 1) * TT]

            def proj_mm(w_, wB, fl, tt):
                bi, bk, bb = bank_alloc()
                fns = []
                for c in range(8):
                    fns.append(lambda e, c=c, bk=bk: e.matmul(bk[:, :], lhsT=w_[:, c, fl * 128:(fl + 1) * 128],
                                                              rhs=xn_rhs(c, tt), start=(c == 0), stop=(c == 7)))
                P.group(PE, fns, reads=[wB, B_xn[tt // 2][tt % 2]], writes=[bb])
                return bk, bb

            wq = []
            order = [10, 11, 12, 13, 14, 15, 6, 7, 8, 9, 0, 2, 3, 1, 4, 5]
            blocks = {}
            oi = {"i": 0}

            def prefetch_in(n=1):
                for _ in range(n):
                    if oi["i"] < len(order):
                        b = order[oi["i"]]
                        oi["i"] += 1
                        blocks[b] = load_wsm(w_inx, b * 256)

            prefetch_in(4)

            def load_rope(tt):
                i = nxt("rope", 2)
                P.dma(SP, S_rope[i], lambda e: e.dma_start(out=ropes[:, i, 0:2 * TT], in_=rope[:, tt, :]), writes=[B_rope[i]])
                return ropes[:, i, 0:2 * TT], B_rope[i]

            def rope_proj(kind, pre=None):
                base = 6 if kind == "q" else 10
                loaded = dict(pre or {})

                def get_rope(i):
                    if i not in loaded:
                        loaded[i] = load_rope(i % 4)
                    return loaded[i]

                for g in range(2):
                  hblocks = {h: blocks.pop(base + h) for h in (2 * g, 2 * g + 1)}
                  for tt in range(4):
                    i_ = g * 4 + tt
                    rp, rB = get_rope(i_)
                    if i_ + 1 < 8:
                        get_rope(i_ + 1)
                    for h in (2 * g, 2 * g + 1):
                        w_, wB = hblocks[h]
                        qk, qb = proj_mm(w_, wB, 0, tt)
                        rk, rb = proj_mm(w_, wB, 1, tt)
                        t1, t1B = tmp_alloc()
                        t2, t2B = tmp_alloc()
                        P.op(DVE, lambda e, qk=qk, t1=t1, rp=rp: e.tensor_tensor(out=t1, in0=qk[:, :], in1=rp[:, 0:TT],
                                                                                 op=ALU.mult),
                             reads=[qb, rB], writes=[t1B])
                        P.op(DVE, lambda e, rk=rk, t2=t2, rp=rp: e.tensor_tensor(out=t2, in0=rk[:, :], in1=rp[:, TT:2 * TT],
                                                                                 op=ALU.mult),
                             reads=[rb, rB], writes=[t2B])
                        if kind == "q":
                            P.op(DVE, lambda e, t1=t1, t2=t2, h=h, tt=tt: e.tensor_tensor(
                                out=QT[:, h, tt * TT:(tt + 1) * TT], in0=t1, in1=t2, op=ALU.add),
                                reads=[t1B, t2B], writes=[B_QT[h][tt]])
                        else:
                            si = nxt("kst", 2)
                            P.op(DVE, lambda e, t1=t1, t2=t2, si=si: e.tensor_tensor(out=kst[:, si, :], in0=t1, in1=t2,
                                                                                     op=ALU.add),
                                 reads=[t1B, t2B], writes=[B_kst[si]])
                            dst = kT_mine[tt // 2].ap()[:, h * SL + (tt % 2) * TT:h * SL + (tt % 2 + 1) * TT]
                            P.dma(SP, S_kst[si], lambda e, si=si, dst=dst: e.dma_start(out=dst, in_=kst[:, si, :]),
                                  reads=[B_kst[si]], writes=[B_kTm[h][tt]])
                  for h in (2 * g, 2 * g + 1):
                    ws_release(hblocks[h][1])
                  prefetch_in(2)


            rope_proj("k")
            q_pre = {0: load_rope(0), 1: load_rope(1)}
            wv0, wv0B = blocks.pop(14)
            wv1, wv1B = blocks.pop(15)
            v_view = [[v_mine[s_][hp].ap().rearrange("(h p) (k e) -> k p h e", h=2, p=128, k=8) for hp in range(2)]
                  for s_ in range(2)]
            def v_gather(s_):
                for hp in range(2):
                    P.dma(POOL, S_ccv[2 * s_ + hp], lambda e, s_=s_, hp=hp: e.collective_compute(
                        "AllGather", ALU.bypass, replica_groups=GROUPS, dma_qos="P2", ins=[v_mine[s_][hp].ap().opt()],
                        outs=[v_all[s_][hp].ap().opt()]),
                        reads=B_vm[8 * s_:8 * s_ + 8], writes=[B_vall[2 * s_ + hp]])

            for ts in range(16):
                s_, kt = ts // 8, ts % 8
                si = nxt("vst", 4)
                for half, (wv, wvB) in enumerate(((wv0, wv0B), (wv1, wv1B))):
                    bi, bk, bb = bank_alloc()
                    fns = []
                    for c in range(8):
                        fns.append(lambda e, c=c, bk=bk, wv=wv, s_=s_, kt=kt: e.matmul(
                            bk[:, 0:256], lhsT=xn[:, s_, c, kt * 128:(kt + 1) * 128], rhs=wv[:, c, :],
                            start=(c == 0), stop=(c == 7)))
                    P.group(PE, fns, reads=[wvB, B_xn[s_][kt // 4]], writes=[bb])
                    P.op(ACT, lambda e, bk=bk, si=si, half=half: e.activation(
                        out=vst[:, si, 2 * half:2 * half + 2, 0:128],
                        in_=bk[:, 0:256].rearrange("p (h e) -> p h e", h=2), func=AF.Copy),
                        reads=[bb], writes=[B_vst[si]])
                for hp in range(2):
                    P.dma(SP, S_vst[si], lambda e, si=si, s_=s_, kt=kt, hp=hp: e.dma_start(
                        out=v_view[s_][hp][kt], in_=vst[:, si, 2 * hp:2 * hp + 2, :]),
                        reads=[B_vst[si]], writes=[B_vm[ts]] if hp == 1 else [])
            ws_release(wv0B)
            ws_release(wv1B)
            prefetch_in(2)

            if STOP < 3:
                raise _Stop()
            for s_ in range(2):
                P.dma(POOL, S_cck[s_], lambda e, s_=s_: e.collective_compute(
                    "AllGather", ALU.bypass, replica_groups=GROUPS, dma_qos="P2", ins=[kT_mine[s_].ap().opt()],
                    outs=[kT_all[s_].ap().opt()]),
                    reads=[B_kTm[h][2 * s_ + t] for h in range(4) for t in range(2)], writes=[B_kTall[s_]])
                v_gather(s_)

            rope_proj("q", pre=q_pre)

            for cc in range(4):
                if cc == 0:
                    wb_, wbB = blocks.pop(0)
                elif cc == 2:
                    wb_, wbB = blocks.pop(1)
                wcx, wcxB = blocks.pop(2 + cc)
                for slot in range(2):
                    for ttl in range(2):
                        tt = 2 * slot + ttl
                        ck, cb_ = proj_mm(wcx, wcxB, 0, tt)
                        xk, xb_ = proj_mm(wcx, wcxB, 1, tt)
                        t, tB = tmp_alloc()
                        P.op(ACT, lambda e, ck=ck, t=t: e.activation(out=t, in_=ck[:, :], func=AF.Copy),
                             reads=[cb_], writes=[tB])
                        P.op(DVE, lambda e, t=t, xk=xk, ttl=ttl: e.tensor_tensor(
                            out=ubuf[:, 2 + ttl * TT:2 + (ttl + 1) * TT], in0=t, in1=xk[:, :], op=ALU.mult),
                            reads=[tB, xb_, B_u], writes=[B_u])
                    w0 = vecs_sb[:, 40 + 0 + cc:40 + 0 + cc + 1]
                    w1 = vecs_sb[:, 40 + 4 + cc:40 + 4 + cc + 1]
                    w2 = vecs_sb[:, 40 + 8 + cc:40 + 8 + cc + 1]
                    P.op(DVE, lambda e, w2=w2: e.tensor_scalar(out=accb[:, :], in0=ubuf[:, 2:SL + 2], scalar1=w2,
                                                               scalar2=None, op0=ALU.mult),
                         reads=[B_u, B_const, B_acc], writes=[B_acc])
                    P.op(DVE, lambda e, w1=w1: e.scalar_tensor_tensor(out=accb[:, :], in0=ubuf[:, 1:SL + 1], scalar=w1,
                                                                      in1=accb[:, :], op0=ALU.mult, op1=ALU.add),
                         reads=[B_u, B_const, B_acc], writes=[B_acc])
                    P.op(DVE, lambda e, w0=w0: e.scalar_tensor_tensor(out=accb[:, :], in0=ubuf[:, 0:SL], scalar=w0,
                                                                      in1=accb[:, :], op0=ALU.mult, op1=ALU.add),
                         reads=[B_u, B_const, B_acc], writes=[B_acc])
                    hcol = (cc * 2 + slot) * 2
                    P.op(DVE, lambda e, hcol=hcol: e.tensor_copy(out=halo_st[:, hcol:hcol + 2], in_=ubuf[:, SL:SL + 2]),
                         reads=[B_u, B_halo], writes=[B_halo])
                    P.op(DVE, lambda e, hcol=hcol: e.tensor_copy(out=acc01[:, hcol:hcol + 2], in_=accb[:, 0:2]),
                         reads=[B_acc, B_halo], writes=[B_halo])
                    for ttl in range(2):
                        tt = 2 * slot + ttl
                        bk_, bb_ = proj_mm(wb_, wbB, cc % 2, tt)
                        if ttl == 0:
                            P.op(DVE, lambda e, bk_=bk_, hcol=hcol: e.tensor_copy(out=cb01[:, hcol:hcol + 2],
                                                                                   in_=bk_[:, 0:2]),
                                 reads=[bb_, B_halo], writes=[B_halo])
                        P.op(DVE, lambda e, bk_=bk_, ttl=ttl, tt=tt, cc=cc: e.tensor_tensor(
                            out=ycv[:, cc, tt * TT:(tt + 1) * TT], in0=accb[:, ttl * TT:(ttl + 1) * TT], in1=bk_[:, :],
                            op=ALU.mult), reads=[bb_, B_acc], writes=[B_ycv[cc][tt]])
                ws_release(wcxB)
                if cc % 2 == 1:
                    ws_release(wbB)
                prefetch_in(1 if cc == 0 else 2)
            P.dma(SP, S_halo, lambda e: e.dma_start(out=halo_mine.ap(), in_=halo_st[:, :]), reads=[B_halo],
                  writes=[B_halom])
            P.dma(POOL, S_cc[2], lambda e: e.collective_compute("AllGather", ALU.bypass, replica_groups=GROUPS, dma_qos="P2",
                                                                ins=[halo_mine.ap().opt()], outs=[halo_all.ap().opt()]),
                  reads=[B_halom], writes=[B_haloall])

            if STOP < 4:
                raise _Stop()
            alias_barrier(B_xn[0] + B_xn[1], [b for r in B_yat for b in r])
            alias_barrier(B_vst, B_Kb + B_Vb)
            cmask = arena[:, A0 + 8288:A0 + 8288 + 2048].rearrange("p (m q) -> p m q", m=4)
            B_cmask = Buf()
            attn_bufs.append(B_cmask)
            alias_barrier(B_kst, [B_cmask])
            P.op(POOL, lambda e: e.memset(cmask[:, :, :], 0.0), writes=[B_cmask])
            for mm_ in range(4):
                P.op(POOL, lambda e, mm_=mm_: e.affine_select(
                    out=cmask[:, mm_, :], in_=cmask[:, mm_, :], pattern=[[1, TT]], compare_op=ALU.is_ge,
                    fill=NEG, base=-128 * mm_, channel_multiplier=-1), reads=[B_cmask], writes=[B_cmask])
            SEL = 384
            O_BANKS = [0, 1, 2]
            ST_BANKS = [3, 4, 5, 6]
            st_rr = {"i": 0}
            obanks_B = [B_bank[b] for b in O_BANKS]

            def oacc(a):
                b = a // 3
                off = (a % 3) * 129
                return banks[O_BANKS[b]][:, off:off + 129]

            oab = ropes
            B_oab = [Buf(), Buf()]
            alias_barrier(B_rope, B_oab)
            qsel = wsm[:, 0:3, :].rearrange("p s f -> p (s f)").rearrange("p (b v q) -> p b v q", b=2, v=3)
            B_qsel = [Buf(), Buf()]
            alias_barrier(B_ws, B_qsel)
            kaux = xn[:, 1, 0:2, :].rearrange("p c t -> p (c t)").rearrange("p (s k) -> p s k", s=2)
            vaux = xn[:, 1, 2:5, :].rearrange("p c t -> p (c t)")[:, 0:2112].rearrange("p (s k) -> p s k", s=2)
            B_kaux = [Buf(), Buf()]
            B_vaux = [Buf(), Buf()]
            alias_barrier(B_xn[1], B_kaux + B_vaux)
            S_kaux = [dsem(f"d_kx{i}") for i in range(2)]
            S_vaux = [dsem(f"d_vx{i}") for i in range(2)]

            def sAcol(v):
                return bc_sb[:, SEL + v:SEL + v + 1]

            def sBcol(v):
                return bc_sb[:, SEL + 3 + v:SEL + 3 + v + 1]

            work = []
            for h in range(4):
                for qt in range(2):
                    units = []
                    units.append(("B", [("own", None, 1)] + [("g", r, 0) for r in range(4)]))
                    units.append(("A", [("own", None, 0)]))
                    for v in range(3):
                        units.append((("V", v), [("v", v, None)]))
                    for ui, (ukind, chunks) in enumerate(units):
                        for ci, (kind, r, ks) in enumerate(chunks):
                            if kind == "own":
                                kts = list(range(0, 4 * qt + 4))
                            else:
                                kts = list(range(8))
                            work.append(dict(h=h, qt=qt, ukind=ukind, kind=kind, r=r, ks=ks, kts=kts,
                                             first=(ci == 0), lastc=(ci == len(chunks) - 1),
                                             lastu=(ui == len(units) - 1)))
            nv = {"n": 0}

            def kv_issue(w, i):
                h, ks, r = w["h"], w["ks"], w["r"]
                if w["kind"] == "own":
                    src_k = kT_mine[ks].ap()[:, h * SL:(h + 1) * SL]
                    src_v = v_mine[ks][h // 2].ap()[(h % 2) * 128:(h % 2 + 1) * 128, :]
                    kdeps = [B_kTm[h][2 * ks], B_kTm[h][2 * ks + 1]]
                    vdeps = B_vm[8 * ks:8 * ks + 8]
                else:
                    if w["kind"] == "v":
                        r, ks = w["r"], 0
                    src_k = kT_all[ks].ap()[r * 128:(r + 1) * 128, h * SL:(h + 1) * SL]
                    src_v = v_all[ks][h // 2].ap()[r * 256 + (h % 2) * 128:r * 256 + (h % 2 + 1) * 128, :]
                    kdeps = [B_kTall[ks]]
                    vdeps = [B_vall[2 * ks + h // 2]]
                P.dma(SP, S_Kb[i], lambda e: e.dma_start(out=Kb[:, i, :], in_=src_k), reads=kdeps, writes=[B_Kb[i]])
                P.dma(SP, S_Vb[i], lambda e: e.dma_start(out=Vb[:, i, :, :].rearrange("p k e -> p (k e)"), in_=src_v),
                      reads=vdeps, writes=[B_Vb[i]])
                if w["kind"] == "v":
                    ax = nv["n"] % 2
                    nv["n"] += 1
                    w["ax"] = ax
                    r2 = w["r"] + 1
                    src_k2 = kT_all[1].ap()[r2 * 128:(r2 + 1) * 128, h * SL:(h + 1) * SL]
                    src_v2 = v_all[1][h // 2].ap()[r2 * 256 + (h % 2) * 128:r2 * 256 + (h % 2 + 1) * 128, :]
                    P.dma(SP, S_kaux[ax], lambda e: e.dma_start(out=kaux[:, ax, :], in_=src_k2),
                          reads=[B_kTall[1]], writes=[B_kaux[ax]])
                    P.dma(SP, S_vaux[ax], lambda e: e.dma_start(out=vaux[:, ax, :], in_=src_v2),
                          reads=[B_vall[2 + h // 2]], writes=[B_vaux[ax]])

            issued = {"n": 0}

            def ensure_loaded(upto):
                while issued["n"] < len(work) and issued["n"] <= upto:
                    kv_issue(work[issued["n"]], issued["n"] % 3)
                    issued["n"] += 1

            tiles = []
            for wi, w in enumerate(work):
                for ki, kt in enumerate(w["kts"]):
                    tiles.append((wi, ki, kt))

            def make_qsel(h):
                hb = h % 2
                for v in range(3):
                    P.op(DVE, lambda e, hb=hb, v=v, h=h: e.tensor_scalar(
                        out=qsel[:, hb, v, :], in0=QT[:, h, 0:SL], scalar1=sAcol(v), scalar2=None, op0=ALU.mult),
                        reads=[B_QT[h][0], B_QT[h][1], B_const], writes=[B_qsel[hb]])
                    P.op(DVE, lambda e, hb=hb, v=v, h=h: e.scalar_tensor_tensor(
                        out=qsel[:, hb, v, :], in0=QT[:, h, SL:2 * SL], scalar=sBcol(v), in1=qsel[:, hb, v, :],
                        op0=ALU.mult, op1=ALU.add),
                        reads=[B_QT[h][2], B_QT[h][3], B_const, B_qsel[hb]], writes=[B_qsel[hb]])

            def blend_kv(w, bi):
                v, ax = w["r"], w["ax"]
                P.op(DVE, lambda e, bi=bi, v=v: e.tensor_scalar(
                    out=Kb[:, bi, :], in0=Kb[:, bi, :], scalar1=sAcol(v), scalar2=None, op0=ALU.mult),
                    reads=[B_Kb[bi], B_const], writes=[B_Kb[bi]])
                P.op(DVE, lambda e, bi=bi, v=v, ax=ax: e.scalar_tensor_tensor(
                    out=Kb[:, bi, :], in0=kaux[:, ax, :], scalar=sBcol(v), in1=Kb[:, bi, :], op0=ALU.mult,
                    op1=ALU.add), reads=[B_Kb[bi], B_kaux[ax], B_const], writes=[B_Kb[bi]])
                vflat = Vb[:, bi, :, :].rearrange("p k e -> p (k e)")
                P.op(DVE, lambda e, vflat=vflat, v=v: e.tensor_scalar(
                    out=vflat, in0=vflat, scalar1=sAcol(v), scalar2=None, op0=ALU.mult),
                    reads=[B_Vb[bi], B_const], writes=[B_Vb[bi]])
                P.op(DVE, lambda e, vflat=vflat, v=v, ax=ax: e.scalar_tensor_tensor(
                    out=vflat, in0=vaux[:, ax, :], scalar=sBcol(v), in1=vflat, op0=ALU.mult, op1=ALU.add),
                    reads=[B_Vb[bi], B_vaux[ax], B_const], writes=[B_Vb[bi]])

            def emit_qk(wi, ki, kt):
                w = work[wi]
                bi = wi % 3
                h, qt = w["h"], w["qt"]
                if ki == 0:
                    if w["ukind"] == "B" and qt == 0 and w["first"]:
                        make_qsel(h)
                if w["kind"] == "v":
                    v = w["r"]
                    q_ap = lambda m, h=h, qt=qt, v=v: qsel[m * 64:(m + 1) * 64, h % 2, v, qt * TT:(qt + 1) * TT]
                    q_B = [B_qsel[h % 2]]
                else:
                    slot = 0 if w["ukind"] == "A" else 1
                    tt = 2 * slot + qt
                    q_ap = lambda m, h=h, tt=tt: QT[m * 64:(m + 1) * 64, h, tt * TT:(tt + 1) * TT]
                    q_B = [B_QT[h][tt]]
                crossing = (w["kind"] == "own") and (kt >= 4 * qt)
                es_ = []
                for m in range(2):
                    sb_i = ST_BANKS[m * 2 + (st_rr["i"] % 2)]
                    stb = banks[sb_i]
                    qlo = 128 * (kt - 4 * qt) if crossing else 0
                    fns = [lambda e, stb=stb, m=m, bi=bi, kt=kt, q_ap=q_ap, crossing=crossing, qlo=qlo: e.matmul(
                        stb[:, qlo:TT], lhsT=Kb[m * 64:(m + 1) * 64, bi, kt * 128:(kt + 1) * 128],
                        rhs=q_ap(m)[:, qlo:TT], start=True, stop=not crossing)]
                    rds = [B_Kb[bi]] + q_B
                    if crossing:
                        mm_ = kt - 4 * qt
                        fns.append(lambda e, stb=stb, mm_=mm_, qlo=qlo: e.matmul(
                            stb[:, qlo:TT], lhsT=ident[:, :], rhs=cmask[:, mm_, qlo:TT], start=False, stop=True))
                        rds = rds + [B_cmask, B_const]
                    P.group(PE, fns, reads=rds, writes=[B_bank[sb_i]])
                    ei = nxt("E", 4)
                    P.op(ACT, lambda e, stb=stb, ei=ei, qlo=qlo: e.activation(
                        out=Eb[:, ei, qlo:TT], in_=stb[:, qlo:TT], func=AF.Exp, scale=0.125),
                        reads=[B_bank[sb_i]], writes=[B_E[ei]])
                    es_.append(ei)
                st_rr["i"] += 1
                return tuple(es_)

            def emit_av(wi, ki, kt, es_):
                w = work[wi]
                bi = wi % 3
                first = w["first"] and ki == 0
                last = w["lastc"] and ki == len(w["kts"]) - 1
                crossing = (w["kind"] == "own") and (kt >= 4 * w["qt"])
                qs_lo = (kt - 4 * w["qt"]) if crossing else 0
                for b3 in range(3):
                    fns = []
                    for a in range(3 * b3, min(3 * b3 + 3, 8)):
                        qs, m = a // 2, a % 2
                        if qs < qs_lo:
                            continue
                        st_flag = first and (a % 3 == 0)
                        fns.append(lambda e, a=a, m=m, qs=qs, es_=es_, bi=bi, kt=kt, st_flag=st_flag,
                                   last=last: e.matmul(
                            oacc(a), lhsT=Eb[:, es_[m], qs * 128:(qs + 1) * 128], rhs=Vb[:, bi, kt, 0:129],
                            start=st_flag, stop=last, skip_group_check=True))
                    if fns:
                        P.group(PE, fns, reads=[B_E[es_[0]], B_E[es_[1]], B_Vb[bi]], writes=[obanks_B[b3]])
                if ki == len(w["kts"]) - 1:
                    ensure_loaded(wi + 3)
                    if wi + 2 < len(work) and work[wi + 2]["kind"] == "v":
                        blend_kv(work[wi + 2], (wi + 2) % 3)
                if last:
                    unit_done(w)

            def unit_done(w):
                uk = w["ukind"]
                for b3 in range(3):
                    ncol = 387 if b3 < 2 else 258
                    src = banks[O_BANKS[b3]][:, 0:ncol]
                    if uk in ("A", "B"):
                        sl = 0 if uk == "A" else 1
                        P.op(DVE, lambda e, src=src, sl=sl, b3=b3, ncol=ncol: e.tensor_copy(
                            out=oab[:, sl, 387 * b3:387 * b3 + ncol], in_=src),
                            reads=[obanks_B[b3]], writes=[B_oab[sl]])
                    else:
                        v = uk[1]
                        for sl, scol in ((0, sAcol(v)), (1, sBcol(v))):
                            dst = oab[:, sl, 387 * b3:387 * b3 + ncol]
                            P.op(DVE, lambda e, src=src, dst=dst, scol=scol: e.scalar_tensor_tensor(
                                out=dst, in0=src, scalar=scol, in1=dst, op0=ALU.mult, op1=ALU.add),
                                reads=[obanks_B[b3], B_oab[sl], B_const], writes=[B_oab[sl]])
                if w["lastu"]:
                    post(w["h"], w["qt"], 0)
                    post(w["h"], w["qt"], 1)

            cur = {"ti": 0}
            otile = tmp[:, :, :].rearrange("p a (b e) -> p (a b) e", e=128)
            B_otile = [Buf() for _ in range(4)]
            alias_barrier(B_tmp, B_otile)
            par_ctr = {"n": 0}

            def post(h, qt, slot):
                tt = 2 * slot + qt
                q0 = tt * TT
                oB = B_oab[slot]
                par = (par_ctr["n"] // 2) % 2
                par_ctr["n"] += 1
                ps = par * 2 + slot
                oTB = B_otile[ps]
                c_ss = 40 + ps * 4
                c_rs = 56 + ps * 4

                def osb(a):
                    return oab[:, slot, a * 129:(a + 1) * 129]

                P.op(DVE, lambda e: e.tensor_copy(
                    out=small[:, 8:16], in_=oab[:, slot, 0:1032].rearrange("p (a e) -> p a e", e=129)[:, :, 128]),
                    reads=[oB, B_small], writes=[B_small])
                P.op(DVE, lambda e: e.reciprocal(out=small[:, 16:24], in_=small[:, 8:16]),
                     reads=[B_small], writes=[B_small])
                P.op(DVE, lambda e: e.tensor_scalar(
                    out=small[:, 24:28], in0=small[:, 16:24].rearrange("p (q m) -> p q m", m=2)[:, :, 1],
                    scalar1=LAM, scalar2=None, op0=ALU.mult), reads=[B_small, B_const], writes=[B_small])
                for qs in range(4):
                    oi_ = nxt("ob", 2)
                    tb = tbuf[:, oi_, :]
                    ob = otile[:, ps * 4 + qs, :]
                    P.op(DVE, lambda e, qs=qs, tb=tb: e.tensor_scalar(
                        out=tb, in0=osb(qs * 2 + 1)[:, 0:128], scalar1=small[:, 24 + qs:25 + qs], scalar2=None,
                        op0=ALU.mult), reads=[oB, B_small], writes=[B_tbuf[oi_]])
                    P.op(DVE, lambda e, qs=qs, tb=tb, ob=ob: e.scalar_tensor_tensor(
                        out=ob, in0=osb(qs * 2)[:, 0:128], scalar=small[:, 16 + 2 * qs:17 + 2 * qs], in1=tb,
                        op0=ALU.mult, op1=ALU.subtract), reads=[oB, B_small, B_tbuf[oi_]],
                        writes=[oTB])
                    P.op(DVE, lambda e, qs=qs, ob=ob, tb=tb: e.scalar_tensor_tensor(
                        out=tb, in0=ob, scalar=1.0, in1=ob, op0=ALU.mult, op1=ALU.mult,
                        accum_out=small[:, 28 + qs:29 + qs]), reads=[oTB, B_small],
                        writes=[B_tbuf[oi_], B_small])
                P.op(DVE, lambda e: e.tensor_scalar(
                    out=small[:, c_ss:c_ss + 4], in0=small[:, 28:32], scalar1=1.0 / 128.0,
                    scalar2=EPS, op0=ALU.mult, op1=ALU.add), reads=[B_small], writes=[oTB])

                def stage2_fn():
                    P.op(ACT, lambda e: e.activation(out=small[:, c_ss:c_ss + 4], in_=small[:, c_ss:c_ss + 4],
                                                     func=AF.Ln), reads=[oTB], writes=[oTB])
                    P.op(ACT, lambda e: e.activation(out=small[:, c_rs:c_rs + 4], in_=small[:, c_ss:c_ss + 4],
                                                     func=AF.Exp, scale=-0.5), reads=[oTB], writes=[oTB])
                    for qs in range(4):
                        ob = otile[:, ps * 4 + qs, :]
                        P.op(DVE, lambda e, qs=qs, ob=ob: e.scalar_tensor_tensor(
                            out=ybf[:, qs, :], in0=ob, scalar=small[:, c_rs + qs:c_rs + qs + 1], in1=gsub[:, :],
                            op0=ALU.mult, op1=ALU.mult), reads=[oTB, B_const], writes=[B_ybf])

                    def stage2b_fn():
                        fns = []
                        for qs in range(4):
                            fns.append(lambda e, qs=qs: e.transpose(out=tpb[:, qs * 128:(qs + 1) * 128],
                                                                    in_=ybf[:, qs, :], identity=ident[:, :]))
                        P.group(PE, fns, reads=[B_ybf, B_const], writes=[B_tpb])
                        P.op(DVE, lambda e, h=h, q0=q0: e.tensor_copy(out=yat[:, h, q0:q0 + TT], in_=tpb[:, :]),
                             reads=[B_tpb], writes=[B_yat[h][tt]])

                    stage2.append((cur["ti"] + 6, stage2b_fn))
                    stage2.sort(key=lambda x: x[0])

                stage2.append((cur["ti"] + 20 + 8 * slot, stage2_fn))
                stage2.sort(key=lambda x: x[0])

            def halo_fix():
                P.dma(SP, S_halo, lambda e: e.dma_start(out=halo_sb[:, :, :],
                                                        in_=halo_all.ap().rearrange("(r p) f -> p r f", p=128)),
                      reads=[B_haloall], writes=[B_halo_sb])

                hs4 = halo_sb[:, :, :].rearrange("p r (c s t) -> p r c s t", c=4, s=2)
                hf4 = hfix[:, 0:16].rearrange("p (c s t) -> p c s t", c=4, s=2)
                P.op(DVE, lambda e: e.memset(hfix[:, :], 0.0), reads=[B_hfix], writes=[B_hfix])
                for r in range(4):
                    P.op(DVE, lambda e, r=r: e.scalar_tensor_tensor(
                        out=hf4[:, :, 0, :], in0=hs4[:, r, :, 0, :], scalar=bc_sb[:, SEL + 6 + r:SEL + 7 + r],
                        in1=hf4[:, :, 0, :], op0=ALU.mult, op1=ALU.add), reads=[B_halo_sb, B_const, B_hfix], writes=[B_hfix])
                    P.op(DVE, lambda e, r=r: e.scalar_tensor_tensor(
                        out=hf4[:, :, 1, :], in0=hs4[:, r, :, 0, :], scalar=bc_sb[:, SEL + 10 + r:SEL + 11 + r],
                        in1=hf4[:, :, 1, :], op0=ALU.mult, op1=ALU.add), reads=[B_halo_sb, B_const, B_hfix], writes=[B_hfix])
                    if r >= 1:
                        P.op(DVE, lambda e, r=r: e.scalar_tensor_tensor(
                            out=hf4[:, :, 1, :], in0=hs4[:, r, :, 1, :], scalar=bc_sb[:, 400 + (r - 1):401 + (r - 1)],
                            in1=hf4[:, :, 1, :], op0=ALU.mult, op1=ALU.add), reads=[B_halo_sb, B_const, B_hfix],
                            writes=[B_hfix])
                for cc in range(4):
                    w0 = vecs_sb[:, 40 + cc:41 + cc]
                    w1 = vecs_sb[:, 44 + cc:45 + cc]
                    for slot in range(2):
                        hc = (cc * 2 + slot) * 2
                        um2 = hfix[:, hc:hc + 1]
                        um1 = hfix[:, hc + 1:hc + 2]
                        a0 = acc01[:, hc:hc + 1]
                        a1 = acc01[:, hc + 1:hc + 2]
                        f0 = hfix[:, 16 + hc:16 + hc + 1]
                        f1 = hfix[:, 16 + hc + 1:16 + hc + 2]
                        P.op(DVE, lambda e, um1=um1, w1=w1, a0=a0, f0=f0: e.scalar_tensor_tensor(
                            out=f0, in0=um1, scalar=w1, in1=a0, op0=ALU.mult, op1=ALU.add),
                            reads=[B_hfix, B_halo, B_const], writes=[B_hfix])
                        P.op(DVE, lambda e, um2=um2, w0=w0, f0=f0: e.scalar_tensor_tensor(
                            out=f0, in0=um2, scalar=w0, in1=f0, op0=ALU.mult, op1=ALU.add),
                            reads=[B_hfix, B_const], writes=[B_hfix])
                        P.op(DVE, lambda e, um1=um1, w0=w0, a1=a1, f1=f1: e.scalar_tensor_tensor(
                            out=f1, in0=um1, scalar=w0, in1=a1, op0=ALU.mult, op1=ALU.add),
                            reads=[B_hfix, B_halo, B_const], writes=[B_hfix])
                        tok0 = slot * SL
                        P.op(DVE, lambda e, cc=cc, tok0=tok0, hc=hc: e.tensor_tensor(
                            out=ycv[:, cc, tok0:tok0 + 2], in0=hfix[:, 16 + hc:16 + hc + 2], in1=cb01[:, hc:hc + 2],
                            op=ALU.mult), reads=[B_hfix, B_halo], writes=[B_ycv[cc][2 * slot]])


            ensure_loaded(2)
            pend = None
            stage2 = []
            for ti, (wi, ki, kt) in enumerate(tiles):
                es_ = emit_qk(wi, ki, kt)
                if pend is not None:
                    emit_av(*pend)
                pend = (wi, ki, kt, es_)
                cur["ti"] = ti
                if ti == 150:
                    halo_fix()
                while stage2 and stage2[0][0] <= ti:
                    stage2.pop(0)[1]()
            emit_av(*pend)
            while stage2:
                stage2.pop(0)[1]()
            alias_barrier(B_qsel, B_ws)
            alias_barrier(B_kaux + B_vaux, B_xn[1])
            alias_barrier(B_oab, B_rope)
            alias_barrier(B_otile, B_tmp)


            if STOP < 5:
                raise _Stop()
            oblocks = [load_wsm(w_out, b * 256) for b in range(4)]
            for slot in (1, 0):
                for b in range(4):
                    ow, oB = oblocks[b]
                    for dl in range(2):
                        dc = 2 * b + dl
                        for ttl in range(2):
                            tt = 2 * slot + ttl
                            bi, bk, bb = bank_alloc()
                            fns = []
                            for c in range(8):
                                src = ycv[:, c, tt * TT:(tt + 1) * TT] if c < 4 else yat[:, c - 4, tt * TT:(tt + 1) * TT]
                                fns.append(lambda e, c=c, bk=bk, src=src, ow=ow, dl=dl: e.matmul(
                                    bk[:, :], lhsT=ow[:, c, dl * 128:(dl + 1) * 128], rhs=src, start=(c == 0),
                                    stop=(c == 7)))
                            P.group(PE, fns, reads=[oB] + [B_ycv[c][tt] for c in range(4)]
                                    + [B_yat[c][tt] for c in range(4)], writes=[bb])
                            hb = B_hT[dc][tt]
                            P.op(DVE, lambda e, bk=bk, dc=dc, tt=tt: e.tensor_tensor(
                                out=hT[:, dc, tt * TT:(tt + 1) * TT], in0=bk[:, :], in1=hT[:, dc, tt * TT:(tt + 1) * TT],
                                op=ALU.add), reads=[bb, hb], writes=[hb])
                    if slot == 0 and b == 0 and STOP >= 6:
                        norm_sq(1, 16)
                    if slot == 0 and b == 2 and STOP >= 6:
                        norm_stat(1)
                        norm_apply(1, 16)
            for b in range(4):
                ws_release(oblocks[b][1])

            if STOP < 6:
                raise _Stop()
            alias_barrier([b for r in B_yat for b in r], B_xn[0])
            alias_barrier(attn_bufs, B_ffn_arena)
            ffn(1, w_g2, w_u2, w_d2, hook=(0, 16))
            ffn(0, w_g2, w_u2, w_d2, hook=(1, 24))

            if STOP < 7:
                raise _Stop()
            alias_barrier(B_ffn_arena, [B_pT])
            P.dma(POOL, S_pT, lambda e: e.dma_start(out=pTb[:, :, :], in_=pT.rearrange("(k p) t -> p k t", p=128)),
                  writes=[B_pT])
            wpp_sb = arena[:, 24576:26624].rearrange("p (k d) -> p k d", k=2)
            wppB = Buf()
            alias_barrier(attn_bufs, [wppB, B_pT])
            S_wpp = dsem("d_wpp")
            P.dma(POOL, S_wpp, lambda e: e.dma_start(out=wpp_sb, in_=w_pp.rearrange("(k p) d -> p k d", p=128)),
                  writes=[wppB])
            gblocks = [load_wsm(w_pg, b * 256) for b in range(4)]
            outv = outT.rearrange("(c p) t -> p c t", p=128)

            def final_norm(slot, part="both"):
                if part in ("both", "early"):
                    norm_sq(slot, 32)
                if part == "early":
                    return
                norm_stat(slot)
                for ttl in range(2):
                    tt = 2 * slot + ttl
                    r = rstd[:, ttl, :]
                    for c in range(8):
                        ti_ = nxt("tmp", 4)
                        t, tB = tmp[:, ti_, :], B_tmp[ti_]
                        P.op(DVE, lambda e, c=c, r=r, tt=tt, t=t: e.scalar_tensor_tensor(
                            out=t, in0=hT[:, c, tt * TT:(tt + 1) * TT], scalar=vecs_sb[:, 32 + c:33 + c], in1=r,
                            op0=ALU.mult, op1=ALU.mult), reads=[B_hT[c][tt], B_rstd[ttl], B_const], writes=[tB])
                        P.dma(SP, S_out[ti_], lambda e, c=c, tt=tt, t=t: e.dma_start(
                            out=outv[:, c, tt * TT:(tt + 1) * TT], in_=t), reads=[tB])

            drip = []
            for slot in (1, 0):
                for b in range(4):
                    gw, gB = gblocks[b]
                    for dl in range(2):
                        dc = 2 * b + dl
                        for ttl in range(2):
                            tt = 2 * slot + ttl
                            gk, gb = proj_mm(gw, gB, dl, tt)
                            bi, pk, pb = bank_alloc()
                            fns = []
                            for k in range(2):
                                fns.append(lambda e, k=k, pk=pk, dc=dc, tt=tt: e.matmul(
                                    pk[:, :], lhsT=wpp_sb[:, k, dc * 128:(dc + 1) * 128],
                                    rhs=pTb[:, k, tt * TT:(tt + 1) * TT], start=(k == 0), stop=(k == 1)))
                            P.group(PE, fns, reads=[wppB, B_pT], writes=[pb])
                            t, tB = tmp_alloc()
                            P.op(ACT, lambda e, gk=gk, t=t: e.activation(out=t, in_=gk[:, :], func=AF.Sigmoid),
                                 reads=[gb], writes=[tB])
                            P.op(DVE, lambda e, t=t, pk=pk: e.tensor_tensor(out=t, in0=t, in1=pk[:, :], op=ALU.mult),
                                 reads=[tB, pb], writes=[tB])
                            hb = B_hT[dc][tt]
                            P.op(DVE, lambda e, t=t, dc=dc, tt=tt: e.tensor_tensor(
                                out=hT[:, dc, tt * TT:(tt + 1) * TT], in0=t, in1=hT[:, dc, tt * TT:(tt + 1) * TT],
                                op=ALU.add), reads=[tB, hb], writes=[hb])
                            if drip:
                                drip.pop(0)()
                    if slot == 1 and b == 0:
                        drip += norm_sq_list(0, 24)
                    if slot == 1 and b == 2:
                        while drip:
                            drip.pop(0)()
                        norm_stat(0)
                        norm_apply(0, 24)
                    if STOP >= 8 and slot == 0 and b == 0:
                        drip += norm_sq_list(1, 32)
                    if STOP >= 8 and slot == 0 and b == 2:
                        while drip:
                            drip.pop(0)()
                        final_norm(1, "late")
            if STOP >= 8:
                final_norm(0)
            for b in range(4):
                ws_release(gblocks[b][1])
            if STOP < 8:
                raise _Stop()

        stopped = False
        try:
            body()
        except _Stop:
            stopped = True
        if stopped:
            outv_dbg = outT.rearrange("(c p) t -> p c t", p=128)
            for c in range(8):
                oi_ = c % 4
                P.dma(SP, S_out[oi_], lambda e, c=c: e.dma_start(out=outv_dbg[:, c, :], in_=hT[:, c, :]),
                      reads=[B_hT[c][t] for t in range(4)])
        finals = [(d.sem, d.cnt) for d in S_out if d.cnt > 0]

        def _check():
            engs = [PE, ACT, DVE, POOL, SP]
            pc = {e.name: 0 for e in engs}
            semv = {}
            progress = True
            while progress:
                progress = False
                for e in engs:
                    while pc[e.name] < len(e.ops):
                        waits, fn, inc = e.ops[pc[e.name]]
                        if any(semv.get(id(sm), 0) < v for sm, v in waits):
                            break
                        if inc is not None:
                            semv[id(inc[0])] = semv.get(id(inc[0]), 0) + (inc[1] if inc[1] else 1)
                        pc[e.name] += 1
                        progress = True
            stuck = {e.name: (pc[e.name], len(e.ops)) for e in engs if pc[e.name] < len(e.ops)}
            if stuck:
                msg = []
                for e in engs:
                    if pc[e.name] < len(e.ops):
                        waits, fn, inc = e.ops[pc[e.name]]
                        msg.append((e.name, pc[e.name], len(e.ops),
                                    [(getattr(sm, "name", str(sm)), v, semv.get(id(sm), 0)) for sm, v in waits]))
                raise RuntimeError(f"DEADLOCK in recorded program: {msg}")
            print("sync check ok; ops per engine:", {e.name: len(e.ops) for e in engs}, flush=True)

        _check()

        with nc.Block() as block:
            @block.tensor
            def _(e):
                emit(e, PE)

            @block.scalar
            def _(e):
                emit(e, ACT)

            @block.vector
            def _(e):
                emit(e, DVE)

            @block.gpsimd
            def _(e):
                emit(e, POOL)

            @block.sync
            def _(e):
                emit(e, SP)
                for s_, v in finals:
                    e.wait_ge(s_, v)
    return nc


def _prep_shared(inp):
    f32 = np.float32
    w_in = np.asarray(inp["w_in"][0], f32)
    cb, cc_, cx = w_in[:, 0:512], w_in[:, 512:1024], w_in[:, 1024:1536]
    q, k, v = w_in[:, 1536:2048], w_in[:, 2048:2560], w_in[:, 2560:3072]
    idx = np.arange(512)
    dloc = idx % 64
    partner = idx - dloc + (dloc + 32) % 64
    rq, rk = q[:, partner], k[:, partner]
    blocks = []
    blocks.append(np.concatenate([cb[:, 0:128], cb[:, 128:256]], 1))
    blocks.append(np.concatenate([cb[:, 256:384], cb[:, 384:512]], 1))
    for c in range(4):
        blocks.append(np.concatenate([cc_[:, c * 128:(c + 1) * 128], cx[:, c * 128:(c + 1) * 128]], 1))
    for h in range(4):
        blocks.append(np.concatenate([q[:, h * 128:(h + 1) * 128], rq[:, h * 128:(h + 1) * 128]], 1))
    for h in range(4):
        blocks.append(np.concatenate([k[:, h * 128:(h + 1) * 128], rk[:, h * 128:(h + 1) * 128]], 1))
    blocks.append(v[:, 0:256])
    blocks.append(v[:, 256:512])
    w_inx = np.ascontiguousarray(np.concatenate(blocks, 1))

    vecs = np.zeros((128, 64), f32)

    def put(col, vec):
        vecs[:, col:col + 8] = np.asarray(vec, f32).reshape(8, 128).T

    put(0, inp["ffn1_norm"][0]); put(8, inp["mix_norm"][0]); put(16, inp["ffn2_norm"][0])
    put(24, inp["ple_norm"][0]); put(32, inp["final_norm"])
    cw = np.asarray(inp["conv_w"][0], f32)
    for tap in range(3):
        vecs[:, 40 + tap * 4:40 + tap * 4 + 4] = cw[tap].reshape(4, 128).T
    shared = {
        "w_g1": np.ascontiguousarray(inp["ffn1_w_gate"][0], dtype=f32), "w_u1": np.ascontiguousarray(inp["ffn1_w_up"][0], dtype=f32),
        "w_d1": np.ascontiguousarray(inp["ffn1_w_down"][0], dtype=f32),
        "w_g2": np.ascontiguousarray(inp["ffn2_w_gate"][0], dtype=f32), "w_u2": np.ascontiguousarray(inp["ffn2_w_up"][0], dtype=f32),
        "w_d2": np.ascontiguousarray(inp["ffn2_w_down"][0], dtype=f32),
        "w_inx": w_inx, "w_out": np.ascontiguousarray(inp["w_out"][0], dtype=f32),
        "w_pg": np.ascontiguousarray(inp["w_ple_gate"][0], dtype=f32), "w_pp": np.ascontiguousarray(inp["w_ple_proj"][0], dtype=f32),
        "vecs": vecs,
    }
    bc = np.zeros((128, 416), f32)
    bc[:, 0:128] = np.asarray(inp["subln_g"][0], f32)[None, :]
    bc[:, 128:192] = np.asarray(inp["lambda_q1"][0], f32)[None, :]
    bc[:, 192:256] = np.asarray(inp["lambda_k1"][0], f32)[None, :]
    bc[:, 256:320] = np.asarray(inp["lambda_q2"][0], f32)[None, :]
    bc[:, 320:384] = np.asarray(inp["lambda_k2"][0], f32)[None, :]
    return shared, bc


def _core_tokens(j):
    a = np.arange(1024 * j, 1024 * j + 1024)
    b = np.arange(1024 * (7 - j), 1024 * (7 - j) + 1024)
    return np.concatenate([a, b])


def kernel(**inp):
    f32 = np.float32
    x = np.asarray(inp["x"], f32)
    p = np.asarray(inp["p"], f32)[0]
    shared, bc0 = _prep_shared(inp)
    inv_freq = (1.0 / (np.float32(10000.0) ** (np.arange(0, 64, 2, dtype=f32) / np.float32(64)))).astype(f32)
    in_maps = []
    for c in range(8):
        b, j = c // 4, c % 4
        tok = _core_tokens(j)
        m = dict(shared)
        m["xT"] = np.ascontiguousarray(x[b, tok, :].T)
        m["pT"] = np.ascontiguousarray(p[b, tok, :].T)
        ang = (tok.astype(f32)[:, None] * inv_freq[None, :]).astype(f32)
        cos = np.cos(ang).astype(f32).T
        sin = np.sin(ang).astype(f32).T
        pi = np.arange(128)
        cosT = cos[pi % 32, :]
        sgn = np.where((pi % 64) < 32, -1.0, 1.0).astype(f32)[:, None]
        sinT = sin[pi % 32, :] * sgn
        rope = np.zeros((128, 4, 1024), f32)
        for tt in range(4):
            rope[:, tt, 0:512] = cosT[:, tt * 512:(tt + 1) * 512]
            rope[:, tt, 512:1024] = sinT[:, tt * 512:(tt + 1) * 512]
        m["rope"] = rope
        bc = bc0.copy()
        S = 384
        for v in range(3):
            bc[:, S + v] = 1.0 if v < j else 0.0
            bc[:, S + 3 + v] = 0.0 if v < j else 1.0
        for r in range(4):
            bc[:, S + 6 + r] = 1.0 if (j >= 1 and r == j - 1) else 0.0
        for r in range(4):
            bc[:, S + 10 + r] = 1.0 if (j == 3 and r == 3) else 0.0
        for r in range(1, 4):
            bc[:, 400 + (r - 1)] = 1.0 if (j <= 2 and r == j + 1) else 0.0
        m["bcast"] = bc
        in_maps.append(m)
    import os
    nc = build_nc(int(os.environ.get('KSTOP', '99')))
    res = run_bass_kernel_spmd(nc, in_maps, core_ids=list(range(8)))
    out = np.zeros((2, 8192, D), f32)
    for c in range(8):
        b, j = c // 4, c % 4
        out[b, _core_tokens(j), :] = res.results[c]["outT"].T
    return out
```
